# Optimizing a Trainium2 kernel written in Bass

```python
import jax, jax.numpy as jnp
from jax import lax
import numpy as np

D_MODEL = 2048
BATCH = 4
SEQ = 2048
DEPTH = 2

GRID_W = 64
CTX_LEN = 256
NORM_EPS = 1e-6

ATT_HEADS = 8
ATT_KV_HEADS = 2
ATT_HEAD_DIM = 128
ATT_GROUP = ATT_HEADS // ATT_KV_HEADS
ATT_WIDTH = ATT_HEADS * ATT_HEAD_DIM
ATT_KV_WIDTH = ATT_KV_HEADS * ATT_HEAD_DIM
Q_BLOCK = 128
ROPE_THETA = 10000.0

RWKV_HEADS = 8
RWKV_HEAD_DIM = 64
RWKV_WIDTH = RWKV_HEADS * RWKV_HEAD_DIM
DECAY_RANK = 64
ICLR_RANK = 64
RWKV_GN_EPS = 1e-5 * RWKV_HEAD_DIM

POOL_GROUPS = 4
POOL_GROUP_DIM = 128
POOL_WIDTH = POOL_GROUPS * POOL_GROUP_DIM
POOL_WINDOWS = (2, 4, 8, 16)

MIX_WIDTH = ATT_WIDTH + RWKV_WIDTH + POOL_WIDTH

IN_SEGMENTS = (
    ('att_q', ATT_WIDTH), ('att_k', ATT_KV_WIDTH), ('att_v', ATT_KV_WIDTH), ('att_g', ATT_WIDTH),
    ('rw_r', RWKV_WIDTH), ('rw_k', RWKV_WIDTH), ('rw_v', RWKV_WIDTH), ('rw_g', RWKV_WIDTH),
    ('rw_wf', DECAY_RANK), ('rw_wb', DECAY_RANK), ('rw_af', ICLR_RANK), ('rw_ab', ICLR_RANK),
    ('pool_x', POOL_WIDTH), ('pool_g', POOL_WIDTH),
)
IN_WIDTH = sum(s for _, s in IN_SEGMENTS)

kernel_name = 'hymba_style_rwkv7_pool_gqa_diffusion_block'


def rms_norm(x, g):
    xf = x.astype(jnp.float32)
    y = xf * lax.rsqrt(jnp.mean(xf * xf, axis=-1, keepdims=True) + NORM_EPS)
    return (y * g.astype(jnp.float32)).astype(x.dtype)


def split_columns(p):
    sizes = [s for _, s in IN_SEGMENTS]
    idx = [int(i) for i in np.cumsum(sizes)[:-1]]
    parts = jnp.split(p, idx, axis=-1)
    return {name: part for (name, _), part in zip(IN_SEGMENTS, parts)}


def to_heads(z, n_heads, head_dim):
    return z.reshape(z.shape[0], z.shape[1], n_heads, head_dim)


def axial_rope_tables(rows):
    half = ATT_HEAD_DIM // 2
    nfreq = half // 2
    inv = ROPE_THETA ** (-(jnp.arange(nfreq, dtype=jnp.float32) * 2.0 / half))
    row = jnp.repeat(jnp.arange(rows, dtype=jnp.float32), GRID_W)
    col = jnp.tile(jnp.arange(GRID_W, dtype=jnp.float32), rows)
    ang = jnp.stack([row, col], axis=-1)[..., None] * inv
    return jnp.cos(ang), jnp.sin(ang)


def apply_rope_2d(x, cos, sin):
    B, T, H, hd = x.shape
    xs = x.astype(jnp.float32).reshape(B, T, H, 2, 2, hd // 4)
    x1, x2 = xs[..., 0, :], xs[..., 1, :]
    c = cos[None, :, None]
    s = sin[None, :, None]
    out = jnp.stack([x1 * c - x2 * s, x2 * c + x1 * s], axis=-2)
    return out.reshape(B, T, H, hd).astype(x.dtype)


def gqa_block_attention(q, keys, vals):
    B, T, H, hd = q.shape
    nblk = T // Q_BLOCK
    qb = q.reshape(B, nblk, Q_BLOCK, ATT_KV_HEADS, ATT_GROUP, hd).transpose(1, 0, 2, 3, 4, 5)
    scale = hd ** -0.5

    def block(qi):
        s = jnp.einsum('bqhgd,bkhd->bhgqk', qi, keys).astype(jnp.float32) * scale
        p = jax.nn.softmax(s, axis=-1).astype(vals.dtype)
        return jnp.einsum('bhgqk,bkhd->bqhgd', p, vals)

    o = lax.map(block, qb)
    return o.transpose(1, 0, 2, 3, 4, 5).reshape(B, T, H * hd)


def token_shift(f, mu_prev, mu_next):
    prev = jnp.pad(f[:, :-1], ((0, 0), (1, 0), (0, 0)))
    nxt = jnp.pad(f[:, 1:], ((0, 0), (0, 1), (0, 0)))
    return f + mu_prev * (prev - f) + mu_next * (nxt - f)


def rwkv_inputs(pp, mu, w0, w_up, a0, a_up, k_k, k_a):
    r = token_shift(pp['rw_r'], mu[0, 0], mu[0, 1])
    k = token_shift(pp['rw_k'], mu[1, 0], mu[1, 1])
    v = token_shift(pp['rw_v'], mu[2, 0], mu[2, 1])
    heads = lambda z: to_heads(z, RWKV_HEADS, RWKV_HEAD_DIM)
    kk = heads(k * k_k).astype(jnp.float32)
    kk = kk / jnp.maximum(jnp.sqrt(jnp.sum(kk * kk, axis=-1, keepdims=True)), 1e-12)
    dirs = []
    for d, (dw, da) in enumerate(((pp['rw_wf'], pp['rw_af']), (pp['rw_wb'], pp['rw_ab']))):
        w_log = -jax.nn.softplus(-(w0[d] + jnp.tanh(dw) @ w_up[d])) - 0.5
        decay = jnp.exp(-jnp.exp(w_log.astype(jnp.float32)))
        a = jax.nn.sigmoid(a0[d] + da @ a_up[d])
        k_mod = k * (1.0 + (a - 1.0) * k_a)
        dirs.append((heads(decay), heads(k_mod), heads(a)))
    return heads(r), heads(v), kk, dirs


def rwkv_scan(S0, r, v, kk, decay, k, a, reverse):
    xs = tuple(jnp.moveaxis(z.astype(jnp.float32), 1, 0) for z in (r, decay, k, v, kk, a))

    def step(S, inp):
        r_t, w_t, k_t, v_t, kk_t, a_t = inp
        sa = jnp.einsum('bhvk,bhk->bhv', S, kk_t)
        S = S * w_t[:, :, None, :] - sa[..., None] * (kk_t * a_t)[:, :, None, :] + v_t[..., None] * k_t[:, :, None, :]
        return S, jnp.einsum('bhvk,bhk->bhv', S, r_t)

    S, o = lax.scan(step, S0, xs, reverse=reverse)
    return S, jnp.moveaxis(o, 0, 1)


def rwkv_output(o, r, v, k_bonus, r_k, ln_w, ln_b, g):
    B, T, H, N = o.shape
    mean = jnp.mean(o, axis=-1, keepdims=True)
    var = jnp.mean(jnp.square(o - mean), axis=-1, keepdims=True)
    gn = (o - mean) * lax.rsqrt(var + RWKV_GN_EPS)
    gn = gn * ln_w.reshape(H, N).astype(jnp.float32) + ln_b.reshape(H, N).astype(jnp.float32)
    bonus = jnp.sum((r * k_bonus * r_k).astype(jnp.float32), axis=-1, keepdims=True) * v.astype(jnp.float32)
    y = (gn + bonus).reshape(B, T, H * N).astype(g.dtype)
    return y * jax.nn.silu(g)


def centred_pool_minus_self(x, window):
    T = x.shape[1]
    xf = x.astype(jnp.float32)
    cs = jnp.pad(jnp.cumsum(xf, axis=1), ((0, 0), (1, 0), (0, 0)))
    t = jnp.arange(T)
    lo = jnp.clip(t - window // 2, 0, T - 1)
    hi = jnp.clip(t + window // 2 - 1, 0, T - 1)
    s = cs[:, hi + 1] - cs[:, lo]
    cnt = (hi - lo + 1).astype(jnp.float32)[None, :, None]
    return (s / cnt - xf).astype(x.dtype)


def pool_mixer(xp, g, pool_w, pool_scale):
    B, T, _ = xp.shape
    parts = [centred_pool_minus_self(xp[..., i * POOL_GROUP_DIM:(i + 1) * POOL_GROUP_DIM], w)
             for i, w in enumerate(POOL_WINDOWS)]
    p = jnp.stack(parts, axis=2)
    y = jnp.einsum('btgc,gcd->btgd', p, pool_w).reshape(B, T, POOL_WIDTH) * pool_scale
    return y * jax.nn.silu(g)


def hybrid_layer(x, xc, c, c_ctx, cos, sin, lp, need_ctx_out):
    B = x.shape[0]
    mod = jax.nn.silu(c) @ lp['ada_w'] + lp['ada_b']
    shift, scale, gate = jnp.split(mod, 3, axis=-1)
    mod_c = jax.nn.silu(c_ctx) @ lp['ada_w'] + lp['ada_b']
    shift_c, scale_c, gate_c = jnp.split(mod_c, 3, axis=-1)

    h = rms_norm(x, lp['pre_norm']) * (1.0 + scale[:, None]) + shift[:, None]
    hc = rms_norm(xc, lp['pre_norm']) * (1.0 + scale_c) + shift_c
    p = split_columns(h @ lp['w_in'])
    pc = split_columns(hc @ lp['w_in'])

    def qkv(pp, use_rope):
        q = rms_norm(to_heads(pp['att_q'], ATT_HEADS, ATT_HEAD_DIM), lp['q_norm'])
        k = rms_norm(to_heads(pp['att_k'], ATT_KV_HEADS, ATT_HEAD_DIM), lp['k_norm'])
        v = to_heads(pp['att_v'], ATT_KV_HEADS, ATT_HEAD_DIM)
        if use_rope:
            q = apply_rope_2d(q, cos, sin)
            k = apply_rope_2d(k, cos, sin)
        return q, k, v

    q_l, k_l, v_l = qkv(p, True)
    q_c, k_c, v_c = qkv(pc, False)
    keys = jnp.concatenate([k_c, k_l], axis=1)
    vals = jnp.concatenate([v_c, v_l], axis=1)
    att_l = gqa_block_attention(q_l, keys, vals) * jax.nn.silu(p['att_g'])

    rw_args = (lp['rw_mu'], lp['rw_w0'], lp['rw_w_up'], lp['rw_a0'], lp['rw_a_up'], lp['rw_k_k'], lp['rw_k_a'])
    r_c, vv_c, kk_c, dirs_c = rwkv_inputs(pc, *rw_args)
    r_l, vv_l, kk_l, dirs_l = rwkv_inputs(p, *rw_args)
    S0 = jnp.zeros((B, RWKV_HEADS, RWKV_HEAD_DIM, RWKV_HEAD_DIM), jnp.float32)
    Sf_c, of_c = rwkv_scan(S0, r_c, vv_c, kk_c, *dirs_c[0], reverse=False)
    Sb_c, ob_c = rwkv_scan(S0, r_c, vv_c, kk_c, *dirs_c[1], reverse=True)
    _, of_l = rwkv_scan(Sf_c, r_l, vv_l, kk_l, *dirs_l[0], reverse=False)
    _, ob_l = rwkv_scan(Sb_c, r_l, vv_l, kk_l, *dirs_l[1], reverse=True)
    kb_l = 0.5 * (dirs_l[0][1] + dirs_l[1][1])
    rw_l = rwkv_output(of_l + ob_l, r_l, vv_l, kb_l, lp['rw_r_k'], lp['rw_ln_w'], lp['rw_ln_b'], p['rw_g'])

    pool_l = pool_mixer(p['pool_x'], p['pool_g'], lp['pool_w'], lp['pool_scale'])

    y = jnp.concatenate([att_l.astype(x.dtype), rw_l.astype(x.dtype), pool_l.astype(x.dtype)], axis=-1) @ lp['w_out']
    x_new = x + gate[:, None] * rms_norm(y, lp['post_norm'])

    if need_ctx_out:
        att_c = gqa_block_attention(q_c, k_c, v_c) * jax.nn.silu(pc['att_g'])
        kb_c = 0.5 * (dirs_c[0][1] + dirs_c[1][1])
        rw_c = rwkv_output(of_c + ob_c, r_c, vv_c, kb_c, lp['rw_r_k'], lp['rw_ln_w'], lp['rw_ln_b'], pc['rw_g'])
        pool_c = pool_mixer(pc['pool_x'], pc['pool_g'], lp['pool_w'], lp['pool_scale'])
        yc = jnp.concatenate([att_c.astype(xc.dtype), rw_c.astype(xc.dtype), pool_c.astype(xc.dtype)], axis=-1) @ lp['w_out']
        xc = xc + gate_c * rms_norm(yc, lp['post_norm'])
    return x_new, xc


def setup_inputs(seed: int = 0) -> dict:
    key = jax.random.key(seed)
    ks = iter(jax.random.split(key, 40))
    f32 = jnp.float32
    nrm = lambda shape, s: jax.random.normal(next(ks), shape, f32) * s
    D = D_MODEL
    w0_base = jnp.linspace(-4.0, 1.0, RWKV_WIDTH, dtype=f32)[None, None, :]
    return {
        'x': nrm((BATCH, SEQ, D), 1.0),
        'c': nrm((BATCH, D), 1.0),
        'ctx': nrm((BATCH, CTX_LEN, D), 1.0),
        'c_ctx': nrm((D,), 1.0),
        'ada_w': nrm((DEPTH, D, 3 * D), D ** -0.5),
        'ada_b': nrm((DEPTH, 3 * D), 0.02),
        'pre_norm': 1.0 + nrm((DEPTH, D), 0.02),
        'post_norm': 1.0 + nrm((DEPTH, D), 0.02),
        'w_in': nrm((DEPTH, D, IN_WIDTH), D ** -0.5),
        'w_out': nrm((DEPTH, MIX_WIDTH, D), MIX_WIDTH ** -0.5),
        'q_norm': 1.0 + nrm((DEPTH, ATT_HEAD_DIM), 0.02),
        'k_norm': 1.0 + nrm((DEPTH, ATT_HEAD_DIM), 0.02),
        'rw_mu': jax.random.uniform(next(ks), (DEPTH, 3, 2, RWKV_WIDTH), f32, 0.0, 0.5),
        'rw_w0': w0_base + nrm((DEPTH, 2, RWKV_WIDTH), 0.1),
        'rw_w_up': nrm((DEPTH, 2, DECAY_RANK, RWKV_WIDTH), 0.1),
        'rw_a0': nrm((DEPTH, 2, RWKV_WIDTH), 0.5),
        'rw_a_up': nrm((DEPTH, 2, ICLR_RANK, RWKV_WIDTH), 0.1),
        'rw_k_k': 0.85 + nrm((DEPTH, RWKV_WIDTH), 0.05),
        'rw_k_a': 1.0 + nrm((DEPTH, RWKV_WIDTH), 0.05),
        'rw_r_k': nrm((DEPTH, RWKV_HEADS, RWKV_HEAD_DIM), 0.1),
        'rw_ln_w': 1.0 + nrm((DEPTH, RWKV_WIDTH), 0.02),
        'rw_ln_b': nrm((DEPTH, RWKV_WIDTH), 0.02),
        'pool_w': nrm((DEPTH, POOL_GROUPS, POOL_GROUP_DIM, POOL_GROUP_DIM), POOL_GROUP_DIM ** -0.5),
        'pool_scale': 1.0 + nrm((DEPTH, POOL_WIDTH), 0.1),
    }


def reference(x, c, ctx, c_ctx, ada_w, ada_b, pre_norm, post_norm, w_in, w_out, q_norm, k_norm,
              rw_mu, rw_w0, rw_w_up, rw_a0, rw_a_up, rw_k_k, rw_k_a, rw_r_k, rw_ln_w, rw_ln_b,
              pool_w, pool_scale):
    T = x.shape[1]
    rows = T // GRID_W
    cos, sin = axial_rope_tables(rows)
    xc = ctx
    for l in range(DEPTH):
        lp = {
            'ada_w': ada_w[l], 'ada_b': ada_b[l], 'pre_norm': pre_norm[l], 'post_norm': post_norm[l],
            'w_in': w_in[l], 'w_out': w_out[l], 'q_norm': q_norm[l], 'k_norm': k_norm[l],
            'rw_mu': rw_mu[l], 'rw_w0': rw_w0[l], 'rw_w_up': rw_w_up[l], 'rw_a0': rw_a0[l],
            'rw_a_up': rw_a_up[l], 'rw_k_k': rw_k_k[l], 'rw_k_a': rw_k_a[l], 'rw_r_k': rw_r_k[l],
            'rw_ln_w': rw_ln_w[l], 'rw_ln_b': rw_ln_b[l], 'pool_w': pool_w[l], 'pool_scale': pool_scale[l],
        }
        x, xc = hybrid_layer(x, xc, c, c_ctx, cos, sin, lp, need_ctx_out=(l < DEPTH - 1))
    return x
```

```python
import contextlib
import numpy as np
import concourse.bass as bass
import concourse.mybir as mybir
from concourse.bass_utils import run_bass_kernel_spmd

F32 = mybir.dt.float32
BF16 = mybir.dt.bfloat16
AF = mybir.ActivationFunctionType
ALU = mybir.AluOpType

D = 2048
TC = 256
TL = 2048
T = TC + TL
NIN = 5888
DEPTH = 2
EPS = 1e-6
TILES = [(0, 256), (256, 512), (768, 512), (1280, 512), (1792, 512)]
O_Q, O_K, O_V, O_G = 0, 1024, 1280, 1536
O_RR, O_RK, O_RV, O_RG = 2560, 3072, 3584, 4096
O_LW, O_LA = 4608, 4736
O_PX, O_PG = 4864, 5376
P_PRE, P_POST, P_QN, P_KN, P_MU, P_W0, P_A0, P_KK, P_KA, P_RK, P_LNW, P_LNB, P_PSC = (
    0, 16, 32, 33, 34, 58, 66, 74, 78, 82, 86, 90, 94)
NPAR = 98
SEM_CAP = 3000
XP = 8 + TC + 16 + TL + 8
XC0, XL0 = 8, 8 + TC + 16
POOL_W = (2, 4, 8, 16)


class StopPhase(Exception):
    pass


class Trk:
    __slots__ = ("w", "r", "excl")

    def __init__(self):
        self.w = {}
        self.r = {}
        self.excl = False


class V:
    def __init__(self, ap, trk=None):
        self.ap = ap
        self.t = trk if trk is not None else Trk()

    def __getitem__(self, idx):
        return V(self.ap[idx], self.t)

    def re(self, pat, **kw):
        return V(self.ap.rearrange(pat, **kw), self.t)

    def sub(self, idx):
        return V(self.ap[idx], Trk())


class K:
    def __init__(self, nc, es):
        self.nc = nc
        self.es = es
        self.eng = {"pe": nc.tensor, "dve": nc.vector, "act": nc.scalar, "pool": nc.gpsimd, "sp": nc.sync}
        self.cnt = {e: 0 for e in self.eng}
        self.sems = {e: [] for e in self.eng}
        self.seen = {e: {} for e in self.eng}
        self.ndma = 24
        self.dsem = [es.enter_context(nc.semaphore(f"dma{i}")) for i in range(self.ndma)]
        self.dval = [0] * self.ndma
        self.dnext = 0
        self.dpools = {"sp": list(range(0, 16)), "act": list(range(0, 16)), "pool": list(range(16, 24))}
        self.dpos = {"sp": 0, "act": 0, "pool": 0}
        self.same_sync = True

    def _sem(self, e, c):
        ep = (c - 1) // SEM_CAP
        while len(self.sems[e]) <= ep:
            self.sems[e].append(self.es.enter_context(self.nc.semaphore(f"s_{e}_{len(self.sems[e])}")))
        return self.sems[e][ep], c - ep * SEM_CAP

    def _wait(self, E, key, c):
        if c <= self.seen[E].get(key, 0):
            return
        self.seen[E][key] = c
        if isinstance(key, tuple):
            self.eng[E].wait_ge(self.dsem[key[1]], c)
        else:
            sem, val = self._sem(key, c)
            self.eng[E].wait_ge(sem, val)

    def _deps(self, E, outs, ins, skip_dma_waw=False):
        need = {}
        for v in ins:
            for k, c in v.t.w.items():
                need[k] = max(need.get(k, 0), c)
            if v.t.excl:
                for k, c in v.t.r.items():
                    if k != E:
                        need[k] = max(need.get(k, 0), c)
        for v in outs:
            for k, c in v.t.w.items():
                if skip_dma_waw and isinstance(k, tuple):
                    continue
                need[k] = max(need.get(k, 0), c)
            for k, c in v.t.r.items():
                need[k] = max(need.get(k, 0), c)
        for k, c in need.items():
            if k == E and (E == "pe" or not self.same_sync):
                continue
            self._wait(E, k, c)

    def op(self, E, fn, outs, ins, sig=True):
        self._deps(E, outs, ins)
        inst = fn()
        c = self.cnt[E] + 1
        if sig:
            self.cnt[E] = c
            sem, _ = self._sem(E, c)
            inst.then_inc(sem, 1)
        for v in ins:
            v.t.r[E] = c
        for v in outs:
            v.t.w = {E: c}
            v.t.r = {}
        return inst

    def dma(self, Q, out, in_, multi=False):
        self._deps(Q, [out], [in_], skip_dma_waw=multi)
        pl = self.dpools[Q]
        i = pl[self.dpos[Q] % len(pl)]
        self.dpos[Q] += 1
        self._wait(Q, ("dma", i), self.dval[i])
        inst = self.eng[Q].dma_start(out=out.ap, in_=in_.ap)
        self.dval[i] += 16
        inst.then_inc(self.dsem[i], 16)
        key = ("dma", i)
        in_.t.r[key] = self.dval[i]
        if multi:
            out.t.w = {k: c for k, c in out.t.w.items() if isinstance(k, tuple)}
            out.t.w[key] = self.dval[i]
        else:
            out.t.w = {key: self.dval[i]}
        out.t.r = {}

    def barrier(self):
        for E in self.eng:
            for e2 in self.eng:
                if e2 != E and self.cnt[e2] > 0:
                    self._wait(E, e2, self.cnt[e2])
            for i in range(self.ndma):
                if self.dval[i] > 0:
                    self._wait(E, ("dma", i), self.dval[i])

    def final_wait(self):
        for i in range(self.ndma):
            if self.dval[i] > 0:
                self._wait("sp", ("dma", i), self.dval[i])

    def mm(self, out, lhsT, rhs, start=True, stop=True, sig=True):
        return self.op("pe", lambda: self.nc.tensor.matmul(out.ap, lhsT.ap, rhs.ap, start=start, stop=stop),
                       [out], [lhsT, rhs], sig=sig)

    def tr(self, out, in_, ident):
        return self.op("pe", lambda: self.nc.tensor.transpose(out.ap, in_.ap, ident.ap), [out], [in_, ident])

    def act(self, out, in_, func, bias=None, scale=None):
        ins = [in_]
        kw = {}
        if bias is not None:
            if isinstance(bias, V):
                ins.append(bias)
                kw["bias"] = bias.ap
            else:
                kw["bias"] = float(bias)
        if scale is not None:
            if isinstance(scale, V):
                ins.append(scale)
                kw["scale"] = scale.ap
            else:
                kw["scale"] = float(scale)
        return self.op("act", lambda: self.nc.scalar.activation(out.ap, in_.ap, func, **kw), [out], ins)

    def _e(self, E):
        return self.eng[E]

    def tt(self, E, out, a, b, op):
        return self.op(E, lambda: self._e(E).tensor_tensor(out.ap, a.ap, b.ap, op), [out], [a, b])

    def ts(self, E, out, a, s1, op0, s2=None, op1=None):
        ins = [a]
        s1a = s1.ap if isinstance(s1, V) else float(s1)
        if isinstance(s1, V):
            ins.append(s1)
        if s2 is None:
            return self.op(E, lambda: self._e(E).tensor_scalar(out.ap, a.ap, s1a, None, op0), [out], ins)
        s2a = s2.ap if isinstance(s2, V) else float(s2)
        if isinstance(s2, V):
            ins.append(s2)
        return self.op(E, lambda: self._e(E).tensor_scalar(out.ap, a.ap, s1a, s2a, op0, op1), [out], ins)

    def stt(self, E, out, in0, s, in1, op0, op1):
        ins = [in0, in1]
        sa = s.ap if isinstance(s, V) else float(s)
        if isinstance(s, V):
            ins.append(s)
        return self.op(E, lambda: self._e(E).scalar_tensor_tensor(out.ap, in0.ap, sa, in1.ap, op0, op1), [out], ins)

    def copy(self, E, out, in_):
        if E == "act":
            return self.op(E, lambda: self.nc.scalar.copy(out.ap, in_.ap), [out], [in_])
        return self.op(E, lambda: self._e(E).tensor_copy(out.ap, in_.ap), [out], [in_])

    def memset(self, E, out, val):
        return self.op(E, lambda: self._e(E).memset(out.ap, val), [out], [])

    def recip(self, out, in_):
        return self.op("dve", lambda: self.nc.vector.reciprocal(out.ap, in_.ap), [out], [in_])

    def scan(self, out, d0, d1, init, op0, op1):
        return self.op("dve", lambda: self.nc.vector.tensor_tensor_scan(out.ap, d0.ap, d1.ap, init, op0, op1),
                       [out], [d0, d1])


def build_program(dbg=None, nlayers=DEPTH):
    nc = bass.Bass("TRN2", target_bir_lowering=False)
    dram = lambda n, s, d=F32, kind="ExternalInput": nc.dram_tensor(n, list(s), d, kind=kind).ap()
    xT_d = dram("xT", [D, T])
    cT_d = dram("cT", [128, 16, 2])
    adaw_d = dram("ada_w", [DEPTH, D, 3 * D])
    adab_d = dram("ada_bT", [DEPTH, 128, 48])
    win_d = dram("w_in", [DEPTH, D, NIN])
    wout_d = dram("w_out", [DEPTH, D, D])
    par_d = dram("params", [DEPTH, 128, NPAR])
    cst_d = dram("consts", [128, 128 * 4])
    rope_d = dram("rope", [128, 2 * TL])
    icnt_d = dram("icnt", [128, 4 * XP])
    poolw_d = dram("pool_w", [DEPTH, 4, 128, 128])
    rwc_d = dram("rwc", [128, 3328])
    rwwup_d = dram("rw_w_up", [DEPTH, 2, 64, 512])
    rwaup_d = dram("rw_a_up", [DEPTH, 2, 64, 512])
    outT_d = dram("outT", [D, TL], kind="ExternalOutput")
    mixT_d = dram("mixT_scr", [D, T], BF16, kind="Internal")
    PT_d = dram("PT_scr", [NIN, T], kind="Internal")
    x1T_d = dram("x1T_scr", [D, T], kind="Internal")
    dbg_d = {}
    if dbg:
        for n, s in dbg.items():
            if not n.startswith("_"):
                dbg_d[n] = dram(n, s, kind="ExternalOutput")

    with contextlib.ExitStack() as es:
        k = K(nc, es)
        sb = lambda n, s, d=F32: V(es.enter_context(nc.sbuf_tensor(n, list(s), d))[:])
        xT = V(xT_d); cT = V(cT_d); adaw = V(adaw_d); adab = V(adab_d); win = V(win_d); wout = V(wout_d)
        par = V(par_d); cst = V(cst_d); outT = V(outT_d); PT = V(PT_d); x1T = V(x1T_d)
        rwc = V(rwc_d)
        rope = V(rope_d); icnt = V(icnt_d); poolw = V(poolw_d); mixT = V(mixT_d)
        PTb = [PT.sub((slice(b_ * 128, (b_ + 1) * 128), slice(None))) for b_ in range(NIN // 128)]
        mixb = [mixT.sub((slice(b_ * 128, (b_ + 1) * 128), slice(None))) for b_ in range(16)]
        cons = sb("cons", [128, 512])
        ident, ones, perm, bones = cons[:, 0:128], cons[:, 128:256], cons[:, 256:384], cons[:, 384:512]
        prm = [sb(f"prm{l}", [128, NPAR]) for l in range(DEPTH)]
        mod = [sb(f"mod{l}", [128, 48, 2]) for l in range(DEPTH)]
        s1 = [sb(f"s1_{l}", [128, 16, 2]) for l in range(DEPTH)]
        g1 = [sb(f"g1_{l}", [128, 16, 2]) for l in range(DEPTH)]
        banks = [V(es.enter_context(nc.psum_tensor(f"ps{i}", [128, 512], F32))[:]) for i in range(8)]
        for b_ in banks:
            b_.t.excl = True
        bank_i = [0]
        uid = [0]

        def uname(n):
            uid[0] += 1
            return f"{n}_u{uid[0]}"

        def bank():
            b = banks[bank_i[0] % 8]
            bank_i[0] += 1
            return b

        k.dma("sp", cons, cst)
        consb = sb("consb", [128, 512], BF16)
        k.copy("dve", consb, cons)
        identb_g, ones_bg, perm_b, bones_b = consb[:, 0:128], consb[:, 128:256], consb[:, 256:384], consb[:, 384:512]
        for l in range(DEPTH):
            k.dma("sp", prm[l], par[l])

        def ada_gen(l, pb):
            with contextlib.ExitStack() as ps:
                lsb = lambda n, s, d=F32: V(ps.enter_context(nc.sbuf_tensor(uname(n), list(s), d))[:])
                ct = lsb("ada_ct", [128, 16, 2])
                cs = lsb("ada_cs", [128, 16, 2])
                ab = lsb("ada_b", [128, 48])
                wb = [lsb(f"ada_w{i}", [128, 16, 256]) for i in range(2)]
                k.dma("sp", ct, cT)
                k.dma("sp", ab, adab[l])
                k.act(cs, ct, AF.Silu)
                pv = pb[:, 0:96].re("p (b j) -> p b j", j=2)
                k.dma("sp", wb[0], adaw[l][:, 0:256].re("(c p) n -> p c n", p=128))
                for cb in range(24):
                    w = wb[cb % 2]
                    if cb + 1 < 24:
                        k.dma("sp", wb[(cb + 1) % 2], adaw[l][:, (cb + 1) * 256:(cb + 2) * 256].re("(c p) n -> p c n", p=128))
                    for j in range(2):
                        blk = cb * 2 + j
                        for c in range(16):
                            k.mm(pv[:, blk, :], w[:, c, j * 128:(j + 1) * 128], cs[:, c, :], start=(c == 0), stop=(c == 15))
                    yield
                for j in range(2):
                    k.tt("dve", mod[l][:, :, j], pv[:, :, j], ab, ALU.add)
                    k.stt("dve", s1[l][:, :, j], mod[l][:, 16:32, j], 1.0, prm[l][:, P_PRE:P_PRE + 16], ALU.add, ALU.mult)
                    k.tt("dve", g1[l][:, :, j], mod[l][:, 32:48, j], prm[l][:, P_POST:P_POST + 16], ALU.mult)

        def phase_ada(l):
            for _ in ada_gen(l, bank()):
                pass
            k.barrier()

        def phase_hproj(l, xsrc):
            with contextlib.ExitStack() as ps:
                lsb = lambda n, s, d=F32: V(ps.enter_context(nc.sbuf_tensor(uname(n), list(s), d))[:])
                hT = lsb("hT", [128, 16, T], BF16)
                hts = [hT.sub((slice(None), slice(None), slice(t0, t0 + tn))) for (t0, tn) in TILES]
                with contextlib.ExitStack() as ps2:
                    lsb2 = lambda n, s, d=F32: V(ps2.enter_context(nc.sbuf_tensor(uname(n), list(s), d))[:])
                    xt = [lsb2(f"h_x{i}", [128, 16, 512]) for i in range(2)]
                    sq = [lsb2(f"h_sq{i}", [128, 512], BF16) for i in range(3)]
                    rs = [lsb2(f"h_rs{i}", [128, 512]) for i in range(2)]
                    tmp = [lsb2(f"h_tmp{i}", [128, 512]) for i in range(3)]
                    for ti, (t0, tn) in enumerate(TILES):
                        j = 1 if ti == 0 else 0
                        x_ = xt[ti % 2]
                        k.dma("sp", x_[:, :, 0:tn], xsrc[:, t0:t0 + tn].re("(c p) t -> p c t", p=128))
                        pb = bank()
                        for c in range(16):
                            s_ = sq[c % 3]
                            k.act(s_[:, 0:tn], x_[:, c, 0:tn], AF.Square)
                            k.mm(pb[:, 0:tn], ones_bg, s_[:, 0:tn], start=(c == 0), stop=(c == 15))
                        r_ = rs[ti % 2]
                        k.act(r_[:, 0:tn], pb[:, 0:tn], AF.Sqrt, bias=EPS, scale=1.0 / D)
                        k.recip(r_[:, 0:tn], r_[:, 0:tn])
                        for c in range(16):
                            t_ = tmp[c % 3]
                            k.stt("dve", t_[:, 0:tn], x_[:, c, 0:tn], s1[l][:, c, j:j + 1], r_[:, 0:tn], ALU.mult, ALU.mult)
                            k.act(hts[ti][:, c, :], t_[:, 0:tn], AF.Identity, bias=mod[l][:, c, j:j + 1])
                    if dbg and "hTd" in dbg and l == dbg.get("_layer", 0):
                        hf = lsb2("h_dbg", [128, T])
                        for c in range(16):
                            for ti, (t0, tn) in enumerate(TILES):
                                k.copy("dve", hf[:, t0:t0 + tn], hts[ti][:, c, :])
                            k.dma("sp", V(dbg_d["hTd"])[c * 128:(c + 1) * 128, :], hf)
                k.barrier()
                if dbg and dbg.get("_stop") == "h":
                    return
                with contextlib.ExitStack() as ps2:
                    lsb2 = lambda n, s, d=F32: V(ps2.enter_context(nc.sbuf_tensor(uname(n), list(s), d))[:])
                    wbuf = [lsb2(f"p_w{i}", [128, 16, 256], BF16) for i in range(3)]
                    stg = [lsb2(f"p_stg{i}", [128, T]) for i in range(3)]
                    for gi in range(NIN // 256):
                        w = wbuf[gi % 3]
                        k.dma("pool", w, win[l][:, gi * 256:(gi + 1) * 256].re("(c p) n -> p c n", p=128))
                        for jb in range(2):
                            blk = gi * 2 + jb
                            st = stg[blk % 3]
                            for ti, (t0, tn) in enumerate(TILES):
                                pb = bank()
                                for c in range(16):
                                    k.mm(pb[:, 0:tn], w[:, c, jb * 128:(jb + 1) * 128], hts[ti][:, c, :],
                                         start=(c == 0), stop=(c == 15), sig=(c == 15))
                                if (blk * 5 + ti) % 2 == 0:
                                    k.copy("act", st[:, t0:t0 + tn], pb[:, 0:tn])
                                else:
                                    k.copy("dve", st[:, t0:t0 + tn], pb[:, 0:tn])
                            k.dma("sp", PTb[blk], st)
            k.barrier()

        def phase_pool(l):
            with contextlib.ExitStack() as ps:
                lsb = lambda n, s, d=F32: V(ps.enter_context(nc.sbuf_tensor(uname(n), list(s), d))[:])
                ic = lsb("pl_ic", [128, 4 * XP])
                pw = lsb("pl_w", [128, 4, 128], BF16)
                xp = [lsb(f"pl_xp{i}", [128, XP]) for i in range(2)]
                A = lsb("pl_A", [128, XP]); B = lsb("pl_B", [128, XP])
                pb16 = [lsb(f"pl_p{i}", [128, XP], BF16) for i in range(2)]
                graw = [lsb(f"pl_g{i}", [128, T]) for i in range(2)]
                stg = [lsb(f"pl_s{i}", [128, T], BF16) for i in range(2)]
                k.dma("sp", ic, icnt)
                k.dma("pool", pw, poolw[l].re("g c d -> c g d"))
                for g in range(4):
                    w = POOL_W[g]; half = w // 2
                    x_ = xp[g % 2]; p16 = pb16[g % 2]; gr = graw[g % 2]; st = stg[g % 2]
                    k.memset("pool", x_, 0.0)
                    k.dma("sp", x_[:, XC0:XC0 + TC], PTb[O_PX // 128 + g][:, 0:TC])
                    k.dma("sp", x_[:, XL0:XL0 + TL], PTb[O_PX // 128 + g][:, TC:T], multi=True)
                    k.dma("sp", gr, PTb[O_PG // 128 + g])
                    src = x_; n = XP; step = 1; bufs = [A, B]; bi = 0
                    while step < w:
                        dst = bufs[bi]; bi ^= 1
                        k.tt("dve", dst[:, 0:n - step], src[:, 0:n - step], src[:, step:n], ALU.add)
                        src = dst; n -= step; step *= 2
                    lo, hi = XC0, XL0 + TL
                    dst = bufs[bi]
                    k.tt("dve", dst[:, lo:hi], src[:, lo - half:hi - half], ic[:, g * XP + lo:g * XP + hi], ALU.mult)
                    k.tt("pool", p16[:, lo:hi], dst[:, lo:hi], x_[:, lo:hi], ALU.subtract)
                    k.act(gr, gr, AF.Silu)
                    for ti, (t0, tn) in enumerate(TILES):
                        xo = XC0 + t0 if ti == 0 else XL0 + (t0 - TC)
                        pb = bank()
                        k.mm(pb[:, 0:tn], pw[:, g, :], p16[:, xo:xo + tn])
                        k.stt("dve", st[:, t0:t0 + tn], pb[:, 0:tn], prm[l][:, P_PSC + g:P_PSC + g + 1], gr[:, t0:t0 + tn],
                              ALU.mult, ALU.mult)
                    k.dma("sp", mixb[12 + g], st)
            k.barrier()

        def phase_att(l, need_ctx, bg=None):
            SC = 128.0 ** -0.5
            with contextlib.ExitStack() as ps:
                lsb = lambda n, s, d=F32: V(ps.enter_context(nc.sbuf_tensor(uname(n), list(s), d))[:])
                rp = lsb("at_rope", [128, 2 * TL])
                raw = [lsb(f"at_raw{i}", [128, T]) for i in range(2)]
                sq = [lsb(f"at_sq{i}", [128, 512], BF16) for i in range(2)]
                rs = [lsb(f"at_rs{i}", [128, 512]) for i in range(2)]
                kn = [lsb(f"at_kn{i}", [128, 512], BF16) for i in range(2)]
                t1 = [lsb(f"at_t1{i}", [128, 512]) for i in range(2)]
                t2 = [lsb(f"at_t2{i}", [128, 512]) for i in range(2)]
                kT = [lsb(f"at_kT{i}", [128, T], BF16) for i in range(2)]
                qT = [lsb(f"at_qT{i}", [128, T], BF16) for i in range(8)]
                vtm = [lsb(f"at_v{i}", [128, 18, 128], BF16) for i in range(2)]
                sg = [lsb(f"at_sg{i}", [128, T], BF16) for i in range(8)]
                pt = [lsb(f"at_pt{i}", [128, 512], BF16) for i in range(6)]
                ones_b = lsb("at_ones", [128, 128], BF16)
                rd = [lsb(f"at_rd{i}", [128, 512]) for i in range(2)]
                ot = [lsb(f"at_o{i}", [128, 512]) for i in range(2)]
                stg = [lsb(f"at_stg{i}", [128, T], BF16) for i in range(2)]
                k.dma("sp", rp, rope)
                k.copy("dve", ones_b, ones)
                cnt = [0]

                def normrope(dst, src_blk, gcol):
                    r_ = raw[cnt[0] % 2]; cnt[0] += 1
                    k.dma("sp", r_, PTb[src_blk])
                    for ti, (t0, tn) in enumerate(TILES):
                        s_ = sq[ti % 2]; rr = rs[ti % 2]; kn_ = kn[ti % 2]
                        k.act(s_[:, 0:tn], r_[:, t0:t0 + tn], AF.Square)
                        pb = bank()
                        k.mm(pb[:, 0:tn], ones_bg, s_[:, 0:tn])
                        k.act(rr[:, 0:tn], pb[:, 0:tn], AF.Ln, bias=EPS, scale=1.0 / 128)
                        k.act(rr[:, 0:tn], rr[:, 0:tn], AF.Exp, scale=-0.5)
                        if ti == 0:
                            k.stt("dve", dst[:, t0:t0 + tn], r_[:, t0:t0 + tn], prm[l][:, gcol:gcol + 1], rr[:, 0:tn],
                                  ALU.mult, ALU.mult)
                            continue
                        k.stt("dve", kn_[:, 0:tn], r_[:, t0:t0 + tn], prm[l][:, gcol:gcol + 1], rr[:, 0:tn],
                              ALU.mult, ALU.mult)
                        pb2 = bank()
                        k.mm(pb2[:, 0:tn], perm_b, kn_[:, 0:tn])
                        a_ = t1[ti % 2]; b_ = t2[ti % 2]
                        lt = t0 - TC
                        k.tt("dve", a_[:, 0:tn], kn_[:, 0:tn], rp[:, lt:lt + tn], ALU.mult)
                        k.tt("dve", b_[:, 0:tn], pb2[:, 0:tn], rp[:, TL + lt:TL + lt + tn], ALU.mult)
                        k.tt("pool", dst[:, t0:t0 + tn], a_[:, 0:tn], b_[:, 0:tn], ALU.add)

                for g in range(2):
                    normrope(kT[g], O_K // 128 + g, P_KN)
                    vr = raw[cnt[0] % 2]; cnt[0] += 1
                    k.dma("sp", vr, PTb[O_V // 128 + g])
                    for kt in range(18):
                        pb = bank()
                        k.tr(pb[:, 0:128], vr[:, kt * 128:(kt + 1) * 128], ident)
                        k.copy("act" if kt % 2 else "dve", vtm[g][:, kt, :], pb[:, 0:128])
                for h in range(8):
                    normrope(qT[h], O_Q // 128 + h, P_QN)
                    gr = raw[cnt[0] % 2]; cnt[0] += 1
                    k.dma("sp", gr, PTb[O_G // 128 + h])
                    k.act(sg[h], gr, AF.Silu)
                gcount = 0
                for h in range(8):
                    g = h // 4
                    q_ = qT[h]; sg_ = sg[h]; st = stg[h % 2]
                    groups = [(TC + 512 * i, 512, 18) for i in range(4)]
                    if need_ctx:
                        groups = [(0, TC, 2)] + groups
                    for (q0, qn, nk) in groups:
                        par = gcount % 2; gcount += 1
                        po = banks[par]; pd = banks[2 + par]
                        sb_ = banks[4:7]
                        if bg is not None:
                            next(bg, None)
                        pend = []

                        def issue_s(kt):
                            pss = sb_[kt % 3]
                            k.mm(pss[:, 0:qn], kT[g][:, kt * 128:(kt + 1) * 128], q_[:, q0:q0 + qn])
                            p_ = pt[kt % 6]
                            k.act(p_[:, 0:qn], pss[:, 0:qn], AF.Exp, scale=SC)
                            return p_
                        look = 2
                        for kt in range(min(look, nk)):
                            pend.append(issue_s(kt))
                        for kt in range(nk):
                            p_ = pend[kt]
                            k.mm(po[:, 0:qn], vtm[g][:, kt, :], p_[:, 0:qn], start=(kt == 0), stop=(kt == nk - 1))
                            k.mm(pd[:, 0:qn], ones_b, p_[:, 0:qn], start=(kt == 0), stop=(kt == nk - 1))
                            if kt + look < nk:
                                pend.append(issue_s(kt + look))
                        rd_ = rd[par]; o_ = ot[par]
                        k.recip(rd_[:, 0:qn], pd[:, 0:qn])
                        k.tt("dve", o_[:, 0:qn], po[:, 0:qn], rd_[:, 0:qn], ALU.mult)
                        k.tt("pool", st[:, q0:q0 + qn], o_[:, 0:qn], sg_[:, q0:q0 + qn], ALU.mult)
                    if need_ctx:
                        k.dma("sp", mixb[h], st)
                    else:
                        k.dma("sp", mixb[h][:, TC:T], st[:, TC:T])
                if bg is not None:
                    for _ in bg:
                        pass
            k.barrier()

        def phase_rwkv(l, need_ctx):
            NCH, C, G = T // 128, 128, 3
            DEC = -float(np.exp(-0.5))
            with contextlib.ExitStack() as ps:
                lsb = lambda n, s, d=F32: V(ps.enter_context(nc.sbuf_tensor(uname(n), list(s), d))[:])
                big = lambda n: lsb(n, [128, T])
                rc = lsb("rw_c", [128, 3328])
                MU2, ML2 = rc[:, 0:512], rc[:, 512:1024]
                LV4, I4 = rc[:, 2304:2816], rc[:, 2816:3328]
                identb = lsb("rw_idb", [128, 128], BF16)
                tdw = big("rw_tdw"); da = big("rw_da")
                wup = lsb("rw_wup", [128, 512]); aup = lsb("rw_aup", [128, 512])
                c0s = lsb("rw_c0", [128, 12]); omka = lsb("rw_omka", [128, 4])
                KR = [lsb(f"rw_KR{d}", [128, 2, T], BF16) for d in range(2)]
                KMb = [lsb(f"rw_KMb{d}", [128, T], BF16) for d in range(2)]
                Abb = [lsb(f"rw_Abb{d}", [128, T], BF16) for d in range(2)]
                gt = [lsb(f"rw_gt{d}", [128, NCH]) for d in range(2)]
                vb = lsb("rw_vb", [128, T], BF16)
                v = big("rw_v"); Oacc = big("rw_O"); bsum = big("rw_bs")
                stg = lsb("rw_stg", [128, T], BF16)

                k.dma("sp", rc, rwc)
                k.copy("dve", identb, ident)
                k.dma("sp", wup, V(rwwup_d)[l].re("d r c -> (d r) c"))
                k.dma("sp", aup, V(rwaup_d)[l].re("d r c -> (d r) c"))
                k.dma("sp", tdw, PTb[O_LW // 128])
                k.dma("sp", da, PTb[O_LA // 128])
                k.act(tdw, tdw, AF.Tanh)
                for i in range(3):
                    mcol = P_MU + i * 8
                    k.ts("dve", c0s[:, i * 4:(i + 1) * 4], prm[l][:, mcol:mcol + 4], -1.0, ALU.mult, 1.0, ALU.add)
                    k.tt("dve", c0s[:, i * 4:(i + 1) * 4], c0s[:, i * 4:(i + 1) * 4], prm[l][:, mcol + 4:mcol + 8], ALU.subtract)
                k.ts("dve", omka, prm[l][:, P_KA:P_KA + 4], -1.0, ALU.mult, 1.0, ALU.add)
                SEGS = ((0, TC, 1), (TC, TL, TC + 2))
                orders = [list(range(NCH)), [1, 0] + list(range(NCH - 1, 1, -1))]
                M2s = [MU2, ML2]

                for blk in range(4):
                    with contextlib.ExitStack() as pp_:
                        psb = lambda n, s, d=F32: V(pp_.enter_context(nc.sbuf_tensor(uname(n), list(s), d))[:])
                        pbig = lambda n: psb(n, [128, T])
                        raw = psb("rw_raw", [128, T + 3]); smask = raw[:, 0:T]
                        r = pbig("rw_r"); kx = pbig("rw_k"); kk = pbig("rw_kk")
                        Pb = pbig("rw_P"); Pe = pbig("rw_Pe"); Ab = pbig("rw_A"); KM = pbig("rw_KM"); E = pbig("rw_E")
                        sq = [psb(f"rw_sq{i}", [128, 512], BF16) for i in range(2)]
                        rn = [psb(f"rw_rn{i}", [128, 512]) for i in range(2)]
                        k.memset("pool", raw, 0.0)
                        for i, (dst, off) in enumerate(((r, O_RR), (kx, O_RK), (v, O_RV))):
                            src = PTb[off // 128 + blk]
                            k.dma("sp", raw[:, 1:1 + TC], src[:, 0:TC])
                            k.dma("sp", raw[:, TC + 2:TC + 2 + TL], src[:, TC:T], multi=True)
                            mcol = P_MU + i * 8 + blk
                            for (a, n, ro) in SEGS:
                                k.ts("dve", dst[:, a:a + n], raw[:, ro:ro + n], c0s[:, i * 4 + blk:i * 4 + blk + 1], ALU.mult)
                                k.stt("dve", dst[:, a:a + n], raw[:, ro - 1:ro - 1 + n], prm[l][:, mcol:mcol + 1],
                                      dst[:, a:a + n], ALU.mult, ALU.add)
                                k.stt("dve", dst[:, a:a + n], raw[:, ro + 1:ro + 1 + n], prm[l][:, mcol + 4:mcol + 5],
                                      dst[:, a:a + n], ALU.mult, ALU.add)
                        k.copy("act", vb, v)
                        k.act(E, kx, AF.Copy, scale=prm[l][:, P_KK + blk:P_KK + blk + 1])
                        for ti, (t0, tn) in enumerate(TILES):
                            s_ = sq[ti % 2]; r_ = rn[ti % 2]
                            k.act(s_[:, 0:tn], E[:, t0:t0 + tn], AF.Square)
                            pb = bank()
                            k.mm(pb[:, 0:tn], bones_b, s_[:, 0:tn])
                            k.ts("dve", r_[:, 0:tn], pb[:, 0:tn], 1e-24, ALU.max)
                            k.act(r_[:, 0:tn], r_[:, 0:tn], AF.Ln)
                            k.act(r_[:, 0:tn], r_[:, 0:tn], AF.Exp, scale=-0.5)
                            k.tt("dve", kk[:, t0:t0 + tn], E[:, t0:t0 + tn], r_[:, 0:tn], ALU.mult)
                        k.memset("pool", smask, 1.0)
                        k.memset("pool", smask.re("p (c t) -> p c t", t=C)[:, :, 0:1], 0.0)
                        for d in range(2):
                            ds = slice(d * 64, (d + 1) * 64)
                            bc = slice(blk * 128, (blk + 1) * 128)
                            for ti, (t0, tn) in enumerate(TILES):
                                pb = bank()
                                k.mm(pb[:, 0:tn], wup[ds, bc], tdw[ds, t0:t0 + tn])
                                k.act(Pe[:, t0:t0 + tn], pb[:, 0:tn], AF.Sigmoid, bias=prm[l][:, P_W0 + d * 4 + blk:P_W0 + d * 4 + blk + 1])
                                pb2 = bank()
                                k.mm(pb2[:, 0:tn], aup[ds, bc], da[ds, t0:t0 + tn])
                                k.act(Ab[:, t0:t0 + tn], pb2[:, 0:tn], AF.Sigmoid, bias=prm[l][:, P_A0 + d * 4 + blk:P_A0 + d * 4 + blk + 1])
                            k.act(Pe, Pe, AF.Copy, scale=DEC)
                            k.scan(Pb, smask, Pe, 0.0, ALU.mult, ALU.add)
                            k.tt("dve", Pe, Pb, Pe, ALU.subtract)
                            Ptot = Pb.re("p (c t) -> p c t", t=C)[:, :, C - 1]
                            k.act(gt[d], Ptot, AF.Exp)
                            k.ts("dve", E, Ab, prm[l][:, P_KA + blk:P_KA + blk + 1], ALU.mult, omka[:, blk:blk + 1], ALU.add)
                            k.tt("dve", KM, E, kx, ALU.mult)
                            k.tt("pool", Ab, Ab, kk, ALU.mult)
                            k.stt("dve", E, r, prm[l][:, P_RK + blk:P_RK + blk + 1], KM, ALU.mult, ALU.mult)
                            for ti, (t0, tn) in enumerate(TILES):
                                pb = bank()
                                k.mm(pb[:, 0:tn], bones, E[:, t0:t0 + tn])
                                if d == 0:
                                    k.act(bsum[:, t0:t0 + tn], pb[:, 0:tn], AF.Copy, scale=0.5)
                                else:
                                    k.stt("dve", bsum[:, t0:t0 + tn], pb[:, 0:tn], 0.5, bsum[:, t0:t0 + tn], ALU.mult, ALU.add)
                            if d == 0:
                                Einc, Eexc, Tmp = Pb, Pe, E
                            else:
                                for c in range(NCH):
                                    cs_ = slice(c * C, (c + 1) * C)
                                    k.act(E[:, cs_], Pe[:, cs_], AF.Identity, bias=Ptot[:, c:c + 1], scale=-1.0)
                                for c in range(NCH):
                                    cs_ = slice(c * C, (c + 1) * C)
                                    k.act(Pe[:, cs_], Pb[:, cs_], AF.Identity, bias=Ptot[:, c:c + 1], scale=-1.0)
                                Einc, Eexc, Tmp = E, Pe, Pb
                            k.act(Tmp, Einc, AF.Exp)
                            k.tt("dve", KR[d][:, 1, :], r, Tmp, ALU.mult)
                            k.act(Tmp, Eexc, AF.Exp)
                            k.tt("dve", KR[d][:, 0, :], kk, Tmp, ALU.mult)
                            k.act(Tmp, Einc, AF.Exp, scale=-1.0)
                            k.tt("dve", KMb[d], KM, Tmp, ALU.mult)
                            k.tt("pool", Abb[d], Ab, Tmp, ALU.mult)
                    k.barrier()
                    with contextlib.ExitStack() as sp_:
                        ssb = lambda n, s, d=F32: V(sp_.enter_context(nc.sbuf_tensor(uname(n), list(s), d))[:])
                        NB = 2 * G
                        akbs = [ssb(f"rw_akb{i}", [128, 4, 512], BF16) for i in range(NB)]
                        ptms = [ssb(f"rw_ptm{i}", [128, 512], BF16) for i in range(NB)]
                        tms = [ssb(f"rw_tm{i}", [128, 768], BF16) for i in range(NB)]
                        invs = [ssb(f"rw_inv{i}", [128, 512], BF16) for i in range(G)]
                        ntms = [ssb(f"rw_ntm{i}", [128, 512], BF16) for i in range(G)]
                        t1s = [ssb(f"rw_t1{i}", [128, 512], BF16) for i in range(G)]
                        Xs = [ssb(f"rw_X{i}", [128, 256], BF16) for i in range(2)]
                        nUs = [ssb(f"rw_nU{i}", [128, 256], BF16) for i in range(2)]
                        Hs = [[ssb(f"rw_H{d}{i}", [128, 64]) for i in range(2)] for d in range(2)]
                        Hbs = [[ssb(f"rw_Hb{d}{i}", [128, 64], BF16) for i in range(2)] for d in range(2)]
                        hgs = [[ssb(f"rw_hg{d}{i}", [128, 64]) for i in range(2)] for d in range(2)]
                        sgate = ssb("rw_sg", [128, T])
                        ft = [ssb(f"rw_ft{i}", [128, 512]) for i in range(4)]
                        p3 = lambda t_: t_.re("p (a m) -> p a m", a=4)
                        k.memset("pool", Oacc, 0.0)
                        for d in range(2):
                            k.memset("dve", Hs[d][0], 0.0)
                            k.memset("dve", Hbs[d][0], 0.0)

                        def stage1(steps):
                            for st in steps:
                                sl = st % NB
                                cs = [slice(orders[d][st] * C, (orders[d][st] + 1) * C) for d in range(2)]
                                pA = bank(); pB = bank()
                                k.mm(pA[:, 0:128], vb[:, cs[0]], identb)
                                k.mm(pA[:, 128:256], KMb[0][:, cs[0]], identb)
                                k.mm(pA[:, 256:384], Abb[0][:, cs[0]], identb)
                                k.mm(pA[:, 384:512], vb[:, cs[1]], identb)
                                k.mm(pB[:, 0:128], KMb[1][:, cs[1]], identb)
                                k.mm(pB[:, 128:256], Abb[1][:, cs[1]], identb)
                                k.copy("act", tms[sl][:, 0:512], pA)
                                k.copy("act", tms[sl][:, 512:768], pB[:, 0:256])
                            yield
                            for st in steps:
                                sl = st % NB
                                cs = [slice(orders[d][st] * C, (orders[d][st] + 1) * C) for d in range(2)]
                                for d in range(2):
                                    for hb in range(2):
                                        hs = slice(hb * 64, (hb + 1) * 64)
                                        pa = bank()
                                        k.mm(pa[:, 0:256], KMb[d][hs, cs[d]], KR[d][hs, :, cs[d]])
                                        k.mm(pa[:, 256:512], Abb[d][hs, cs[d]], KR[d][hs, :, cs[d]])
                                        k.tt("dve", akbs[sl][:, d * 2 + hb, :], pa, M2s[d], ALU.mult)
                            yield
                            for st in steps:
                                sl = st % NB
                                NT4 = akbs[sl][:, :, 256:384]
                                nm4 = ntms[st % G]
                                k.stt("dve", p3(nm4), p3(LV4), 0.0, NT4, ALU.is_equal, ALU.mult)
                                k.tt("pool", ptms[sl], I4, nm4, ALU.subtract)
                            yield
                            for lv in range(1, 7):
                                for st in steps:
                                    sl = st % NB
                                    pq = bank()
                                    for ln in range(4):
                                        k.mm(pq[:, ln * 128:(ln + 1) * 128], ptms[sl][:, ln * 128:(ln + 1) * 128], identb)
                                    k.copy("act", invs[st % G], pq)
                                    k.stt("dve", p3(ntms[st % G]), p3(LV4), float(lv), akbs[sl][:, :, 256:384], ALU.is_equal, ALU.mult)
                                yield
                                for st in steps:
                                    pt1 = bank()
                                    for ln in range(4):
                                        k.mm(pt1[:, ln * 128:(ln + 1) * 128], ntms[st % G][:, ln * 128:(ln + 1) * 128],
                                             invs[st % G][:, ln * 128:(ln + 1) * 128])
                                    k.copy("act", t1s[st % G], pt1)
                                yield
                                for st in steps:
                                    sl = st % NB
                                    pp = bank()
                                    for ln in range(4):
                                        k.mm(pp[:, ln * 128:(ln + 1) * 128], t1s[st % G][:, ln * 128:(ln + 1) * 128],
                                             ptms[sl][:, ln * 128:(ln + 1) * 128])
                                    k.tt("dve", ptms[sl], ptms[sl], pp, ALU.subtract)
                                yield

                        def stage2(steps):
                            for st in steps:
                                sl = st % NB
                                akb = akbs[sl]; ptm = ptms[sl]; tm_ = tms[sl]
                                X = Xs[st % 2]; nU = nUs[st % 2]
                                cch = [orders[d][st] for d in range(2)]
                                cs = [slice(cch[d] * C, (cch[d] + 1) * C) for d in range(2)]
                                vtm = [tm_[:, 0:128], tm_[:, 384:512]]
                                kbtm = [tm_[:, 128:256], tm_[:, 512:640]]
                                bbtm = [tm_[:, 256:384], tm_[:, 640:768]]
                                Hc = [Hs[d][st % 2] for d in range(2)]; Hn = [Hs[d][(st + 1) % 2] for d in range(2)]
                                Hb = [Hbs[d][st % 2] for d in range(2)]; Hbn = [Hbs[d][(st + 1) % 2] for d in range(2)]
                                hg = [hgs[d][st % 2] for d in range(2)]
                                for d in range(2):
                                    k.act(hg[d], Hc[d], AF.Copy, scale=gt[d][:, cch[d]:cch[d] + 1])
                                px = bank()
                                for d in range(2):
                                    for hb in range(2):
                                        ln = d * 2 + hb
                                        hs = slice(hb * 64, (hb + 1) * 64); vs = slice(hb * 64, (hb + 1) * 64)
                                        k.mm(px[:, ln * 64:(ln + 1) * 64], KR[d][hs, 0, cs[d]], Hb[d][hs, :], start=True, stop=False)
                                        k.mm(px[:, ln * 64:(ln + 1) * 64], akb[:, ln, 0:128], vtm[d][:, vs], start=False, stop=True)
                                k.copy("act", X, px[:, 0:256])
                                yield
                                pu = bank()
                                for ln in range(4):
                                    k.mm(pu[:, ln * 64:(ln + 1) * 64], ptm[:, ln * 128:(ln + 1) * 128], X[:, ln * 64:(ln + 1) * 64])
                                k.act(nU, pu[:, 0:256], AF.Copy, scale=-1.0)
                                yield
                                ph = bank()
                                for d in range(2):
                                    for hb in range(2):
                                        ln = d * 2 + hb
                                        hs = slice(hb * 64, (hb + 1) * 64); vs = slice(hb * 64, (hb + 1) * 64)
                                        k.mm(ph[hs, d * 64:(d + 1) * 64], kbtm[d][:, hs], vtm[d][:, vs], start=True, stop=False)
                                        k.mm(ph[hs, d * 64:(d + 1) * 64], bbtm[d][:, hs], nU[:, ln * 64:(ln + 1) * 64], start=False, stop=True)
                                for d in range(2):
                                    k.stt("dve", Hn[d], ph[:, d * 64:(d + 1) * 64], gt[d][:, cch[d]:cch[d] + 1], hg[d], ALU.mult, ALU.add)
                                    k.copy("pool", Hbn[d], Hn[d])
                                po = bank()
                                for d in range(2):
                                    if not (need_ctx or cch[d] >= 2):
                                        continue
                                    for hb in range(2):
                                        ln = d * 2 + hb
                                        hs = slice(hb * 64, (hb + 1) * 64); vs = slice(hb * 64, (hb + 1) * 64)
                                        k.mm(po[hs, d * 128:(d + 1) * 128], Hb[d][hs, :], KR[d][hs, 1, cs[d]], start=True, stop=False)
                                        k.mm(po[hs, d * 128:(d + 1) * 128], vtm[d][:, vs], akb[:, ln, 128:256], start=False, stop=False)
                                        k.mm(po[hs, d * 128:(d + 1) * 128], nU[:, ln * 64:(ln + 1) * 64], akb[:, ln, 384:512], start=False, stop=True)
                                    k.tt("dve", Oacc[:, cs[d]], Oacc[:, cs[d]], po[:, d * 128:(d + 1) * 128], ALU.add)
                                yield

                        groups = [list(range(g0, g0 + G)) for g0 in range(0, NCH, G)]
                        for _ in stage1(groups[0]):
                            pass
                        for gi in range(len(groups)):
                            g1 = stage1(groups[gi + 1]) if gi + 1 < len(groups) else iter(())
                            g2 = stage2(groups[gi])
                            a1 = a2 = True
                            while a1 or a2:
                                if a1:
                                    a1 = next(g1, "end") != "end"
                                if a1:
                                    a1 = next(g1, "end") != "end"
                                if a2:
                                    a2 = next(g2, "end") != "end"
                        k.dma("sp", sgate, PTb[O_RG // 128 + blk])
                        k.act(sgate, sgate, AF.Silu)
                        for ti, (t0, tn) in enumerate(TILES):
                            if ti == 0 and not need_ctx:
                                continue
                            ts_ = slice(t0, t0 + tn)
                            oc = ft[0]; s_ = ft[1]; r_ = ft[2]; bb_ = ft[3]
                            pb = bank()
                            k.mm(pb[:, 0:tn], bones, Oacc[:, ts_])
                            k.stt("dve", oc[:, 0:tn], pb[:, 0:tn], -1.0 / 64, Oacc[:, ts_], ALU.mult, ALU.add)
                            k.act(s_[:, 0:tn], oc[:, 0:tn], AF.Square)
                            pb2 = bank()
                            k.mm(pb2[:, 0:tn], bones, s_[:, 0:tn])
                            k.act(r_[:, 0:tn], pb2[:, 0:tn], AF.Ln, bias=64e-5, scale=1.0 / 64)
                            k.act(r_[:, 0:tn], r_[:, 0:tn], AF.Exp, scale=-0.5)
                            k.tt("dve", oc[:, 0:tn], oc[:, 0:tn], r_[:, 0:tn], ALU.mult)
                            k.ts("dve", oc[:, 0:tn], oc[:, 0:tn], prm[l][:, P_LNW + blk:P_LNW + blk + 1], ALU.mult,
                                 prm[l][:, P_LNB + blk:P_LNB + blk + 1], ALU.add)
                            k.tt("pool", bb_[:, 0:tn], bsum[:, ts_], v[:, ts_], ALU.mult)
                            k.tt("pool", oc[:, 0:tn], oc[:, 0:tn], bb_[:, 0:tn], ALU.add)
                            k.tt("pool", stg[:, ts_], oc[:, 0:tn], sgate[:, ts_], ALU.mult)
                        if need_ctx:
                            k.dma("sp", mixb[8 + blk], stg)
                        else:
                            k.dma("sp", mixb[8 + blk][:, TC:T], stg[:, TC:T])
                    k.barrier()
            k.barrier()

        def phase_out(l, xsrc, xdst, need_ctx):
            TN = 256
            tiles = [(t0, TN) for t0 in range(0 if need_ctx else TC, T, TN)]
            with contextlib.ExitStack() as ps:
                lsb = lambda n, s, d=F32: V(ps.enter_context(nc.sbuf_tensor(uname(n), list(s), d))[:])
                wo = lsb("o_w", [128, 16, D], BF16)
                wos = [wo.sub((slice(None), slice(None), slice(i * 512, (i + 1) * 512))) for i in range(4)]
                mx = [lsb(f"o_mx{i}", [128, 16, TN], BF16) for i in range(3)]
                xt = [lsb(f"o_x{i}", [128, 16, TN]) for i in range(3)]
                y2 = [lsb(f"o_y{i}", [128, 16, TN]) for i in range(2)]
                ys2 = [[y_.sub((slice(None), i, slice(None))) for i in range(16)] for y_ in y2]
                sq = [lsb(f"o_sq{i}", [128, TN], BF16) for i in range(4)]
                rs2 = [lsb(f"o_rs{i}", [128, TN]) for i in range(2)]
                for i in range(4):
                    k.dma("pool", wos[i], wout[l][:, i * 512:(i + 1) * 512].re("(c p) n -> p c n", p=128))

                def load(it):
                    t0, tn = tiles[it]
                    k.dma("sp", mx[it % 3], mixT[:, t0:t0 + tn].re("(c p) t -> p c t", p=128))
                    k.dma("sp", xt[it % 3], xsrc[:, t0:t0 + tn].re("(c p) t -> p c t", p=128))

                load(0)
                if len(tiles) > 1:
                    load(1)
                for it, (t0, tn) in enumerate(tiles):
                    j = 1 if t0 < TC else 0
                    m_ = mx[it % 3]; x_ = xt[it % 3]; ys = ys2[it % 2]; rs = rs2[it % 2]
                    pss = banks[0]
                    for db in range(16):
                        pb = banks[1 + db % 7]
                        for c in range(16):
                            k.mm(pb[:, 0:tn], wos[db // 4][:, c, (db % 4) * 128:(db % 4 + 1) * 128], m_[:, c, :],
                                 start=(c == 0), stop=(c == 15), sig=(c == 15))
                        k.copy("dve" if db % 2 else "act", ys[db], pb[:, 0:tn])
                        k.act(sq[db % 4], ys[db], AF.Square)
                        if db >= 2:
                            k.mm(pss[:, 0:tn], ones_bg, sq[(db - 2) % 4], start=(db == 2), stop=False)
                    for db in (14, 15):
                        k.mm(pss[:, 0:tn], ones_bg, sq[db % 4], start=False, stop=(db == 15))
                    if it + 2 < len(tiles):
                        load(it + 2)
                    k.act(rs, pss[:, 0:tn], AF.Ln, bias=EPS, scale=1.0 / D)
                    k.act(rs, rs, AF.Exp, scale=-0.5)
                    for db in range(16):
                        k.stt("dve", ys[db], ys[db], g1[l][:, db, j:j + 1], rs, ALU.mult, ALU.mult)
                        k.tt("pool" if db % 3 == 2 else "dve", x_[:, db, :], x_[:, db, :], ys[db], ALU.add)
                    if xdst is outT:
                        k.dma("sp", outT[:, t0 - TC:t0 - TC + tn].re("(c p) t -> p c t", p=128), x_)
                    else:
                        k.dma("sp", xdst[:, t0:t0 + tn].re("(c p) t -> p c t", p=128), x_)
            k.barrier()

        stop = dbg.get("_stop") if dbg else None
        only = dbg.get("_only") if dbg else None
        phase_ada(0)
        xsrc = xT
        for l in range(nlayers):
            last = (l == DEPTH - 1)
            if stop == "ada":
                break
            phase_hproj(l, xsrc)
            if dbg and "PT" in dbg and l == dbg.get("_layer", 0):
                with contextlib.ExitStack() as ps:
                    bnc = V(ps.enter_context(nc.sbuf_tensor(uname("dbg_b"), [128, T], F32))[:])
                    for blk in range(NIN // 128):
                        k.dma("sp", bnc, PTb[blk])
                        k.dma("sp", V(dbg_d["PT"])[blk * 128:(blk + 1) * 128, :], bnc)
                k.barrier()
            if stop in ("h", "proj"):
                break
            if dbg and dbg.get("_zero_mix"):
                with contextlib.ExitStack() as ps:
                    zz = V(ps.enter_context(nc.sbuf_tensor(uname("dbg_z"), [128, T], BF16))[:])
                    k.memset("dve", zz, 0.0)
                    for b_ in range(16):
                        k.dma("sp", mixb[b_], zz)
                k.barrier()
            if only is None or "pool" in only:
                phase_pool(l)
            if only is None or "att" in only:
                bg = ada_gen(l + 1, banks[7]) if (l + 1 < nlayers) else None
                phase_att(l, need_ctx=not last, bg=bg)
            elif l + 1 < nlayers:
                phase_ada(l + 1)
            if only is None or "rwkv" in only:
                if phase_rwkv(l, need_ctx=not last):
                    break
            if dbg and "mix" in dbg and l == dbg.get("_layer", 0):
                with contextlib.ExitStack() as ps:
                    b16 = V(ps.enter_context(nc.sbuf_tensor(uname("dbg_m16"), [128, T], BF16))[:])
                    b32 = V(ps.enter_context(nc.sbuf_tensor(uname("dbg_m32"), [128, T], F32))[:])
                    for b_ in range(16):
                        k.dma("sp", b16, mixb[b_])
                        k.copy("dve", b32, b16)
                        k.dma("sp", V(dbg_d["mix"])[b_ * 128:(b_ + 1) * 128, :], b32)
                k.barrier()
            if stop == "mix":
                break
            phase_out(l, xsrc, outT if last else x1T, need_ctx=not last)
            if dbg and "x1" in dbg and l == 0:
                with contextlib.ExitStack() as ps:
                    bnc = V(ps.enter_context(nc.sbuf_tensor(uname("dbg_x"), [128, T], F32))[:])
                    for b_ in range(16):
                        k.dma("sp", bnc, x1T[b_ * 128:(b_ + 1) * 128, :])
                        k.dma("sp", V(dbg_d["x1"])[b_ * 128:(b_ + 1) * 128, :], bnc)
                k.barrier()
            xsrc = x1T
        if dbg and "mod" in dbg:
            for l in range(nlayers):
                k.dma("sp", V(dbg_d["mod"])[l], mod[l])
        k.barrier()
        k.final_wait()
    return nc


def make_consts():
    c = np.zeros((128, 512), np.float32)
    c[:, 0:128] = np.eye(128, dtype=np.float32)
    c[:, 128:256] = 1.0
    for i in range(128):
        j = i % 64
        pi = i + 32 if j < 32 else i - 32
        c[pi, 256 + i] = 1.0
    c[0:64, 384:448] = 1.0
    c[64:128, 448:512] = 1.0
    return c


def make_rope():
    half = 64
    nfreq = 32
    inv = (np.float32(10000.0) ** (-(np.arange(nfreq, dtype=np.float32) * np.float32(2.0) / np.float32(half)))).astype(np.float32)
    t = np.arange(TL)
    row = (t // 64).astype(np.float32)
    col = (t % 64).astype(np.float32)
    out = np.zeros((128, 2 * TL), np.float32)
    for i in range(128):
        axis, hf, fr = i // 64, (i % 64) // 32, i % 32
        ang = ((row if axis == 0 else col) * inv[fr]).astype(np.float32)
        out[i, :TL] = np.cos(ang)
        out[i, TL:] = np.sin(ang) * (-1.0 if hf == 0 else 1.0)
    return out


def make_icnt():
    tab = np.zeros((4, XP), np.float32)
    for g, w in enumerate(POOL_W):
        h = w // 2
        for (x0, n) in ((XC0, TC), (XL0, TL)):
            t = np.arange(n)
            lo = np.clip(t - h, 0, n - 1)
            hi = np.clip(t + h - 1, 0, n - 1)
            tab[g, x0:x0 + n] = 1.0 / (hi - lo + 1).astype(np.float32)
    return np.ascontiguousarray(np.broadcast_to(tab.reshape(1, -1), (128, 4 * XP)))


def make_rwc():
    c = np.zeros((128, 3328), np.float32)
    ii = np.arange(128)
    x = ii[:, None] ^ ii[None, :]
    lvl = np.full((128, 128), -1.0, np.float32)
    nz = x > 0
    lvl[nz] = np.floor(np.log2(x[nz])).astype(np.float32)
    lvL = np.where(ii[:, None] > ii[None, :], lvl, -1.0).astype(np.float32)
    c[:, 1792:2048] = np.concatenate([lvL, lvL], axis=1)
    c[:, 2048:2304] = np.concatenate([lvL.T, lvL.T], axis=1)
    c[:, 2304:2816] = np.concatenate([lvL.T, lvL.T, lvL, lvL], axis=1)
    c[:, 2816:3328] = np.concatenate([np.eye(128, dtype=np.float32)] * 4, axis=1)
    tS = np.triu(np.ones((128, 128), np.float32), 1)
    tI = np.triu(np.ones((128, 128), np.float32), 0)
    c[:, 0:512] = np.concatenate([tS, tI, tS, tI], axis=1)
    c[:, 512:1024] = np.concatenate([tS.T, tI.T, tS.T, tI.T], axis=1)
    c[:, 1024:1280] = np.concatenate([np.eye(128, dtype=np.float32)] * 2, axis=1)
    c[:, 1280:1536] = np.concatenate([tS.T, tS.T], axis=1)
    c[:, 1536:1792] = np.concatenate([tS, tS], axis=1)
    return c


def chunkT(v):
    return np.ascontiguousarray(v.reshape(-1, 128).T)


def make_inputs(b, inp):
    f = np.float32
    xT = np.ascontiguousarray(np.concatenate([inp["ctx"][b], inp["x"][b]], axis=0).T.astype(f))
    cT = np.stack([chunkT(inp["c"][b]), chunkT(inp["c_ctx"])], axis=-1).astype(f)
    params = np.zeros((DEPTH, 128, NPAR), f)
    for l in range(DEPTH):
        P = params[l]
        P[:, P_PRE:P_PRE + 16] = chunkT(inp["pre_norm"][l])
        P[:, P_POST:P_POST + 16] = chunkT(inp["post_norm"][l])
        P[:, P_QN] = inp["q_norm"][l]
        P[:, P_KN] = inp["k_norm"][l]
        P[:, P_MU:P_MU + 24] = chunkT(inp["rw_mu"][l].reshape(-1))
        P[:, P_W0:P_W0 + 8] = chunkT(inp["rw_w0"][l].reshape(-1))
        P[:, P_A0:P_A0 + 8] = chunkT(inp["rw_a0"][l].reshape(-1))
        P[:, P_KK:P_KK + 4] = chunkT(inp["rw_k_k"][l])
        P[:, P_KA:P_KA + 4] = chunkT(inp["rw_k_a"][l])
        P[:, P_RK:P_RK + 4] = chunkT(inp["rw_r_k"][l].reshape(-1))
        P[:, P_LNW:P_LNW + 4] = chunkT(inp["rw_ln_w"][l])
        P[:, P_LNB:P_LNB + 4] = chunkT(inp["rw_ln_b"][l])
        P[:, P_PSC:P_PSC + 4] = chunkT(inp["pool_scale"][l])
    adabT = np.stack([chunkT(inp["ada_b"][l]) for l in range(DEPTH)]).astype(f)
    return {
        "xT": xT, "cT": np.ascontiguousarray(cT), "ada_w": inp["ada_w"], "ada_bT": adabT,
        "w_in": inp["w_in"], "w_out": inp["w_out"], "params": params, "consts": make_consts(),
        "rope": make_rope(), "icnt": make_icnt(), "pool_w": inp["pool_w"],
        "rwc": make_rwc(), "rw_w_up": inp["rw_w_up"], "rw_a_up": inp["rw_a_up"],
    }


def kernel(**inputs):
    inp = {k_: np.asarray(v) for k_, v in inputs.items()}
    nc = build_program()
    in_maps = [make_inputs(c % 4, inp) for c in range(8)]
    res = run_bass_kernel_spmd(nc, in_maps, core_ids=list(range(8)))
    out = np.stack([np.ascontiguousarray(res.results[b]["outT"].T) for b in range(4)], axis=0)
    return out.astype(np.float32)
```

```python
import contextlib
import numpy as np
import concourse.bass as bass
import concourse.mybir as mybir
from concourse.bass_utils import run_bass_kernel_spmd

F32 = mybir.dt.float32
BF16 = mybir.dt.bfloat16
AF = mybir.ActivationFunctionType
ALU = mybir.AluOpType

D = 2048
TC = 256
TL = 2048
T = TC + TL
NIN = 5888
DEPTH = 2
EPS = 1e-6
TILES = [(0, 256), (256, 512), (768, 512), (1280, 512), (1792, 512)]
O_Q, O_K, O_V, O_G = 0, 1024, 1280, 1536
O_RR, O_RK, O_RV, O_RG = 2560, 3072, 3584, 4096
O_LW, O_LA = 4608, 4736
O_PX, O_PG = 4864, 5376
P_PRE, P_POST, P_QN, P_KN, P_MU, P_W0, P_A0, P_KK, P_KA, P_RK, P_LNW, P_LNB, P_PSC = (
    0, 16, 32, 33, 34, 58, 66, 74, 78, 82, 86, 90, 94)
NPAR = 98
SEM_CAP = 3000
XP = 8 + TC + 16 + TL + 8
XC0, XL0 = 8, 8 + TC + 16
POOL_W = (2, 4, 8, 16)


class StopPhase(Exception):
    pass


class Trk:
    __slots__ = ("w", "r", "excl")

    def __init__(self):
        self.w = {}
        self.r = {}
        self.excl = False


class V:
    def __init__(self, ap, trk=None):
        self.ap = ap
        self.t = trk if trk is not None else Trk()

    def __getitem__(self, idx):
        return V(self.ap[idx], self.t)

    def re(self, pat, **kw):
        return V(self.ap.rearrange(pat, **kw), self.t)

    def sub(self, idx):
        return V(self.ap[idx], Trk())


class K:
    def __init__(self, nc, es):
        self.nc = nc
        self.es = es
        self.eng = {"pe": nc.tensor, "dve": nc.vector, "act": nc.scalar, "pool": nc.gpsimd, "sp": nc.sync}
        self.cnt = {e: 0 for e in self.eng}
        self.sems = {e: [] for e in self.eng}
        self.seen = {e: {} for e in self.eng}
        self.ndma = 24
        self.dsem = [es.enter_context(nc.semaphore(f"dma{i}")) for i in range(self.ndma)]
        self.dval = [0] * self.ndma
        self.dnext = 0
        self.dpools = {"sp": list(range(0, 16)), "act": list(range(0, 16)), "pool": list(range(16, 24))}
        self.dpos = {"sp": 0, "act": 0, "pool": 0}
        self.same_sync = True

    def _sem(self, e, c):
        ep = (c - 1) // SEM_CAP
        while len(self.sems[e]) <= ep:
            self.sems[e].append(self.es.enter_context(self.nc.semaphore(f"s_{e}_{len(self.sems[e])}")))
        return self.sems[e][ep], c - ep * SEM_CAP

    def _wait(self, E, key, c):
        if c <= self.seen[E].get(key, 0):
            return
        self.seen[E][key] = c
        if isinstance(key, tuple):
            self.eng[E].wait_ge(self.dsem[key[1]], c)
        else:
            sem, val = self._sem(key, c)
            self.eng[E].wait_ge(sem, val)

    def _deps(self, E, outs, ins, skip_dma_waw=False):
        need = {}
        for v in ins:
            for k, c in v.t.w.items():
                need[k] = max(need.get(k, 0), c)
            if v.t.excl:
                for k, c in v.t.r.items():
                    if k != E:
                        need[k] = max(need.get(k, 0), c)
        for v in outs:
            for k, c in v.t.w.items():
                if skip_dma_waw and isinstance(k, tuple):
                    continue
                need[k] = max(need.get(k, 0), c)
            for k, c in v.t.r.items():
                need[k] = max(need.get(k, 0), c)
        for k, c in need.items():
            if k == E and (E == "pe" or not self.same_sync):
                continue
            self._wait(E, k, c)

    def op(self, E, fn, outs, ins, sig=True):
        self._deps(E, outs, ins)
        inst = fn()
        c = self.cnt[E] + 1
        if sig:
            self.cnt[E] = c
            sem, _ = self._sem(E, c)
            inst.then_inc(sem, 1)
        for v in ins:
            v.t.r[E] = c
        for v in outs:
            v.t.w = {E: c}
            v.t.r = {}
        return inst

    def dma(self, Q, out, in_, multi=False):
        self._deps(Q, [out], [in_], skip_dma_waw=multi)
        pl = self.dpools[Q]
        i = pl[self.dpos[Q] % len(pl)]
        self.dpos[Q] += 1
        self._wait(Q, ("dma", i), self.dval[i])
        inst = self.eng[Q].dma_start(out=out.ap, in_=in_.ap)
        self.dval[i] += 16
        inst.then_inc(self.dsem[i], 16)
        key = ("dma", i)
        in_.t.r[key] = self.dval[i]
        if multi:
            out.t.w = {k: c for k, c in out.t.w.items() if isinstance(k, tuple)}
            out.t.w[key] = self.dval[i]
        else:
            out.t.w = {key: self.dval[i]}
        out.t.r = {}

    def barrier(self):
        for E in self.eng:
            for e2 in self.eng:
                if e2 != E and self.cnt[e2] > 0:
                    self._wait(E, e2, self.cnt[e2])
            for i in range(self.ndma):
                if self.dval[i] > 0:
                    self._wait(E, ("dma", i), self.dval[i])

    def final_wait(self):
        for i in range(self.ndma):
            if self.dval[i] > 0:
                self._wait("sp", ("dma", i), self.dval[i])

    def mm(self, out, lhsT, rhs, start=True, stop=True, sig=True):
        return self.op("pe", lambda: self.nc.tensor.matmul(out.ap, lhsT.ap, rhs.ap, start=start, stop=stop),
                       [out], [lhsT, rhs], sig=sig)

    def tr(self, out, in_, ident):
        return self.op("pe", lambda: self.nc.tensor.transpose(out.ap, in_.ap, ident.ap), [out], [in_, ident])

    def act(self, out, in_, func, bias=None, scale=None):
        ins = [in_]
        kw = {}
        if bias is not None:
            if isinstance(bias, V):
                ins.append(bias)
                kw["bias"] = bias.ap
            else:
                kw["bias"] = float(bias)
        if scale is not None:
            if isinstance(scale, V):
                ins.append(scale)
                kw["scale"] = scale.ap
            else:
                kw["scale"] = float(scale)
        return self.op("act", lambda: self.nc.scalar.activation(out.ap, in_.ap, func, **kw), [out], ins)

    def _e(self, E):
        return self.eng[E]

    def tt(self, E, out, a, b, op):
        return self.op(E, lambda: self._e(E).tensor_tensor(out.ap, a.ap, b.ap, op), [out], [a, b])

    def ts(self, E, out, a, s1, op0, s2=None, op1=None):
        ins = [a]
        s1a = s1.ap if isinstance(s1, V) else float(s1)
        if isinstance(s1, V):
            ins.append(s1)
        if s2 is None:
            return self.op(E, lambda: self._e(E).tensor_scalar(out.ap, a.ap, s1a, None, op0), [out], ins)
        s2a = s2.ap if isinstance(s2, V) else float(s2)
        if isinstance(s2, V):
            ins.append(s2)
        return self.op(E, lambda: self._e(E).tensor_scalar(out.ap, a.ap, s1a, s2a, op0, op1), [out], ins)

    def stt(self, E, out, in0, s, in1, op0, op1):
        ins = [in0, in1]
        sa = s.ap if isinstance(s, V) else float(s)
        if isinstance(s, V):
            ins.append(s)
        return self.op(E, lambda: self._e(E).scalar_tensor_tensor(out.ap, in0.ap, sa, in1.ap, op0, op1), [out], ins)

    def copy(self, E, out, in_):
        if E == "act":
            return self.op(E, lambda: self.nc.scalar.copy(out.ap, in_.ap), [out], [in_])
        return self.op(E, lambda: self._e(E).tensor_copy(out.ap, in_.ap), [out], [in_])

    def memset(self, E, out, val):
        return self.op(E, lambda: self._e(E).memset(out.ap, val), [out], [])

    def recip(self, out, in_):
        return self.op("dve", lambda: self.nc.vector.reciprocal(out.ap, in_.ap), [out], [in_])

    def scan(self, out, d0, d1, init, op0, op1):
        return self.op("dve", lambda: self.nc.vector.tensor_tensor_scan(out.ap, d0.ap, d1.ap, init, op0, op1),
                       [out], [d0, d1])


def build_program(dbg=None, nlayers=DEPTH):
    nc = bass.Bass("TRN2", target_bir_lowering=False)
    dram = lambda n, s, d=F32, kind="ExternalInput": nc.dram_tensor(n, list(s), d, kind=kind).ap()
    xT_d = dram("xT", [D, T])
    cT_d = dram("cT", [128, 16, 2])
    adaw_d = dram("ada_w", [DEPTH, D, 3 * D])
    adab_d = dram("ada_bT", [DEPTH, 128, 48])
    win_d = dram("w_in", [DEPTH, D, NIN])
    wout_d = dram("w_out", [DEPTH, D, D])
    par_d = dram("params", [DEPTH, 128, NPAR])
    cst_d = dram("consts", [128, 128 * 4])
    rope_d = dram("rope", [128, 2 * TL])
    icnt_d = dram("icnt", [128, 4 * XP])
    poolw_d = dram("pool_w", [DEPTH, 4, 128, 128])
    rwc_d = dram("rwc", [128, 3328])
    rwwup_d = dram("rw_w_up", [DEPTH, 2, 64, 512])
    rwaup_d = dram("rw_a_up", [DEPTH, 2, 64, 512])
    outT_d = dram("outT", [D, TL], kind="ExternalOutput")
    mixT_d = dram("mixT_scr", [D, T], BF16, kind="Internal")
    PT_d = dram("PT_scr", [NIN, T], kind="Internal")
    x1T_d = dram("x1T_scr", [D, T], kind="Internal")
    dbg_d = {}
    if dbg:
        for n, s in dbg.items():
            if not n.startswith("_"):
                dbg_d[n] = dram(n, s, kind="ExternalOutput")

    with contextlib.ExitStack() as es:
        k = K(nc, es)
        sb = lambda n, s, d=F32: V(es.enter_context(nc.sbuf_tensor(n, list(s), d))[:])
        xT = V(xT_d); cT = V(cT_d); adaw = V(adaw_d); adab = V(adab_d); win = V(win_d); wout = V(wout_d)
        par = V(par_d); cst = V(cst_d); outT = V(outT_d); PT = V(PT_d); x1T = V(x1T_d)
        rwc = V(rwc_d)
        rope = V(rope_d); icnt = V(icnt_d); poolw = V(poolw_d); mixT = V(mixT_d)
        PTb = [PT.sub((slice(b_ * 128, (b_ + 1) * 128), slice(None))) for b_ in range(NIN // 128)]
        mixb = [mixT.sub((slice(b_ * 128, (b_ + 1) * 128), slice(None))) for b_ in range(16)]
        cons = sb("cons", [128, 512])
        ident, ones, perm, bones = cons[:, 0:128], cons[:, 128:256], cons[:, 256:384], cons[:, 384:512]
        prm = [sb(f"prm{l}", [128, NPAR]) for l in range(DEPTH)]
        mod = [sb(f"mod{l}", [128, 48, 2]) for l in range(DEPTH)]
        s1 = [sb(f"s1_{l}", [128, 16, 2]) for l in range(DEPTH)]
        g1 = [sb(f"g1_{l}", [128, 16, 2]) for l in range(DEPTH)]
        banks = [V(es.enter_context(nc.psum_tensor(f"ps{i}", [128, 512], F32))[:]) for i in range(8)]
        for b_ in banks:
            b_.t.excl = True
        bank_i = [0]
        uid = [0]

        def uname(n):
            uid[0] += 1
            return f"{n}_u{uid[0]}"

        def bank():
            b = banks[bank_i[0] % 8]
            bank_i[0] += 1
            return b

        k.dma("sp", cons, cst)
        consb = sb("consb", [128, 512], BF16)
        k.copy("dve", consb, cons)
        identb_g, ones_bg, perm_b, bones_b = consb[:, 0:128], consb[:, 128:256], consb[:, 256:384], consb[:, 384:512]
        for l in range(DEPTH):
            k.dma("sp", prm[l], par[l])

        def ada_gen(l, pb):
            with contextlib.ExitStack() as ps:
                lsb = lambda n, s, d=F32: V(ps.enter_context(nc.sbuf_tensor(uname(n), list(s), d))[:])
                ct = lsb("ada_ct", [128, 16, 2])
                cs = lsb("ada_cs", [128, 16, 2])
                ab = lsb("ada_b", [128, 48])
                wb = [lsb(f"ada_w{i}", [128, 16, 256]) for i in range(2)]
                k.dma("sp", ct, cT)
                k.dma("sp", ab, adab[l])
                k.act(cs, ct, AF.Silu)
                pv = pb[:, 0:96].re("p (b j) -> p b j", j=2)
                k.dma("sp", wb[0], adaw[l][:, 0:256].re("(c p) n -> p c n", p=128))
                for cb in range(24):
                    w = wb[cb % 2]
                    if cb + 1 < 24:
                        k.dma("sp", wb[(cb + 1) % 2], adaw[l][:, (cb + 1) * 256:(cb + 2) * 256].re("(c p) n -> p c n", p=128))
                    for j in range(2):
                        blk = cb * 2 + j
                        for c in range(16):
                            k.mm(pv[:, blk, :], w[:, c, j * 128:(j + 1) * 128], cs[:, c, :], start=(c == 0), stop=(c == 15))
                    yield
                for j in range(2):
                    k.tt("dve", mod[l][:, :, j], pv[:, :, j], ab, ALU.add)
                    k.stt("dve", s1[l][:, :, j], mod[l][:, 16:32, j], 1.0, prm[l][:, P_PRE:P_PRE + 16], ALU.add, ALU.mult)
                    k.tt("dve", g1[l][:, :, j], mod[l][:, 32:48, j], prm[l][:, P_POST:P_POST + 16], ALU.mult)

        def phase_ada(l):
            for _ in ada_gen(l, bank()):
                pass
            k.barrier()

        def phase_hproj(l, xsrc):
            with contextlib.ExitStack() as ps:
                lsb = lambda n, s, d=F32: V(ps.enter_context(nc.sbuf_tensor(uname(n), list(s), d))[:])
                hT = lsb("hT", [128, 16, T], BF16)
                hts = [hT.sub((slice(None), slice(None), slice(t0, t0 + tn))) for (t0, tn) in TILES]
                with contextlib.ExitStack() as ps2:
                    lsb2 = lambda n, s, d=F32: V(ps2.enter_context(nc.sbuf_tensor(uname(n), list(s), d))[:])
                    xt = [lsb2(f"h_x{i}", [128, 16, 512]) for i in range(2)]
                    sq = [lsb2(f"h_sq{i}", [128, 512], BF16) for i in range(3)]
                    rs = [lsb2(f"h_rs{i}", [128, 512]) for i in range(2)]
                    tmp = [lsb2(f"h_tmp{i}", [128, 512]) for i in range(3)]
                    for ti, (t0, tn) in enumerate(TILES):
                        j = 1 if ti == 0 else 0
                        x_ = xt[ti % 2]
                        k.dma("sp", x_[:, :, 0:tn], xsrc[:, t0:t0 + tn].re("(c p) t -> p c t", p=128))
                        pb = bank()
                        for c in range(16):
                            s_ = sq[c % 3]
                            k.act(s_[:, 0:tn], x_[:, c, 0:tn], AF.Square)
                            k.mm(pb[:, 0:tn], ones_bg, s_[:, 0:tn], start=(c == 0), stop=(c == 15))
                        r_ = rs[ti % 2]
                        k.act(r_[:, 0:tn], pb[:, 0:tn], AF.Sqrt, bias=EPS, scale=1.0 / D)
                        k.recip(r_[:, 0:tn], r_[:, 0:tn])
                        for c in range(16):
                            t_ = tmp[c % 3]
                            k.stt("dve", t_[:, 0:tn], x_[:, c, 0:tn], s1[l][:, c, j:j + 1], r_[:, 0:tn], ALU.mult, ALU.mult)
                            k.act(hts[ti][:, c, :], t_[:, 0:tn], AF.Identity, bias=mod[l][:, c, j:j + 1])
                    if dbg and "hTd" in dbg and l == dbg.get("_layer", 0):
                        hf = lsb2("h_dbg", [128, T])
                        for c in range(16):
                            for ti, (t0, tn) in enumerate(TILES):
                                k.copy("dve", hf[:, t0:t0 + tn], hts[ti][:, c, :])
                            k.dma("sp", V(dbg_d["hTd"])[c * 128:(c + 1) * 128, :], hf)
                k.barrier()
                if dbg and dbg.get("_stop") == "h":
                    return
                with contextlib.ExitStack() as ps2:
                    lsb2 = lambda n, s, d=F32: V(ps2.enter_context(nc.sbuf_tensor(uname(n), list(s), d))[:])
                    wbuf = [lsb2(f"p_w{i}", [128, 16, 256], BF16) for i in range(3)]
                    stg = [lsb2(f"p_stg{i}", [128, T]) for i in range(3)]
                    for gi in range(NIN // 256):
                        w = wbuf[gi % 3]
                        k.dma("pool", w, win[l][:, gi * 256:(gi + 1) * 256].re("(c p) n -> p c n", p=128))
                        for jb in range(2):
                            blk = gi * 2 + jb
                            st = stg[blk % 3]
                            for ti, (t0, tn) in enumerate(TILES):
                                pb = bank()
                                for c in range(16):
                                    k.mm(pb[:, 0:tn], w[:, c, jb * 128:(jb + 1) * 128], hts[ti][:, c, :],
                                         start=(c == 0), stop=(c == 15), sig=(c == 15))
                                if (blk * 5 + ti) % 2 == 0:
                                    k.copy("act", st[:, t0:t0 + tn], pb[:, 0:tn])
                                else:
                                    k.copy("dve", st[:, t0:t0 + tn], pb[:, 0:tn])
                            k.dma("sp", PTb[blk], st)
            k.barrier()

        def phase_pool(l):
            with contextlib.ExitStack() as ps:
                lsb = lambda n, s, d=F32: V(ps.enter_context(nc.sbuf_tensor(uname(n), list(s), d))[:])
                ic = lsb("pl_ic", [128, 4 * XP])
                pw = lsb("pl_w", [128, 4, 128], BF16)
                xp = [lsb(f"pl_xp{i}", [128, XP]) for i in range(2)]
                A = lsb("pl_A", [128, XP]); B = lsb("pl_B", [128, XP])
                pb16 = [lsb(f"pl_p{i}", [128, XP], BF16) for i in range(2)]
                graw = [lsb(f"pl_g{i}", [128, T]) for i in range(2)]
                stg = [lsb(f"pl_s{i}", [128, T], BF16) for i in range(2)]
                k.dma("sp", ic, icnt)
                k.dma("pool", pw, poolw[l].re("g c d -> c g d"))
                for g in range(4):
                    w = POOL_W[g]; half = w // 2
                    x_ = xp[g % 2]; p16 = pb16[g % 2]; gr = graw[g % 2]; st = stg[g % 2]
                    k.memset("pool", x_, 0.0)
                    k.dma("sp", x_[:, XC0:XC0 + TC], PTb[O_PX // 128 + g][:, 0:TC])
                    k.dma("sp", x_[:, XL0:XL0 + TL], PTb[O_PX // 128 + g][:, TC:T], multi=True)
                    k.dma("sp", gr, PTb[O_PG // 128 + g])
                    src = x_; n = XP; step = 1; bufs = [A, B]; bi = 0
                    while step < w:
                        dst = bufs[bi]; bi ^= 1
                        k.tt("dve", dst[:, 0:n - step], src[:, 0:n - step], src[:, step:n], ALU.add)
                        src = dst; n -= step; step *= 2
                    lo, hi = XC0, XL0 + TL
                    dst = bufs[bi]
                    k.tt("dve", dst[:, lo:hi], src[:, lo - half:hi - half], ic[:, g * XP + lo:g * XP + hi], ALU.mult)
                    k.tt("pool", p16[:, lo:hi], dst[:, lo:hi], x_[:, lo:hi], ALU.subtract)
                    k.act(gr, gr, AF.Silu)
                    for ti, (t0, tn) in enumerate(TILES):
                        xo = XC0 + t0 if ti == 0 else XL0 + (t0 - TC)
                        pb = bank()
                        k.mm(pb[:, 0:tn], pw[:, g, :], p16[:, xo:xo + tn])
                        k.stt("dve", st[:, t0:t0 + tn], pb[:, 0:tn], prm[l][:, P_PSC + g:P_PSC + g + 1], gr[:, t0:t0 + tn],
                              ALU.mult, ALU.mult)
                    k.dma("sp", mixb[12 + g], st)
            k.barrier()

        def phase_att(l, need_ctx, bg=None):
            SC = 128.0 ** -0.5
            with contextlib.ExitStack() as ps:
                lsb = lambda n, s, d=F32: V(ps.enter_context(nc.sbuf_tensor(uname(n), list(s), d))[:])
                rp = lsb("at_rope", [128, 2 * TL])
                raw = [lsb(f"at_raw{i}", [128, T]) for i in range(2)]
                sq = [lsb(f"at_sq{i}", [128, 512], BF16) for i in range(2)]
                rs = [lsb(f"at_rs{i}", [128, 512]) for i in range(2)]
                kn = [lsb(f"at_kn{i}", [128, 512], BF16) for i in range(2)]
                t1 = [lsb(f"at_t1{i}", [128, 512]) for i in range(2)]
                t2 = [lsb(f"at_t2{i}", [128, 512]) for i in range(2)]
                kT = [lsb(f"at_kT{i}", [128, T], BF16) for i in range(2)]
                qT = [lsb(f"at_qT{i}", [128, T], BF16) for i in range(8)]
                vtm = [lsb(f"at_v{i}", [128, 18, 128], BF16) for i in range(2)]
                sg = [lsb(f"at_sg{i}", [128, T], BF16) for i in range(8)]
                pt = [lsb(f"at_pt{i}", [128, 512], BF16) for i in range(6)]
                ones_b = lsb("at_ones", [128, 128], BF16)
                rd = [lsb(f"at_rd{i}", [128, 512]) for i in range(2)]
                ot = [lsb(f"at_o{i}", [128, 512]) for i in range(2)]
                stg = [lsb(f"at_stg{i}", [128, T], BF16) for i in range(2)]
                k.dma("sp", rp, rope)
                k.copy("dve", ones_b, ones)
                cnt = [0]

                def normrope(dst, src_blk, gcol):
                    r_ = raw[cnt[0] % 2]; cnt[0] += 1
                    k.dma("sp", r_, PTb[src_blk])
                    for ti, (t0, tn) in enumerate(TILES):
                        s_ = sq[ti % 2]; rr = rs[ti % 2]; kn_ = kn[ti % 2]
                        k.act(s_[:, 0:tn], r_[:, t0:t0 + tn], AF.Square)
                        pb = bank()
                        k.mm(pb[:, 0:tn], ones_bg, s_[:, 0:tn])
                        k.act(rr[:, 0:tn], pb[:, 0:tn], AF.Ln, bias=EPS, scale=1.0 / 128)
                        k.act(rr[:, 0:tn], rr[:, 0:tn], AF.Exp, scale=-0.5)
                        if ti == 0:
                            k.stt("dve", dst[:, t0:t0 + tn], r_[:, t0:t0 + tn], prm[l][:, gcol:gcol + 1], rr[:, 0:tn],
                                  ALU.mult, ALU.mult)
                            continue
                        k.stt("dve", kn_[:, 0:tn], r_[:, t0:t0 + tn], prm[l][:, gcol:gcol + 1], rr[:, 0:tn],
                              ALU.mult, ALU.mult)
                        pb2 = bank()
                        k.mm(pb2[:, 0:tn], perm_b, kn_[:, 0:tn])
                        a_ = t1[ti % 2]; b_ = t2[ti % 2]
                        lt = t0 - TC
                        k.tt("dve", a_[:, 0:tn], kn_[:, 0:tn], rp[:, lt:lt + tn], ALU.mult)
                        k.tt("dve", b_[:, 0:tn], pb2[:, 0:tn], rp[:, TL + lt:TL + lt + tn], ALU.mult)
                        k.tt("pool", dst[:, t0:t0 + tn], a_[:, 0:tn], b_[:, 0:tn], ALU.add)

                for g in range(2):
                    normrope(kT[g], O_K // 128 + g, P_KN)
                    vr = raw[cnt[0] % 2]; cnt[0] += 1
                    k.dma("sp", vr, PTb[O_V // 128 + g])
                    for kt in range(18):
                        pb = bank()
                        k.tr(pb[:, 0:128], vr[:, kt * 128:(kt + 1) * 128], ident)
                        k.copy("act" if kt % 2 else "dve", vtm[g][:, kt, :], pb[:, 0:128])
                for h in range(8):
                    normrope(qT[h], O_Q // 128 + h, P_QN)
                    gr = raw[cnt[0] % 2]; cnt[0] += 1
                    k.dma("sp", gr, PTb[O_G // 128 + h])
                    k.act(sg[h], gr, AF.Silu)
                gcount = 0
                for h in range(8):
                    g = h // 4
                    q_ = qT[h]; sg_ = sg[h]; st = stg[h % 2]
                    groups = [(TC + 512 * i, 512, 18) for i in range(4)]
                    if need_ctx:
                        groups = [(0, TC, 2)] + groups
                    for (q0, qn, nk) in groups:
                        par = gcount % 2; gcount += 1
                        po = banks[par]; pd = banks[2 + par]
                        sb_ = banks[4:8] if bg is None else banks[4:7]
                        if bg is not None:
                            next(bg, None)
                        pend = []

                        def issue_s(kt):
                            pss = sb_[kt % len(sb_)]
                            k.mm(pss[:, 0:qn], kT[g][:, kt * 128:(kt + 1) * 128], q_[:, q0:q0 + qn])
                            p_ = pt[kt % 6]
                            k.act(p_[:, 0:qn], pss[:, 0:qn], AF.Exp, scale=SC)
                            return p_
                        look = len(sb_) - 1
                        for kt in range(min(look, nk)):
                            pend.append(issue_s(kt))
                        for kt in range(nk):
                            p_ = pend[kt]
                            k.mm(po[:, 0:qn], vtm[g][:, kt, :], p_[:, 0:qn], start=(kt == 0), stop=(kt == nk - 1))
                            k.mm(pd[:, 0:qn], ones_b, p_[:, 0:qn], start=(kt == 0), stop=(kt == nk - 1))
                            if kt + look < nk:
                                pend.append(issue_s(kt + look))
                        rd_ = rd[par]; o_ = ot[par]
                        k.recip(rd_[:, 0:qn], pd[:, 0:qn])
                        k.tt("dve", o_[:, 0:qn], po[:, 0:qn], rd_[:, 0:qn], ALU.mult)
                        k.tt("pool", st[:, q0:q0 + qn], o_[:, 0:qn], sg_[:, q0:q0 + qn], ALU.mult)
                    if need_ctx:
                        k.dma("sp", mixb[h], st)
                    else:
                        k.dma("sp", mixb[h][:, TC:T], st[:, TC:T])
                if bg is not None:
                    for _ in bg:
                        pass
            k.barrier()

        def phase_rwkv(l, need_ctx):
            NCH, C, G = T // 128, 128, 3
            DEC = -float(np.exp(-0.5))
            with contextlib.ExitStack() as ps:
                lsb = lambda n, s, d=F32: V(ps.enter_context(nc.sbuf_tensor(uname(n), list(s), d))[:])
                big = lambda n: lsb(n, [128, T])
                rc = lsb("rw_c", [128, 3328])
                MU2, ML2 = rc[:, 0:512], rc[:, 512:1024]
                LV4, I4 = rc[:, 2304:2816], rc[:, 2816:3328]
                identb = lsb("rw_idb", [128, 128], BF16)
                tdw = big("rw_tdw"); da = big("rw_da")
                wup = lsb("rw_wup", [128, 512]); aup = lsb("rw_aup", [128, 512])
                c0s = lsb("rw_c0", [128, 12]); omka = lsb("rw_omka", [128, 4])
                KR = [lsb(f"rw_KR{d}", [128, 2, T], BF16) for d in range(2)]
                KMb = [lsb(f"rw_KMb{d}", [128, T], BF16) for d in range(2)]
                Abb = [lsb(f"rw_Abb{d}", [128, T], BF16) for d in range(2)]
                gt = [lsb(f"rw_gt{d}", [128, NCH]) for d in range(2)]
                vb = lsb("rw_vb", [128, T], BF16)
                v = big("rw_v"); Oacc = big("rw_O"); bsum = big("rw_bs")
                stg = lsb("rw_stg", [128, T], BF16)

                k.dma("sp", rc, rwc)
                k.copy("dve", identb, ident)
                k.dma("sp", wup, V(rwwup_d)[l].re("d r c -> (d r) c"))
                k.dma("sp", aup, V(rwaup_d)[l].re("d r c -> (d r) c"))
                k.dma("sp", tdw, PTb[O_LW // 128])
                k.dma("sp", da, PTb[O_LA // 128])
                k.act(tdw, tdw, AF.Tanh)
                for i in range(3):
                    mcol = P_MU + i * 8
                    k.ts("dve", c0s[:, i * 4:(i + 1) * 4], prm[l][:, mcol:mcol + 4], -1.0, ALU.mult, 1.0, ALU.add)
                    k.tt("dve", c0s[:, i * 4:(i + 1) * 4], c0s[:, i * 4:(i + 1) * 4], prm[l][:, mcol + 4:mcol + 8], ALU.subtract)
                k.ts("dve", omka, prm[l][:, P_KA:P_KA + 4], -1.0, ALU.mult, 1.0, ALU.add)
                SEGS = ((0, TC, 1), (TC, TL, TC + 2))
                orders = [list(range(NCH)), [1, 0] + list(range(NCH - 1, 1, -1))]
                M2s = [MU2, ML2]

                for blk in range(4):
                    with contextlib.ExitStack() as pp_:
                        psb = lambda n, s, d=F32: V(pp_.enter_context(nc.sbuf_tensor(uname(n), list(s), d))[:])
                        pbig = lambda n: psb(n, [128, T])
                        raw = psb("rw_raw", [128, T + 3]); smask = raw[:, 0:T]
                        r = pbig("rw_r"); kx = pbig("rw_k"); kk = pbig("rw_kk")
                        Pb = pbig("rw_P"); Pe = pbig("rw_Pe"); Ab = pbig("rw_A"); KM = pbig("rw_KM"); E = pbig("rw_E")
                        sq = [psb(f"rw_sq{i}", [128, 512], BF16) for i in range(2)]
                        rn = [psb(f"rw_rn{i}", [128, 512]) for i in range(2)]
                        k.memset("pool", raw, 0.0)
                        for i, (dst, off) in enumerate(((r, O_RR), (kx, O_RK), (v, O_RV))):
                            src = PTb[off // 128 + blk]
                            k.dma("sp", raw[:, 1:1 + TC], src[:, 0:TC])
                            k.dma("sp", raw[:, TC + 2:TC + 2 + TL], src[:, TC:T], multi=True)
                            mcol = P_MU + i * 8 + blk
                            for (a, n, ro) in SEGS:
                                k.ts("dve", dst[:, a:a + n], raw[:, ro:ro + n], c0s[:, i * 4 + blk:i * 4 + blk + 1], ALU.mult)
                                k.stt("dve", dst[:, a:a + n], raw[:, ro - 1:ro - 1 + n], prm[l][:, mcol:mcol + 1],
                                      dst[:, a:a + n], ALU.mult, ALU.add)
                                k.stt("dve", dst[:, a:a + n], raw[:, ro + 1:ro + 1 + n], prm[l][:, mcol + 4:mcol + 5],
                                      dst[:, a:a + n], ALU.mult, ALU.add)
                        k.copy("act", vb, v)
                        k.act(E, kx, AF.Copy, scale=prm[l][:, P_KK + blk:P_KK + blk + 1])
                        for ti, (t0, tn) in enumerate(TILES):
                            s_ = sq[ti % 2]; r_ = rn[ti % 2]
                            k.act(s_[:, 0:tn], E[:, t0:t0 + tn], AF.Square)
                            pb = bank()
                            k.mm(pb[:, 0:tn], bones_b, s_[:, 0:tn])
                            k.ts("dve", r_[:, 0:tn], pb[:, 0:tn], 1e-24, ALU.max)
                            k.act(r_[:, 0:tn], r_[:, 0:tn], AF.Ln)
                            k.act(r_[:, 0:tn], r_[:, 0:tn], AF.Exp, scale=-0.5)
                            k.tt("dve", kk[:, t0:t0 + tn], E[:, t0:t0 + tn], r_[:, 0:tn], ALU.mult)
                        k.memset("pool", smask, 1.0)
                        k.memset("pool", smask.re("p (c t) -> p c t", t=C)[:, :, 0:1], 0.0)
                        for d in range(2):
                            ds = slice(d * 64, (d + 1) * 64)
                            bc = slice(blk * 128, (blk + 1) * 128)
                            for ti, (t0, tn) in enumerate(TILES):
                                pb = bank()
                                k.mm(pb[:, 0:tn], wup[ds, bc], tdw[ds, t0:t0 + tn])
                                k.act(Pe[:, t0:t0 + tn], pb[:, 0:tn], AF.Sigmoid, bias=prm[l][:, P_W0 + d * 4 + blk:P_W0 + d * 4 + blk + 1])
                                pb2 = bank()
                                k.mm(pb2[:, 0:tn], aup[ds, bc], da[ds, t0:t0 + tn])
                                k.act(Ab[:, t0:t0 + tn], pb2[:, 0:tn], AF.Sigmoid, bias=prm[l][:, P_A0 + d * 4 + blk:P_A0 + d * 4 + blk + 1])
                            k.act(Pe, Pe, AF.Copy, scale=DEC)
                            k.scan(Pb, smask, Pe, 0.0, ALU.mult, ALU.add)
                            k.tt("dve", Pe, Pb, Pe, ALU.subtract)
                            Ptot = Pb.re("p (c t) -> p c t", t=C)[:, :, C - 1]
                            k.act(gt[d], Ptot, AF.Exp)
                            k.ts("dve", E, Ab, prm[l][:, P_KA + blk:P_KA + blk + 1], ALU.mult, omka[:, blk:blk + 1], ALU.add)
                            k.tt("dve", KM, E, kx, ALU.mult)
                            k.tt("pool", Ab, Ab, kk, ALU.mult)
                            k.stt("dve", E, r, prm[l][:, P_RK + blk:P_RK + blk + 1], KM, ALU.mult, ALU.mult)
                            for ti, (t0, tn) in enumerate(TILES):
                                pb = bank()
                                k.mm(pb[:, 0:tn], bones, E[:, t0:t0 + tn])
                                if d == 0:
                                    k.act(bsum[:, t0:t0 + tn], pb[:, 0:tn], AF.Copy, scale=0.5)
                                else:
                                    k.stt("dve", bsum[:, t0:t0 + tn], pb[:, 0:tn], 0.5, bsum[:, t0:t0 + tn], ALU.mult, ALU.add)
                            if d == 0:
                                Einc, Eexc, Tmp = Pb, Pe, E
                            else:
                                for c in range(NCH):
                                    cs_ = slice(c * C, (c + 1) * C)
                                    k.act(E[:, cs_], Pe[:, cs_], AF.Identity, bias=Ptot[:, c:c + 1], scale=-1.0)
                                for c in range(NCH):
                                    cs_ = slice(c * C, (c + 1) * C)
                                    k.act(Pe[:, cs_], Pb[:, cs_], AF.Identity, bias=Ptot[:, c:c + 1], scale=-1.0)
                                Einc, Eexc, Tmp = E, Pe, Pb
                            k.act(Tmp, Einc, AF.Exp)
                            k.tt("dve", KR[d][:, 1, :], r, Tmp, ALU.mult)
                            k.act(Tmp, Eexc, AF.Exp)
                            k.tt("dve", KR[d][:, 0, :], kk, Tmp, ALU.mult)
                            k.act(Tmp, Einc, AF.Exp, scale=-1.0)
                            k.tt("dve", KMb[d], KM, Tmp, ALU.mult)
                            k.tt("pool", Abb[d], Ab, Tmp, ALU.mult)
                    k.barrier()
                    with contextlib.ExitStack() as sp_:
                        ssb = lambda n, s, d=F32: V(sp_.enter_context(nc.sbuf_tensor(uname(n), list(s), d))[:])
                        NB = 2 * G
                        akbs = [ssb(f"rw_akb{i}", [128, 4, 512], BF16) for i in range(NB)]
                        ptms = [ssb(f"rw_ptm{i}", [128, 512], BF16) for i in range(NB)]
                        tms = [ssb(f"rw_tm{i}", [128, 768], BF16) for i in range(NB)]
                        invs = [ssb(f"rw_inv{i}", [128, 512], BF16) for i in range(G)]
                        ntms = [ssb(f"rw_ntm{i}", [128, 512], BF16) for i in range(G)]
                        t1s = [ssb(f"rw_t1{i}", [128, 512], BF16) for i in range(G)]
                        Xs = [ssb(f"rw_X{i}", [128, 256], BF16) for i in range(2)]
                        nUs = [ssb(f"rw_nU{i}", [128, 256], BF16) for i in range(2)]
                        Hs = [[ssb(f"rw_H{d}{i}", [128, 64]) for i in range(2)] for d in range(2)]
                        Hbs = [[ssb(f"rw_Hb{d}{i}", [128, 64], BF16) for i in range(2)] for d in range(2)]
                        hgs = [[ssb(f"rw_hg{d}{i}", [128, 64]) for i in range(2)] for d in range(2)]
                        sgate = ssb("rw_sg", [128, T])
                        ft = [ssb(f"rw_ft{i}", [128, 512]) for i in range(4)]
                        p3 = lambda t_: t_.re("p (a m) -> p a m", a=4)
                        k.memset("pool", Oacc, 0.0)
                        for d in range(2):
                            k.memset("dve", Hs[d][0], 0.0)
                            k.memset("dve", Hbs[d][0], 0.0)

                        def stage1(steps):
                            for st in steps:
                                sl = st % NB
                                cs = [slice(orders[d][st] * C, (orders[d][st] + 1) * C) for d in range(2)]
                                pA = bank(); pB = bank()
                                k.mm(pA[:, 0:128], vb[:, cs[0]], identb)
                                k.mm(pA[:, 128:256], KMb[0][:, cs[0]], identb)
                                k.mm(pA[:, 256:384], Abb[0][:, cs[0]], identb)
                                k.mm(pA[:, 384:512], vb[:, cs[1]], identb)
                                k.mm(pB[:, 0:128], KMb[1][:, cs[1]], identb)
                                k.mm(pB[:, 128:256], Abb[1][:, cs[1]], identb)
                                k.copy("act", tms[sl][:, 0:512], pA)
                                k.copy("act", tms[sl][:, 512:768], pB[:, 0:256])
                            yield
                            for st in steps:
                                sl = st % NB
                                cs = [slice(orders[d][st] * C, (orders[d][st] + 1) * C) for d in range(2)]
                                for d in range(2):
                                    for hb in range(2):
                                        hs = slice(hb * 64, (hb + 1) * 64)
                                        pa = bank()
                                        k.mm(pa[:, 0:256], KMb[d][hs, cs[d]], KR[d][hs, :, cs[d]])
                                        k.mm(pa[:, 256:512], Abb[d][hs, cs[d]], KR[d][hs, :, cs[d]])
                                        k.tt("dve", akbs[sl][:, d * 2 + hb, :], pa, M2s[d], ALU.mult)
                            yield
                            for st in steps:
                                sl = st % NB
                                NT4 = akbs[sl][:, :, 256:384]
                                nm4 = ntms[st % G]
                                k.stt("dve", p3(nm4), p3(LV4), 0.0, NT4, ALU.is_equal, ALU.mult)
                                k.tt("pool", ptms[sl], I4, nm4, ALU.subtract)
                            yield
                            for lv in range(1, 7):
                                for st in steps:
                                    sl = st % NB
                                    pq = bank()
                                    for ln in range(4):
                                        k.mm(pq[:, ln * 128:(ln + 1) * 128], ptms[sl][:, ln * 128:(ln + 1) * 128], identb)
                                    k.copy("act", invs[st % G], pq)
                                    k.stt("dve", p3(ntms[st % G]), p3(LV4), float(lv), akbs[sl][:, :, 256:384], ALU.is_equal, ALU.mult)
                                yield
                                for st in steps:
                                    pt1 = bank()
                                    for ln in range(4):
                                        k.mm(pt1[:, ln * 128:(ln + 1) * 128], ntms[st % G][:, ln * 128:(ln + 1) * 128],
                                             invs[st % G][:, ln * 128:(ln + 1) * 128])
                                    k.copy("act", t1s[st % G], pt1)
                                yield
                                for st in steps:
                                    sl = st % NB
                                    pp = bank()
                                    for ln in range(4):
                                        k.mm(pp[:, ln * 128:(ln + 1) * 128], t1s[st % G][:, ln * 128:(ln + 1) * 128],
                                             ptms[sl][:, ln * 128:(ln + 1) * 128])
                                    k.tt("dve", ptms[sl], ptms[sl], pp, ALU.subtract)
                                yield

                        def stage2(steps):
                            for st in steps:
                                sl = st % NB
                                akb = akbs[sl]; ptm = ptms[sl]; tm_ = tms[sl]
                                X = Xs[st % 2]; nU = nUs[st % 2]
                                cch = [orders[d][st] for d in range(2)]
                                cs = [slice(cch[d] * C, (cch[d] + 1) * C) for d in range(2)]
                                vtm = [tm_[:, 0:128], tm_[:, 384:512]]
                                kbtm = [tm_[:, 128:256], tm_[:, 512:640]]
                                bbtm = [tm_[:, 256:384], tm_[:, 640:768]]
                                Hc = [Hs[d][st % 2] for d in range(2)]; Hn = [Hs[d][(st + 1) % 2] for d in range(2)]
                                Hb = [Hbs[d][st % 2] for d in range(2)]; Hbn = [Hbs[d][(st + 1) % 2] for d in range(2)]
                                hg = [hgs[d][st % 2] for d in range(2)]
                                for d in range(2):
                                    k.act(hg[d], Hc[d], AF.Copy, scale=gt[d][:, cch[d]:cch[d] + 1])
                                px = bank()
                                for d in range(2):
                                    for hb in range(2):
                                        ln = d * 2 + hb
                                        hs = slice(hb * 64, (hb + 1) * 64); vs = slice(hb * 64, (hb + 1) * 64)
                                        k.mm(px[:, ln * 64:(ln + 1) * 64], KR[d][hs, 0, cs[d]], Hb[d][hs, :], start=True, stop=False)
                                        k.mm(px[:, ln * 64:(ln + 1) * 64], akb[:, ln, 0:128], vtm[d][:, vs], start=False, stop=True)
                                k.copy("act", X, px[:, 0:256])
                                yield
                                pu = bank()
                                for ln in range(4):
                                    k.mm(pu[:, ln * 64:(ln + 1) * 64], ptm[:, ln * 128:(ln + 1) * 128], X[:, ln * 64:(ln + 1) * 64])
                                k.act(nU, pu[:, 0:256], AF.Copy, scale=-1.0)
                                yield
                                ph = bank()
                                for d in range(2):
                                    for hb in range(2):
                                        ln = d * 2 + hb
                                        hs = slice(hb * 64, (hb + 1) * 64); vs = slice(hb * 64, (hb + 1) * 64)
                                        k.mm(ph[hs, d * 64:(d + 1) * 64], kbtm[d][:, hs], vtm[d][:, vs], start=True, stop=False)
                                        k.mm(ph[hs, d * 64:(d + 1) * 64], bbtm[d][:, hs], nU[:, ln * 64:(ln + 1) * 64], start=False, stop=True)
                                for d in range(2):
                                    k.stt("dve", Hn[d], ph[:, d * 64:(d + 1) * 64], gt[d][:, cch[d]:cch[d] + 1], hg[d], ALU.mult, ALU.add)
                                    k.copy("pool", Hbn[d], Hn[d])
                                po = bank()
                                for d in range(2):
                                    if not (need_ctx or cch[d] >= 2):
                                        continue
                                    for hb in range(2):
                                        ln = d * 2 + hb
                                        hs = slice(hb * 64, (hb + 1) * 64); vs = slice(hb * 64, (hb + 1) * 64)
                                        k.mm(po[hs, d * 128:(d + 1) * 128], Hb[d][hs, :], KR[d][hs, 1, cs[d]], start=True, stop=False)
                                        k.mm(po[hs, d * 128:(d + 1) * 128], vtm[d][:, vs], akb[:, ln, 128:256], start=False, stop=False)
                                        k.mm(po[hs, d * 128:(d + 1) * 128], nU[:, ln * 64:(ln + 1) * 64], akb[:, ln, 384:512], start=False, stop=True)
                                    k.tt("dve", Oacc[:, cs[d]], Oacc[:, cs[d]], po[:, d * 128:(d + 1) * 128], ALU.add)
                                yield

                        groups = [list(range(g0, g0 + G)) for g0 in range(0, NCH, G)]
                        for _ in stage1(groups[0]):
                            pass
                        for gi in range(len(groups)):
                            g1 = stage1(groups[gi + 1]) if gi + 1 < len(groups) else iter(())
                            g2 = stage2(groups[gi])
                            a1 = a2 = True
                            while a1 or a2:
                                if a1:
                                    a1 = next(g1, "end") != "end"
                                if a1:
                                    a1 = next(g1, "end") != "end"
                                if a2:
                                    a2 = next(g2, "end") != "end"
                        k.dma("sp", sgate, PTb[O_RG // 128 + blk])
                        k.act(sgate, sgate, AF.Silu)
                        for ti, (t0, tn) in enumerate(TILES):
                            if ti == 0 and not need_ctx:
                                continue
                            ts_ = slice(t0, t0 + tn)
                            oc = ft[0]; s_ = ft[1]; r_ = ft[2]; bb_ = ft[3]
                            pb = bank()
                            k.mm(pb[:, 0:tn], bones, Oacc[:, ts_])
                            k.stt("dve", oc[:, 0:tn], pb[:, 0:tn], -1.0 / 64, Oacc[:, ts_], ALU.mult, ALU.add)
                            k.act(s_[:, 0:tn], oc[:, 0:tn], AF.Square)
                            pb2 = bank()
                            k.mm(pb2[:, 0:tn], bones, s_[:, 0:tn])
                            k.act(r_[:, 0:tn], pb2[:, 0:tn], AF.Ln, bias=64e-5, scale=1.0 / 64)
                            k.act(r_[:, 0:tn], r_[:, 0:tn], AF.Exp, scale=-0.5)
                            k.tt("dve", oc[:, 0:tn], oc[:, 0:tn], r_[:, 0:tn], ALU.mult)
                            k.ts("dve", oc[:, 0:tn], oc[:, 0:tn], prm[l][:, P_LNW + blk:P_LNW + blk + 1], ALU.mult,
                                 prm[l][:, P_LNB + blk:P_LNB + blk + 1], ALU.add)
                            k.tt("pool", bb_[:, 0:tn], bsum[:, ts_], v[:, ts_], ALU.mult)
                            k.tt("pool", oc[:, 0:tn], oc[:, 0:tn], bb_[:, 0:tn], ALU.add)
                            k.tt("pool", stg[:, ts_], oc[:, 0:tn], sgate[:, ts_], ALU.mult)
                        if need_ctx:
                            k.dma("sp", mixb[8 + blk], stg)
                        else:
                            k.dma("sp", mixb[8 + blk][:, TC:T], stg[:, TC:T])
                    k.barrier()
            k.barrier()

        def phase_out(l, xsrc, xdst, need_ctx):
            TN = 256
            tiles = [(t0, TN) for t0 in range(0 if need_ctx else TC, T, TN)]
            with contextlib.ExitStack() as ps:
                lsb = lambda n, s, d=F32: V(ps.enter_context(nc.sbuf_tensor(uname(n), list(s), d))[:])
                wo = lsb("o_w", [128, 16, D], BF16)
                wos = [wo.sub((slice(None), slice(None), slice(i * 512, (i + 1) * 512))) for i in range(4)]
                mx = [lsb(f"o_mx{i}", [128, 16, TN], BF16) for i in range(3)]
                xt = [lsb(f"o_x{i}", [128, 16, TN]) for i in range(3)]
                y2 = [lsb(f"o_y{i}", [128, 16, TN]) for i in range(2)]
                ys2 = [[y_.sub((slice(None), i, slice(None))) for i in range(16)] for y_ in y2]
                sq = [lsb(f"o_sq{i}", [128, TN], BF16) for i in range(4)]
                rs2 = [lsb(f"o_rs{i}", [128, TN]) for i in range(2)]
                for i in range(4):
                    k.dma("pool", wos[i], wout[l][:, i * 512:(i + 1) * 512].re("(c p) n -> p c n", p=128))

                def load(it):
                    t0, tn = tiles[it]
                    k.dma("sp", mx[it % 3], mixT[:, t0:t0 + tn].re("(c p) t -> p c t", p=128))
                    k.dma("sp", xt[it % 3], xsrc[:, t0:t0 + tn].re("(c p) t -> p c t", p=128))

                load(0)
                if len(tiles) > 1:
                    load(1)
                for it, (t0, tn) in enumerate(tiles):
                    j = 1 if t0 < TC else 0
                    m_ = mx[it % 3]; x_ = xt[it % 3]; ys = ys2[it % 2]; rs = rs2[it % 2]
                    pss = banks[0]
                    for db in range(16):
                        pb = banks[1 + db % 7]
                        for c in range(16):
                            k.mm(pb[:, 0:tn], wos[db // 4][:, c, (db % 4) * 128:(db % 4 + 1) * 128], m_[:, c, :],
                                 start=(c == 0), stop=(c == 15), sig=(c == 15))
                        k.copy("dve" if db % 2 else "act", ys[db], pb[:, 0:tn])
                        k.act(sq[db % 4], ys[db], AF.Square)
                        if db >= 2:
                            k.mm(pss[:, 0:tn], ones_bg, sq[(db - 2) % 4], start=(db == 2), stop=False)
                    for db in (14, 15):
                        k.mm(pss[:, 0:tn], ones_bg, sq[db % 4], start=False, stop=(db == 15))
                    if it + 2 < len(tiles):
                        load(it + 2)
                    k.act(rs, pss[:, 0:tn], AF.Ln, bias=EPS, scale=1.0 / D)
                    k.act(rs, rs, AF.Exp, scale=-0.5)
                    for db in range(16):
                        k.stt("dve", ys[db], ys[db], g1[l][:, db, j:j + 1], rs, ALU.mult, ALU.mult)
                        k.tt("pool" if db % 3 == 2 else "dve", x_[:, db, :], x_[:, db, :], ys[db], ALU.add)
                    if xdst is outT:
                        k.dma("sp", outT[:, t0 - TC:t0 - TC + tn].re("(c p) t -> p c t", p=128), x_)
                    else:
                        k.dma("sp", xdst[:, t0:t0 + tn].re("(c p) t -> p c t", p=128), x_)
            k.barrier()

        stop = dbg.get("_stop") if dbg else None
        only = dbg.get("_only") if dbg else None
        for l in range(nlayers):
            phase_ada(l)
        xsrc = xT
        for l in range(nlayers):
            last = (l == DEPTH - 1)
            if stop == "ada":
                break
            phase_hproj(l, xsrc)
            if dbg and "PT" in dbg and l == dbg.get("_layer", 0):
                with contextlib.ExitStack() as ps:
                    bnc = V(ps.enter_context(nc.sbuf_tensor(uname("dbg_b"), [128, T], F32))[:])
                    for blk in range(NIN // 128):
                        k.dma("sp", bnc, PTb[blk])
                        k.dma("sp", V(dbg_d["PT"])[blk * 128:(blk + 1) * 128, :], bnc)
                k.barrier()
            if stop in ("h", "proj"):
                break
            if dbg and dbg.get("_zero_mix"):
                with contextlib.ExitStack() as ps:
                    zz = V(ps.enter_context(nc.sbuf_tensor(uname("dbg_z"), [128, T], BF16))[:])
                    k.memset("dve", zz, 0.0)
                    for b_ in range(16):
                        k.dma("sp", mixb[b_], zz)
                k.barrier()
            if only is None or "pool" in only:
                phase_pool(l)
            if only is None or "att" in only:
                phase_att(l, need_ctx=not last)
            if only is None or "rwkv" in only:
                if phase_rwkv(l, need_ctx=not last):
                    break
            if dbg and "mix" in dbg and l == dbg.get("_layer", 0):
                with contextlib.ExitStack() as ps:
                    b16 = V(ps.enter_context(nc.sbuf_tensor(uname("dbg_m16"), [128, T], BF16))[:])
                    b32 = V(ps.enter_context(nc.sbuf_tensor(uname("dbg_m32"), [128, T], F32))[:])
                    for b_ in range(16):
                        k.dma("sp", b16, mixb[b_])
                        k.copy("dve", b32, b16)
                        k.dma("sp", V(dbg_d["mix"])[b_ * 128:(b_ + 1) * 128, :], b32)
                k.barrier()
            if stop == "mix":
                break
            phase_out(l, xsrc, outT if last else x1T, need_ctx=not last)
            if dbg and "x1" in dbg and l == 0:
                with contextlib.ExitStack() as ps:
                    bnc = V(ps.enter_context(nc.sbuf_tensor(uname("dbg_x"), [128, T], F32))[:])
                    for b_ in range(16):
                        k.dma("sp", bnc, x1T[b_ * 128:(b_ + 1) * 128, :])
                        k.dma("sp", V(dbg_d["x1"])[b_ * 128:(b_ + 1) * 128, :], bnc)
                k.barrier()
            xsrc = x1T
        if dbg and "mod" in dbg:
            for l in range(nlayers):
                k.dma("sp", V(dbg_d["mod"])[l], mod[l])
        k.barrier()
        k.final_wait()
    return nc


def make_consts():
    c = np.zeros((128, 512), np.float32)
    c[:, 0:128] = np.eye(128, dtype=np.float32)
    c[:, 128:256] = 1.0
    for i in range(128):
        j = i % 64
        pi = i + 32 if j < 32 else i - 32
        c[pi, 256 + i] = 1.0
    c[0:64, 384:448] = 1.0
    c[64:128, 448:512] = 1.0
    return c


def make_rope():
    half = 64
    nfreq = 32
    inv = (np.float32(10000.0) ** (-(np.arange(nfreq, dtype=np.float32) * np.float32(2.0) / np.float32(half)))).astype(np.float32)
    t = np.arange(TL)
    row = (t // 64).astype(np.float32)
    col = (t % 64).astype(np.float32)
    out = np.zeros((128, 2 * TL), np.float32)
    for i in range(128):
        axis, hf, fr = i // 64, (i % 64) // 32, i % 32
        ang = ((row if axis == 0 else col) * inv[fr]).astype(np.float32)
        out[i, :TL] = np.cos(ang)
        out[i, TL:] = np.sin(ang) * (-1.0 if hf == 0 else 1.0)
    return out


def make_icnt():
    tab = np.zeros((4, XP), np.float32)
    for g, w in enumerate(POOL_W):
        h = w // 2
        for (x0, n) in ((XC0, TC), (XL0, TL)):
            t = np.arange(n)
            lo = np.clip(t - h, 0, n - 1)
            hi = np.clip(t + h - 1, 0, n - 1)
            tab[g, x0:x0 + n] = 1.0 / (hi - lo + 1).astype(np.float32)
    return np.ascontiguousarray(np.broadcast_to(tab.reshape(1, -1), (128, 4 * XP)))


def make_rwc():
    c = np.zeros((128, 3328), np.float32)
    ii = np.arange(128)
    x = ii[:, None] ^ ii[None, :]
    lvl = np.full((128, 128), -1.0, np.float32)
    nz = x > 0
    lvl[nz] = np.floor(np.log2(x[nz])).astype(np.float32)
    lvL = np.where(ii[:, None] > ii[None, :], lvl, -1.0).astype(np.float32)
    c[:, 1792:2048] = np.concatenate([lvL, lvL], axis=1)
    c[:, 2048:2304] = np.concatenate([lvL.T, lvL.T], axis=1)
    c[:, 2304:2816] = np.concatenate([lvL.T, lvL.T, lvL, lvL], axis=1)
    c[:, 2816:3328] = np.concatenate([np.eye(128, dtype=np.float32)] * 4, axis=1)
    tS = np.triu(np.ones((128, 128), np.float32), 1)
    tI = np.triu(np.ones((128, 128), np.float32), 0)
    c[:, 0:512] = np.concatenate([tS, tI, tS, tI], axis=1)
    c[:, 512:1024] = np.concatenate([tS.T, tI.T, tS.T, tI.T], axis=1)
    c[:, 1024:1280] = np.concatenate([np.eye(128, dtype=np.float32)] * 2, axis=1)
    c[:, 1280:1536] = np.concatenate([tS.T, tS.T], axis=1)
    c[:, 1536:1792] = np.concatenate([tS, tS], axis=1)
    return c


def chunkT(v):
    return np.ascontiguousarray(v.reshape(-1, 128).T)


def make_inputs(b, inp):
    f = np.float32
    xT = np.ascontiguousarray(np.concatenate([inp["ctx"][b], inp["x"][b]], axis=0).T.astype(f))
    cT = np.stack([chunkT(inp["c"][b]), chunkT(inp["c_ctx"])], axis=-1).astype(f)
    params = np.zeros((DEPTH, 128, NPAR), f)
    for l in range(DEPTH):
        P = params[l]
        P[:, P_PRE:P_PRE + 16] = chunkT(inp["pre_norm"][l])
        P[:, P_POST:P_POST + 16] = chunkT(inp["post_norm"][l])
        P[:, P_QN] = inp["q_norm"][l]
        P[:, P_KN] = inp["k_norm"][l]
        P[:, P_MU:P_MU + 24] = chunkT(inp["rw_mu"][l].reshape(-1))
        P[:, P_W0:P_W0 + 8] = chunkT(inp["rw_w0"][l].reshape(-1))
        P[:, P_A0:P_A0 + 8] = chunkT(inp["rw_a0"][l].reshape(-1))
        P[:, P_KK:P_KK + 4] = chunkT(inp["rw_k_k"][l])
        P[:, P_KA:P_KA + 4] = chunkT(inp["rw_k_a"][l])
        P[:, P_RK:P_RK + 4] = chunkT(inp["rw_r_k"][l].reshape(-1))
        P[:, P_LNW:P_LNW + 4] = chunkT(inp["rw_ln_w"][l])
        P[:, P_LNB:P_LNB + 4] = chunkT(inp["rw_ln_b"][l])
        P[:, P_PSC:P_PSC + 4] = chunkT(inp["pool_scale"][l])
    adabT = np.stack([chunkT(inp["ada_b"][l]) for l in range(DEPTH)]).astype(f)
    return {
        "xT": xT, "cT": np.ascontiguousarray(cT), "ada_w": inp["ada_w"], "ada_bT": adabT,
        "w_in": inp["w_in"], "w_out": inp["w_out"], "params": params, "consts": make_consts(),
        "rope": make_rope(), "icnt": make_icnt(), "pool_w": inp["pool_w"],
        "rwc": make_rwc(), "rw_w_up": inp["rw_w_up"], "rw_a_up": inp["rw_a_up"],
    }


def kernel(**inputs):
    inp = {k_: np.asarray(v) for k_, v in inputs.items()}
    nc = build_program()
    in_maps = [make_inputs(c % 4, inp) for c in range(8)]
    res = run_bass_kernel_spmd(nc, in_maps, core_ids=list(range(8)))
    out = np.stack([np.ascontiguousarray(res.results[b]["outT"].T) for b in range(4)], axis=0)
    return out.astype(np.float32)
```

```python
import contextlib
import numpy as np
import concourse.bass as bass
import concourse.mybir as mybir
from concourse.bass_utils import run_bass_kernel_spmd

F32 = mybir.dt.float32
BF16 = mybir.dt.bfloat16
AF = mybir.ActivationFunctionType
ALU = mybir.AluOpType

D = 2048
TC = 256
TL = 2048
T = TC + TL
NIN = 5888
DEPTH = 2
EPS = 1e-6
TILES = [(0, 256), (256, 512), (768, 512), (1280, 512), (1792, 512)]
O_Q, O_K, O_V, O_G = 0, 1024, 1280, 1536
O_RR, O_RK, O_RV, O_RG = 2560, 3072, 3584, 4096
O_LW, O_LA = 4608, 4736
O_PX, O_PG = 4864, 5376
P_PRE, P_POST, P_QN, P_KN, P_MU, P_W0, P_A0, P_KK, P_KA, P_RK, P_LNW, P_LNB, P_PSC = (
    0, 16, 32, 33, 34, 58, 66, 74, 78, 82, 86, 90, 94)
NPAR = 98
SEM_CAP = 3000
XP = 8 + TC + 16 + TL + 8
XC0, XL0 = 8, 8 + TC + 16
POOL_W = (2, 4, 8, 16)


class StopPhase(Exception):
    pass


class Trk:
    __slots__ = ("w", "r", "excl")

    def __init__(self):
        self.w = {}
        self.r = {}
        self.excl = False


class V:
    def __init__(self, ap, trk=None):
        self.ap = ap
        self.t = trk if trk is not None else Trk()

    def __getitem__(self, idx):
        return V(self.ap[idx], self.t)

    def re(self, pat, **kw):
        return V(self.ap.rearrange(pat, **kw), self.t)

    def sub(self, idx):
        return V(self.ap[idx], Trk())


class K:
    def __init__(self, nc, es):
        self.nc = nc
        self.es = es
        self.eng = {"pe": nc.tensor, "dve": nc.vector, "act": nc.scalar, "pool": nc.gpsimd, "sp": nc.sync}
        self.cnt = {e: 0 for e in self.eng}
        self.sems = {e: [] for e in self.eng}
        self.seen = {e: {} for e in self.eng}
        self.ndma = 24
        self.dsem = [es.enter_context(nc.semaphore(f"dma{i}")) for i in range(self.ndma)]
        self.dval = [0] * self.ndma
        self.dnext = 0
        self.dpools = {"sp": list(range(0, 16)), "act": list(range(0, 16)), "pool": list(range(16, 24))}
        self.dpos = {"sp": 0, "act": 0, "pool": 0}
        self.same_sync = True

    def _sem(self, e, c):
        ep = (c - 1) // SEM_CAP
        while len(self.sems[e]) <= ep:
            self.sems[e].append(self.es.enter_context(self.nc.semaphore(f"s_{e}_{len(self.sems[e])}")))
        return self.sems[e][ep], c - ep * SEM_CAP

    def _wait(self, E, key, c):
        if c <= self.seen[E].get(key, 0):
            return
        self.seen[E][key] = c
        if isinstance(key, tuple):
            self.eng[E].wait_ge(self.dsem[key[1]], c)
        else:
            sem, val = self._sem(key, c)
            self.eng[E].wait_ge(sem, val)

    def _deps(self, E, outs, ins, skip_dma_waw=False):
        need = {}
        for v in ins:
            for k, c in v.t.w.items():
                need[k] = max(need.get(k, 0), c)
            if v.t.excl:
                for k, c in v.t.r.items():
                    if k != E:
                        need[k] = max(need.get(k, 0), c)
        for v in outs:
            for k, c in v.t.w.items():
                if skip_dma_waw and isinstance(k, tuple):
                    continue
                need[k] = max(need.get(k, 0), c)
            for k, c in v.t.r.items():
                need[k] = max(need.get(k, 0), c)
        for k, c in need.items():
            if k == E and (E == "pe" or not self.same_sync):
                continue
            self._wait(E, k, c)

    def op(self, E, fn, outs, ins, sig=True):
        self._deps(E, outs, ins)
        inst = fn()
        c = self.cnt[E] + 1
        if sig:
            self.cnt[E] = c
            sem, _ = self._sem(E, c)
            inst.then_inc(sem, 1)
        for v in ins:
            v.t.r[E] = c
        for v in outs:
            v.t.w = {E: c}
            v.t.r = {}
        return inst

    def dma(self, Q, out, in_, multi=False):
        self._deps(Q, [out], [in_], skip_dma_waw=multi)
        pl = self.dpools[Q]
        i = pl[self.dpos[Q] % len(pl)]
        self.dpos[Q] += 1
        self._wait(Q, ("dma", i), self.dval[i])
        inst = self.eng[Q].dma_start(out=out.ap, in_=in_.ap)
        self.dval[i] += 16
        inst.then_inc(self.dsem[i], 16)
        key = ("dma", i)
        in_.t.r[key] = self.dval[i]
        if multi:
            out.t.w = {k: c for k, c in out.t.w.items() if isinstance(k, tuple)}
            out.t.w[key] = self.dval[i]
        else:
            out.t.w = {key: self.dval[i]}
        out.t.r = {}

    def barrier(self):
        for E in self.eng:
            for e2 in self.eng:
                if e2 != E and self.cnt[e2] > 0:
                    self._wait(E, e2, self.cnt[e2])
            for i in range(self.ndma):
                if self.dval[i] > 0:
                    self._wait(E, ("dma", i), self.dval[i])

    def final_wait(self):
        for i in range(self.ndma):
            if self.dval[i] > 0:
                self._wait("sp", ("dma", i), self.dval[i])

    def mm(self, out, lhsT, rhs, start=True, stop=True, sig=True):
        return self.op("pe", lambda: self.nc.tensor.matmul(out.ap, lhsT.ap, rhs.ap, start=start, stop=stop),
                       [out], [lhsT, rhs], sig=sig)

    def tr(self, out, in_, ident):
        return self.op("pe", lambda: self.nc.tensor.transpose(out.ap, in_.ap, ident.ap), [out], [in_, ident])

    def act(self, out, in_, func, bias=None, scale=None):
        ins = [in_]
        kw = {}
        if bias is not None:
            if isinstance(bias, V):
                ins.append(bias)
                kw["bias"] = bias.ap
            else:
                kw["bias"] = float(bias)
        if scale is not None:
            if isinstance(scale, V):
                ins.append(scale)
                kw["scale"] = scale.ap
            else:
                kw["scale"] = float(scale)
        return self.op("act", lambda: self.nc.scalar.activation(out.ap, in_.ap, func, **kw), [out], ins)

    def _e(self, E):
        return self.eng[E]

    def tt(self, E, out, a, b, op):
        return self.op(E, lambda: self._e(E).tensor_tensor(out.ap, a.ap, b.ap, op), [out], [a, b])

    def ts(self, E, out, a, s1, op0, s2=None, op1=None):
        ins = [a]
        s1a = s1.ap if isinstance(s1, V) else float(s1)
        if isinstance(s1, V):
            ins.append(s1)
        if s2 is None:
            return self.op(E, lambda: self._e(E).tensor_scalar(out.ap, a.ap, s1a, None, op0), [out], ins)
        s2a = s2.ap if isinstance(s2, V) else float(s2)
        if isinstance(s2, V):
            ins.append(s2)
        return self.op(E, lambda: self._e(E).tensor_scalar(out.ap, a.ap, s1a, s2a, op0, op1), [out], ins)

    def stt(self, E, out, in0, s, in1, op0, op1):
        ins = [in0, in1]
        sa = s.ap if isinstance(s, V) else float(s)
        if isinstance(s, V):
            ins.append(s)
        return self.op(E, lambda: self._e(E).scalar_tensor_tensor(out.ap, in0.ap, sa, in1.ap, op0, op1), [out], ins)

    def copy(self, E, out, in_):
        if E == "act":
            return self.op(E, lambda: self.nc.scalar.copy(out.ap, in_.ap), [out], [in_])
        return self.op(E, lambda: self._e(E).tensor_copy(out.ap, in_.ap), [out], [in_])

    def memset(self, E, out, val):
        return self.op(E, lambda: self._e(E).memset(out.ap, val), [out], [])

    def recip(self, out, in_):
        return self.op("dve", lambda: self.nc.vector.reciprocal(out.ap, in_.ap), [out], [in_])

    def scan(self, out, d0, d1, init, op0, op1):
        return self.op("dve", lambda: self.nc.vector.tensor_tensor_scan(out.ap, d0.ap, d1.ap, init, op0, op1),
                       [out], [d0, d1])


def build_program(dbg=None, nlayers=DEPTH):
    nc = bass.Bass("TRN2", target_bir_lowering=False)
    dram = lambda n, s, d=F32, kind="ExternalInput": nc.dram_tensor(n, list(s), d, kind=kind).ap()
    xT_d = dram("xT", [D, T])
    cT_d = dram("cT", [128, 16, 2])
    adaw_d = dram("ada_w", [DEPTH, D, 3 * D])
    adab_d = dram("ada_bT", [DEPTH, 128, 48])
    win_d = dram("w_in", [DEPTH, D, NIN])
    wout_d = dram("w_out", [DEPTH, D, D])
    par_d = dram("params", [DEPTH, 128, NPAR])
    cst_d = dram("consts", [128, 128 * 4])
    rope_d = dram("rope", [128, 2 * TL])
    icnt_d = dram("icnt", [128, 4 * XP])
    poolw_d = dram("pool_w", [DEPTH, 4, 128, 128])
    rwc_d = dram("rwc", [128, 3328])
    rwwup_d = dram("rw_w_up", [DEPTH, 2, 64, 512])
    rwaup_d = dram("rw_a_up", [DEPTH, 2, 64, 512])
    outT_d = dram("outT", [D, TL], kind="ExternalOutput")
    mixT_d = dram("mixT_scr", [D, T], BF16, kind="Internal")
    PT_d = dram("PT_scr", [NIN, T], kind="Internal")
    x1T_d = dram("x1T_scr", [D, T], kind="Internal")
    dbg_d = {}
    if dbg:
        for n, s in dbg.items():
            if not n.startswith("_"):
                dbg_d[n] = dram(n, s, kind="ExternalOutput")

    with contextlib.ExitStack() as es:
        k = K(nc, es)
        sb = lambda n, s, d=F32: V(es.enter_context(nc.sbuf_tensor(n, list(s), d))[:])
        xT = V(xT_d); cT = V(cT_d); adaw = V(adaw_d); adab = V(adab_d); win = V(win_d); wout = V(wout_d)
        par = V(par_d); cst = V(cst_d); outT = V(outT_d); PT = V(PT_d); x1T = V(x1T_d)
        rwc = V(rwc_d)
        rope = V(rope_d); icnt = V(icnt_d); poolw = V(poolw_d); mixT = V(mixT_d)
        PTb = [PT.sub((slice(b_ * 128, (b_ + 1) * 128), slice(None))) for b_ in range(NIN // 128)]
        mixb = [mixT.sub((slice(b_ * 128, (b_ + 1) * 128), slice(None))) for b_ in range(16)]
        cons = sb("cons", [128, 512])
        ident, ones, perm, bones = cons[:, 0:128], cons[:, 128:256], cons[:, 256:384], cons[:, 384:512]
        prm = [sb(f"prm{l}", [128, NPAR]) for l in range(DEPTH)]
        mod = [sb(f"mod{l}", [128, 48, 2]) for l in range(DEPTH)]
        s1 = [sb(f"s1_{l}", [128, 16, 2]) for l in range(DEPTH)]
        g1 = [sb(f"g1_{l}", [128, 16, 2]) for l in range(DEPTH)]
        banks = [V(es.enter_context(nc.psum_tensor(f"ps{i}", [128, 512], F32))[:]) for i in range(8)]
        for b_ in banks:
            b_.t.excl = True
        bank_i = [0]
        uid = [0]

        def uname(n):
            uid[0] += 1
            return f"{n}_u{uid[0]}"

        def bank():
            b = banks[bank_i[0] % 8]
            bank_i[0] += 1
            return b

        k.dma("sp", cons, cst)
        consb = sb("consb", [128, 512], BF16)
        k.copy("dve", consb, cons)
        identb_g, ones_bg, perm_b, bones_b = consb[:, 0:128], consb[:, 128:256], consb[:, 256:384], consb[:, 384:512]
        for l in range(DEPTH):
            k.dma("sp", prm[l], par[l])

        def ada_gen(l, pb):
            with contextlib.ExitStack() as ps:
                lsb = lambda n, s, d=F32: V(ps.enter_context(nc.sbuf_tensor(uname(n), list(s), d))[:])
                ct = lsb("ada_ct", [128, 16, 2])
                cs = lsb("ada_cs", [128, 16, 2])
                ab = lsb("ada_b", [128, 48])
                wb = [lsb(f"ada_w{i}", [128, 16, 256]) for i in range(2)]
                k.dma("sp", ct, cT)
                k.dma("sp", ab, adab[l])
                k.act(cs, ct, AF.Silu)
                pv = pb[:, 0:96].re("p (b j) -> p b j", j=2)
                k.dma("sp", wb[0], adaw[l][:, 0:256].re("(c p) n -> p c n", p=128))
                for cb in range(24):
                    w = wb[cb % 2]
                    if cb + 1 < 24:
                        k.dma("sp", wb[(cb + 1) % 2], adaw[l][:, (cb + 1) * 256:(cb + 2) * 256].re("(c p) n -> p c n", p=128))
                    for j in range(2):
                        blk = cb * 2 + j
                        for c in range(16):
                            k.mm(pv[:, blk, :], w[:, c, j * 128:(j + 1) * 128], cs[:, c, :], start=(c == 0), stop=(c == 15))
                    yield
                for j in range(2):
                    k.tt("dve", mod[l][:, :, j], pv[:, :, j], ab, ALU.add)
                    k.stt("dve", s1[l][:, :, j], mod[l][:, 16:32, j], 1.0, prm[l][:, P_PRE:P_PRE + 16], ALU.add, ALU.mult)
                    k.tt("dve", g1[l][:, :, j], mod[l][:, 32:48, j], prm[l][:, P_POST:P_POST + 16], ALU.mult)

        def phase_ada(l):
            for _ in ada_gen(l, bank()):
                pass
            k.barrier()

        def phase_hproj(l, xsrc):
            with contextlib.ExitStack() as ps:
                lsb = lambda n, s, d=F32: V(ps.enter_context(nc.sbuf_tensor(uname(n), list(s), d))[:])
                hT = lsb("hT", [128, 16, T], BF16)
                hts = [hT.sub((slice(None), slice(None), slice(t0, t0 + tn))) for (t0, tn) in TILES]
                with contextlib.ExitStack() as ps2:
                    lsb2 = lambda n, s, d=F32: V(ps2.enter_context(nc.sbuf_tensor(uname(n), list(s), d))[:])
                    xt = [lsb2(f"h_x{i}", [128, 16, 512]) for i in range(2)]
                    sq = [lsb2(f"h_sq{i}", [128, 512], BF16) for i in range(3)]
                    rs = [lsb2(f"h_rs{i}", [128, 512]) for i in range(2)]
                    tmp = [lsb2(f"h_tmp{i}", [128, 512]) for i in range(3)]
                    for ti, (t0, tn) in enumerate(TILES):
                        j = 1 if ti == 0 else 0
                        x_ = xt[ti % 2]
                        k.dma("sp", x_[:, :, 0:tn], xsrc[:, t0:t0 + tn].re("(c p) t -> p c t", p=128))
                        pb = bank()
                        for c in range(16):
                            s_ = sq[c % 3]
                            k.act(s_[:, 0:tn], x_[:, c, 0:tn], AF.Square)
                            k.mm(pb[:, 0:tn], ones_bg, s_[:, 0:tn], start=(c == 0), stop=(c == 15))
                        r_ = rs[ti % 2]
                        k.act(r_[:, 0:tn], pb[:, 0:tn], AF.Sqrt, bias=EPS, scale=1.0 / D)
                        k.recip(r_[:, 0:tn], r_[:, 0:tn])
                        for c in range(16):
                            t_ = tmp[c % 3]
                            k.stt("dve", t_[:, 0:tn], x_[:, c, 0:tn], s1[l][:, c, j:j + 1], r_[:, 0:tn], ALU.mult, ALU.mult)
                            k.act(hts[ti][:, c, :], t_[:, 0:tn], AF.Identity, bias=mod[l][:, c, j:j + 1])
                    if dbg and "hTd" in dbg and l == dbg.get("_layer", 0):
                        hf = lsb2("h_dbg", [128, T])
                        for c in range(16):
                            for ti, (t0, tn) in enumerate(TILES):
                                k.copy("dve", hf[:, t0:t0 + tn], hts[ti][:, c, :])
                            k.dma("sp", V(dbg_d["hTd"])[c * 128:(c + 1) * 128, :], hf)
                k.barrier()
                if dbg and dbg.get("_stop") == "h":
                    return
                with contextlib.ExitStack() as ps2:
                    lsb2 = lambda n, s, d=F32: V(ps2.enter_context(nc.sbuf_tensor(uname(n), list(s), d))[:])
                    wbuf = [lsb2(f"p_w{i}", [128, 16, 256], BF16) for i in range(3)]
                    stg = [lsb2(f"p_stg{i}", [128, T]) for i in range(3)]
                    for gi in range(NIN // 256):
                        w = wbuf[gi % 3]
                        k.dma("pool", w, win[l][:, gi * 256:(gi + 1) * 256].re("(c p) n -> p c n", p=128))
                        for jb in range(2):
                            blk = gi * 2 + jb
                            st = stg[blk % 3]
                            for ti, (t0, tn) in enumerate(TILES):
                                pb = bank()
                                for c in range(16):
                                    k.mm(pb[:, 0:tn], w[:, c, jb * 128:(jb + 1) * 128], hts[ti][:, c, :],
                                         start=(c == 0), stop=(c == 15), sig=(c == 15))
                                if (blk * 5 + ti) % 2 == 0:
                                    k.copy("act", st[:, t0:t0 + tn], pb[:, 0:tn])
                                else:
                                    k.copy("dve", st[:, t0:t0 + tn], pb[:, 0:tn])
                            k.dma("sp", PTb[blk], st)
            k.barrier()

        def phase_pool(l):
            with contextlib.ExitStack() as ps:
                lsb = lambda n, s, d=F32: V(ps.enter_context(nc.sbuf_tensor(uname(n), list(s), d))[:])
                ic = lsb("pl_ic", [128, 4 * XP])
                pw = lsb("pl_w", [128, 4, 128], BF16)
                xp = [lsb(f"pl_xp{i}", [128, XP]) for i in range(2)]
                A = lsb("pl_A", [128, XP]); B = lsb("pl_B", [128, XP])
                pb16 = [lsb(f"pl_p{i}", [128, XP], BF16) for i in range(2)]
                graw = [lsb(f"pl_g{i}", [128, T]) for i in range(2)]
                stg = [lsb(f"pl_s{i}", [128, T], BF16) for i in range(2)]
                k.dma("sp", ic, icnt)
                k.dma("pool", pw, poolw[l].re("g c d -> c g d"))
                for g in range(4):
                    w = POOL_W[g]; half = w // 2
                    x_ = xp[g % 2]; p16 = pb16[g % 2]; gr = graw[g % 2]; st = stg[g % 2]
                    k.memset("pool", x_, 0.0)
                    k.dma("sp", x_[:, XC0:XC0 + TC], PTb[O_PX // 128 + g][:, 0:TC])
                    k.dma("sp", x_[:, XL0:XL0 + TL], PTb[O_PX // 128 + g][:, TC:T], multi=True)
                    k.dma("sp", gr, PTb[O_PG // 128 + g])
                    src = x_; n = XP; step = 1; bufs = [A, B]; bi = 0
                    while step < w:
                        dst = bufs[bi]; bi ^= 1
                        k.tt("dve", dst[:, 0:n - step], src[:, 0:n - step], src[:, step:n], ALU.add)
                        src = dst; n -= step; step *= 2
                    lo, hi = XC0, XL0 + TL
                    dst = bufs[bi]
                    k.tt("dve", dst[:, lo:hi], src[:, lo - half:hi - half], ic[:, g * XP + lo:g * XP + hi], ALU.mult)
                    k.tt("pool", p16[:, lo:hi], dst[:, lo:hi], x_[:, lo:hi], ALU.subtract)
                    k.act(gr, gr, AF.Silu)
                    for ti, (t0, tn) in enumerate(TILES):
                        xo = XC0 + t0 if ti == 0 else XL0 + (t0 - TC)
                        pb = bank()
                        k.mm(pb[:, 0:tn], pw[:, g, :], p16[:, xo:xo + tn])
                        k.stt("dve", st[:, t0:t0 + tn], pb[:, 0:tn], prm[l][:, P_PSC + g:P_PSC + g + 1], gr[:, t0:t0 + tn],
                              ALU.mult, ALU.mult)
                    k.dma("sp", mixb[12 + g], st)
            k.barrier()

        def phase_att(l, need_ctx, bg=None):
            SC = 128.0 ** -0.5
            with contextlib.ExitStack() as ps:
                lsb = lambda n, s, d=F32: V(ps.enter_context(nc.sbuf_tensor(uname(n), list(s), d))[:])
                rp = lsb("at_rope", [128, 2 * TL])
                raw = [lsb(f"at_raw{i}", [128, T]) for i in range(2)]
                sq = [lsb(f"at_sq{i}", [128, 512], BF16) for i in range(2)]
                rs = [lsb(f"at_rs{i}", [128, 512]) for i in range(2)]
                kn = [lsb(f"at_kn{i}", [128, 512], BF16) for i in range(2)]
                t1 = [lsb(f"at_t1{i}", [128, 512]) for i in range(2)]
                t2 = [lsb(f"at_t2{i}", [128, 512]) for i in range(2)]
                kT = [lsb(f"at_kT{i}", [128, T], BF16) for i in range(2)]
                qT = [lsb(f"at_qT{i}", [128, T], BF16) for i in range(8)]
                vtm = [lsb(f"at_v{i}", [128, 18, 128], BF16) for i in range(2)]
                sg = [lsb(f"at_sg{i}", [128, T], BF16) for i in range(8)]
                pt = [lsb(f"at_pt{i}", [128, 512], BF16) for i in range(6)]
                ones_b = lsb("at_ones", [128, 128], BF16)
                rd = [lsb(f"at_rd{i}", [128, 512]) for i in range(2)]
                ot = [lsb(f"at_o{i}", [128, 512]) for i in range(2)]
                stg = [lsb(f"at_stg{i}", [128, T], BF16) for i in range(2)]
                k.dma("sp", rp, rope)
                k.copy("dve", ones_b, ones)
                cnt = [0]

                def normrope(dst, src_blk, gcol):
                    r_ = raw[cnt[0] % 2]; cnt[0] += 1
                    k.dma("sp", r_, PTb[src_blk])
                    for ti, (t0, tn) in enumerate(TILES):
                        s_ = sq[ti % 2]; rr = rs[ti % 2]; kn_ = kn[ti % 2]
                        k.act(s_[:, 0:tn], r_[:, t0:t0 + tn], AF.Square)
                        pb = bank()
                        k.mm(pb[:, 0:tn], ones_bg, s_[:, 0:tn])
                        k.act(rr[:, 0:tn], pb[:, 0:tn], AF.Ln, bias=EPS, scale=1.0 / 128)
                        k.act(rr[:, 0:tn], rr[:, 0:tn], AF.Exp, scale=-0.5)
                        if ti == 0:
                            k.stt("dve", dst[:, t0:t0 + tn], r_[:, t0:t0 + tn], prm[l][:, gcol:gcol + 1], rr[:, 0:tn],
                                  ALU.mult, ALU.mult)
                            continue
                        k.stt("dve", kn_[:, 0:tn], r_[:, t0:t0 + tn], prm[l][:, gcol:gcol + 1], rr[:, 0:tn],
                              ALU.mult, ALU.mult)
                        pb2 = bank()
                        k.mm(pb2[:, 0:tn], perm_b, kn_[:, 0:tn])
                        a_ = t1[ti % 2]; b_ = t2[ti % 2]
                        lt = t0 - TC
                        k.tt("dve", a_[:, 0:tn], kn_[:, 0:tn], rp[:, lt:lt + tn], ALU.mult)
                        k.tt("dve", b_[:, 0:tn], pb2[:, 0:tn], rp[:, TL + lt:TL + lt + tn], ALU.mult)
                        k.tt("pool", dst[:, t0:t0 + tn], a_[:, 0:tn], b_[:, 0:tn], ALU.add)

                for g in range(2):
                    normrope(kT[g], O_K // 128 + g, P_KN)
                    vr = raw[cnt[0] % 2]; cnt[0] += 1
                    k.dma("sp", vr, PTb[O_V // 128 + g])
                    for kt in range(18):
                        pb = bank()
                        k.tr(pb[:, 0:128], vr[:, kt * 128:(kt + 1) * 128], ident)
                        k.copy("act" if kt % 2 else "dve", vtm[g][:, kt, :], pb[:, 0:128])
                for h in range(8):
                    normrope(qT[h], O_Q // 128 + h, P_QN)
                    gr = raw[cnt[0] % 2]; cnt[0] += 1
                    k.dma("sp", gr, PTb[O_G // 128 + h])
                    k.act(sg[h], gr, AF.Silu)
                gcount = 0
                for h in range(8):
                    g = h // 4
                    q_ = qT[h]; sg_ = sg[h]; st = stg[h % 2]
                    groups = [(TC + 512 * i, 512, 18) for i in range(4)]
                    if need_ctx:
                        groups = [(0, TC, 2)] + groups
                    for (q0, qn, nk) in groups:
                        par = gcount % 2; gcount += 1
                        po = banks[par]; pd = banks[2 + par]
                        sb_ = banks[4:8] if bg is None else banks[4:7]
                        if bg is not None:
                            next(bg, None)
                        pend = []

                        def issue_s(kt):
                            pss = sb_[kt % len(sb_)]
                            k.mm(pss[:, 0:qn], kT[g][:, kt * 128:(kt + 1) * 128], q_[:, q0:q0 + qn])
                            p_ = pt[kt % 6]
                            k.act(p_[:, 0:qn], pss[:, 0:qn], AF.Exp, scale=SC)
                            return p_
                        look = len(sb_) - 1
                        for kt in range(min(look, nk)):
                            pend.append(issue_s(kt))
                        for kt in range(nk):
                            p_ = pend[kt]
                            k.mm(po[:, 0:qn], vtm[g][:, kt, :], p_[:, 0:qn], start=(kt == 0), stop=(kt == nk - 1))
                            k.mm(pd[:, 0:qn], ones_b, p_[:, 0:qn], start=(kt == 0), stop=(kt == nk - 1))
                            if kt + look < nk:
                                pend.append(issue_s(kt + look))
                        rd_ = rd[par]; o_ = ot[par]
                        k.recip(rd_[:, 0:qn], pd[:, 0:qn])
                        k.tt("dve", o_[:, 0:qn], po[:, 0:qn], rd_[:, 0:qn], ALU.mult)
                        k.tt("pool", st[:, q0:q0 + qn], o_[:, 0:qn], sg_[:, q0:q0 + qn], ALU.mult)
                    if need_ctx:
                        k.dma("sp", mixb[h], st)
                    else:
                        k.dma("sp", mixb[h][:, TC:T], st[:, TC:T])
                if bg is not None:
                    for _ in bg:
                        pass
            k.barrier()

        def phase_rwkv(l, need_ctx):
            NCH, C, G = T // 128, 128, 3
            DEC = -float(np.exp(-0.5))
            with contextlib.ExitStack() as ps:
                lsb = lambda n, s, d=F32: V(ps.enter_context(nc.sbuf_tensor(uname(n), list(s), d))[:])
                big = lambda n: lsb(n, [128, T])
                rc = lsb("rw_c", [128, 2048])
                MU2, ML2 = rc[:, 0:512], rc[:, 512:1024]
                LV4, I4 = rc[:, 1024:1536], rc[:, 1536:2048]
                identb = lsb("rw_idb", [128, 128], BF16)
                tdw = big("rw_tdw"); da = big("rw_da")
                wup = lsb("rw_wup", [128, 512]); aup = lsb("rw_aup", [128, 512])
                c0s = lsb("rw_c0", [128, 12]); omka = lsb("rw_omka", [128, 4])
                KR = [lsb(f"rw_KR{d}", [128, 2, T], BF16) for d in range(2)]
                KMb = [lsb(f"rw_KMb{d}", [128, T], BF16) for d in range(2)]
                Abb = [lsb(f"rw_Abb{d}", [128, T], BF16) for d in range(2)]
                gt = [lsb(f"rw_gt{d}", [128, NCH]) for d in range(2)]
                vb = lsb("rw_vb", [128, T], BF16)
                v = big("rw_v"); Oacc = big("rw_O"); bsum = big("rw_bs")
                stg = lsb("rw_stg", [128, T], BF16)

                k.dma("sp", rc[:, 0:1024], rwc[:, 0:1024])
                k.dma("sp", rc[:, 1024:2048], rwc[:, 2304:3328], multi=True)
                k.copy("dve", identb, ident)
                k.dma("sp", wup, V(rwwup_d)[l].re("d r c -> (d r) c"))
                k.dma("sp", aup, V(rwaup_d)[l].re("d r c -> (d r) c"))
                k.dma("sp", tdw, PTb[O_LW // 128])
                k.dma("sp", da, PTb[O_LA // 128])
                k.act(tdw, tdw, AF.Tanh)
                for i in range(3):
                    mcol = P_MU + i * 8
                    k.ts("dve", c0s[:, i * 4:(i + 1) * 4], prm[l][:, mcol:mcol + 4], -1.0, ALU.mult, 1.0, ALU.add)
                    k.tt("dve", c0s[:, i * 4:(i + 1) * 4], c0s[:, i * 4:(i + 1) * 4], prm[l][:, mcol + 4:mcol + 8], ALU.subtract)
                k.ts("dve", omka, prm[l][:, P_KA:P_KA + 4], -1.0, ALU.mult, 1.0, ALU.add)
                SEGS = ((0, TC, 1), (TC, TL, TC + 2))
                orders = [list(range(NCH)), [1, 0] + list(range(NCH - 1, 1, -1))]
                M2s = [MU2, ML2]

                for blk in range(4):
                    with contextlib.ExitStack() as pp_:
                        psb = lambda n, s, d=F32: V(pp_.enter_context(nc.sbuf_tensor(uname(n), list(s), d))[:])
                        pbig = lambda n: psb(n, [128, T])
                        raws = [psb(f"rw_raw{i}", [128, T + 3]) for i in range(2)]; smask = raws[1][:, 0:T]
                        r = pbig("rw_r"); kx = pbig("rw_k"); kk = pbig("rw_kk")
                        Pb = pbig("rw_P"); Pe = pbig("rw_Pe"); Ab = pbig("rw_A"); KM = pbig("rw_KM"); E = pbig("rw_E")
                        sq = [psb(f"rw_sq{i}", [128, 512], BF16) for i in range(2)]
                        rn = [psb(f"rw_rn{i}", [128, 512]) for i in range(2)]
                        k.memset("pool", raws[0], 0.0)
                        k.memset("pool", raws[1], 0.0)
                        for i, (dst, off) in enumerate(((r, O_RR), (kx, O_RK), (v, O_RV))):
                            raw = raws[i % 2]
                            src = PTb[off // 128 + blk]
                            k.dma("sp", raw[:, 1:1 + TC], src[:, 0:TC])
                            k.dma("sp", raw[:, TC + 2:TC + 2 + TL], src[:, TC:T], multi=True)
                            mcol = P_MU + i * 8 + blk
                            for (a, n, ro) in SEGS:
                                k.ts("dve", dst[:, a:a + n], raw[:, ro:ro + n], c0s[:, i * 4 + blk:i * 4 + blk + 1], ALU.mult)
                                k.stt("dve", dst[:, a:a + n], raw[:, ro - 1:ro - 1 + n], prm[l][:, mcol:mcol + 1],
                                      dst[:, a:a + n], ALU.mult, ALU.add)
                                k.stt("dve", dst[:, a:a + n], raw[:, ro + 1:ro + 1 + n], prm[l][:, mcol + 4:mcol + 5],
                                      dst[:, a:a + n], ALU.mult, ALU.add)
                        k.copy("act", vb, v)
                        k.act(E, kx, AF.Copy, scale=prm[l][:, P_KK + blk:P_KK + blk + 1])
                        for ti, (t0, tn) in enumerate(TILES):
                            s_ = sq[ti % 2]; r_ = rn[ti % 2]
                            k.act(s_[:, 0:tn], E[:, t0:t0 + tn], AF.Square)
                            pb = bank()
                            k.mm(pb[:, 0:tn], bones_b, s_[:, 0:tn])
                            k.ts("dve", r_[:, 0:tn], pb[:, 0:tn], 1e-24, ALU.max)
                            k.act(r_[:, 0:tn], r_[:, 0:tn], AF.Ln)
                            k.act(r_[:, 0:tn], r_[:, 0:tn], AF.Exp, scale=-0.5)
                            k.tt("dve", kk[:, t0:t0 + tn], E[:, t0:t0 + tn], r_[:, 0:tn], ALU.mult)
                        k.memset("pool", smask, 1.0)
                        k.memset("pool", smask.re("p (c t) -> p c t", t=C)[:, :, 0:1], 0.0)
                        for d in range(2):
                            ds = slice(d * 64, (d + 1) * 64)
                            bc = slice(blk * 128, (blk + 1) * 128)
                            for ti, (t0, tn) in enumerate(TILES):
                                pb = bank()
                                k.mm(pb[:, 0:tn], wup[ds, bc], tdw[ds, t0:t0 + tn])
                                k.act(Pe[:, t0:t0 + tn], pb[:, 0:tn], AF.Sigmoid, bias=prm[l][:, P_W0 + d * 4 + blk:P_W0 + d * 4 + blk + 1])
                                pb2 = bank()
                                k.mm(pb2[:, 0:tn], aup[ds, bc], da[ds, t0:t0 + tn])
                                k.act(Ab[:, t0:t0 + tn], pb2[:, 0:tn], AF.Sigmoid, bias=prm[l][:, P_A0 + d * 4 + blk:P_A0 + d * 4 + blk + 1])
                            k.act(Pe, Pe, AF.Copy, scale=DEC)
                            k.scan(Pb, smask, Pe, 0.0, ALU.mult, ALU.add)
                            k.tt("dve", Pe, Pb, Pe, ALU.subtract)
                            Ptot = Pb.re("p (c t) -> p c t", t=C)[:, :, C - 1]
                            k.act(gt[d], Ptot, AF.Exp)
                            k.ts("dve", E, Ab, prm[l][:, P_KA + blk:P_KA + blk + 1], ALU.mult, omka[:, blk:blk + 1], ALU.add)
                            k.tt("dve", KM, E, kx, ALU.mult)
                            k.tt("pool", Ab, Ab, kk, ALU.mult)
                            k.stt("dve", E, r, prm[l][:, P_RK + blk:P_RK + blk + 1], KM, ALU.mult, ALU.mult)
                            for ti, (t0, tn) in enumerate(TILES):
                                pb = bank()
                                k.mm(pb[:, 0:tn], bones, E[:, t0:t0 + tn])
                                if d == 0:
                                    k.act(bsum[:, t0:t0 + tn], pb[:, 0:tn], AF.Copy, scale=0.5)
                                else:
                                    k.stt("dve", bsum[:, t0:t0 + tn], pb[:, 0:tn], 0.5, bsum[:, t0:t0 + tn], ALU.mult, ALU.add)
                            if d == 0:
                                Einc, Eexc, Tmp = Pb, Pe, E
                            else:
                                for c in range(NCH):
                                    cs_ = slice(c * C, (c + 1) * C)
                                    k.act(E[:, cs_], Pe[:, cs_], AF.Identity, bias=Ptot[:, c:c + 1], scale=-1.0)
                                for c in range(NCH):
                                    cs_ = slice(c * C, (c + 1) * C)
                                    k.act(Pe[:, cs_], Pb[:, cs_], AF.Identity, bias=Ptot[:, c:c + 1], scale=-1.0)
                                Einc, Eexc, Tmp = E, Pe, Pb
                            k.act(Tmp, Einc, AF.Exp)
                            k.tt("dve", KR[d][:, 1, :], r, Tmp, ALU.mult)
                            k.act(Tmp, Eexc, AF.Exp)
                            k.tt("dve", KR[d][:, 0, :], kk, Tmp, ALU.mult)
                            k.act(Tmp, Einc, AF.Exp, scale=-1.0)
                            k.tt("dve", KMb[d], KM, Tmp, ALU.mult)
                            k.tt("pool", Abb[d], Ab, Tmp, ALU.mult)
                    k.barrier()
                    with contextlib.ExitStack() as sp_:
                        ssb = lambda n, s, d=F32: V(sp_.enter_context(nc.sbuf_tensor(uname(n), list(s), d))[:])
                        NB = 2 * G
                        akbs = [ssb(f"rw_akb{i}", [128, 4, 512], BF16) for i in range(NB)]
                        ptms = [ssb(f"rw_ptm{i}", [128, 512], BF16) for i in range(NB)]
                        tms = [ssb(f"rw_tm{i}", [128, 768], BF16) for i in range(NB)]
                        invs = [ssb(f"rw_inv{i}", [128, 512], BF16) for i in range(G)]
                        ntms = [ssb(f"rw_ntm{i}", [128, 512], BF16) for i in range(G)]
                        t1s = [ssb(f"rw_t1{i}", [128, 512], BF16) for i in range(G)]
                        Xs = [ssb(f"rw_X{i}", [128, 256], BF16) for i in range(2)]
                        nUs = [ssb(f"rw_nU{i}", [128, 256], BF16) for i in range(2)]
                        Hs = [[ssb(f"rw_H{d}{i}", [128, 64]) for i in range(2)] for d in range(2)]
                        Hbs = [[ssb(f"rw_Hb{d}{i}", [128, 64], BF16) for i in range(2)] for d in range(2)]
                        hgs = [[ssb(f"rw_hg{d}{i}", [128, 64]) for i in range(2)] for d in range(2)]
                        sgate = ssb("rw_sg", [128, T])
                        ft = [ssb(f"rw_ft{i}", [128, 512]) for i in range(4)]
                        p3 = lambda t_: t_.re("p (a m) -> p a m", a=4)
                        k.memset("pool", Oacc, 0.0)
                        for d in range(2):
                            k.memset("dve", Hs[d][0], 0.0)
                            k.memset("dve", Hbs[d][0], 0.0)

                        def stage1(steps):
                            for st in steps:
                                sl = st % NB
                                cs = [slice(orders[d][st] * C, (orders[d][st] + 1) * C) for d in range(2)]
                                pA = bank(); pB = bank()
                                k.mm(pA[:, 0:128], vb[:, cs[0]], identb, sig=False)
                                k.mm(pA[:, 128:256], KMb[0][:, cs[0]], identb, sig=False)
                                k.mm(pA[:, 256:384], Abb[0][:, cs[0]], identb, sig=False)
                                k.mm(pA[:, 384:512], vb[:, cs[1]], identb)
                                k.mm(pB[:, 0:128], KMb[1][:, cs[1]], identb, sig=False)
                                k.mm(pB[:, 128:256], Abb[1][:, cs[1]], identb)
                                k.copy("act", tms[sl][:, 0:512], pA)
                                k.copy("act", tms[sl][:, 512:768], pB[:, 0:256])
                            yield
                            for st in steps:
                                sl = st % NB
                                cs = [slice(orders[d][st] * C, (orders[d][st] + 1) * C) for d in range(2)]
                                for d in range(2):
                                    for hb in range(2):
                                        hs = slice(hb * 64, (hb + 1) * 64)
                                        pa = bank()
                                        k.mm(pa[:, 0:256], KMb[d][hs, cs[d]], KR[d][hs, :, cs[d]], sig=False)
                                        k.mm(pa[:, 256:512], Abb[d][hs, cs[d]], KR[d][hs, :, cs[d]])
                                        k.tt("dve", akbs[sl][:, d * 2 + hb, :], pa, M2s[d], ALU.mult)
                            yield
                            for st in steps:
                                sl = st % NB
                                NT4 = akbs[sl][:, :, 256:384]
                                nm4 = ntms[st % G]
                                k.stt("dve", p3(nm4), p3(LV4), 0.0, NT4, ALU.is_equal, ALU.mult)
                                k.tt("pool", ptms[sl], I4, nm4, ALU.subtract)
                            yield
                            for lv in range(1, 7):
                                for st in steps:
                                    sl = st % NB
                                    pq = bank()
                                    for ln in range(4):
                                        k.mm(pq[:, ln * 128:(ln + 1) * 128], ptms[sl][:, ln * 128:(ln + 1) * 128], identb, sig=(ln == 3))
                                    k.copy("act", invs[st % G], pq)
                                    k.stt("dve", p3(ntms[st % G]), p3(LV4), float(lv), akbs[sl][:, :, 256:384], ALU.is_equal, ALU.mult)
                                yield
                                for st in steps:
                                    pt1 = bank()
                                    for ln in range(4):
                                        k.mm(pt1[:, ln * 128:(ln + 1) * 128], ntms[st % G][:, ln * 128:(ln + 1) * 128],
                                             invs[st % G][:, ln * 128:(ln + 1) * 128], sig=(ln == 3))
                                    k.copy("act", t1s[st % G], pt1)
                                yield
                                for st in steps:
                                    sl = st % NB
                                    pp = bank()
                                    for ln in range(4):
                                        k.mm(pp[:, ln * 128:(ln + 1) * 128], t1s[st % G][:, ln * 128:(ln + 1) * 128],
                                             ptms[sl][:, ln * 128:(ln + 1) * 128], sig=(ln == 3))
                                    k.tt("dve", ptms[sl], ptms[sl], pp, ALU.subtract)
                                yield

                        def stage2(steps):
                            for st in steps:
                                sl = st % NB
                                akb = akbs[sl]; ptm = ptms[sl]; tm_ = tms[sl]
                                X = Xs[st % 2]; nU = nUs[st % 2]
                                cch = [orders[d][st] for d in range(2)]
                                cs = [slice(cch[d] * C, (cch[d] + 1) * C) for d in range(2)]
                                vtm = [tm_[:, 0:128], tm_[:, 384:512]]
                                kbtm = [tm_[:, 128:256], tm_[:, 512:640]]
                                bbtm = [tm_[:, 256:384], tm_[:, 640:768]]
                                Hc = [Hs[d][st % 2] for d in range(2)]; Hn = [Hs[d][(st + 1) % 2] for d in range(2)]
                                Hb = [Hbs[d][st % 2] for d in range(2)]; Hbn = [Hbs[d][(st + 1) % 2] for d in range(2)]
                                hg = [hgs[d][st % 2] for d in range(2)]
                                for d in range(2):
                                    k.act(hg[d], Hc[d], AF.Copy, scale=gt[d][:, cch[d]:cch[d] + 1])
                                px = bank()
                                for d in range(2):
                                    for hb in range(2):
                                        ln = d * 2 + hb
                                        hs = slice(hb * 64, (hb + 1) * 64); vs = slice(hb * 64, (hb + 1) * 64)
                                        k.mm(px[:, ln * 64:(ln + 1) * 64], KR[d][hs, 0, cs[d]], Hb[d][hs, :], start=True, stop=False, sig=False)
                                        k.mm(px[:, ln * 64:(ln + 1) * 64], akb[:, ln, 0:128], vtm[d][:, vs], start=False, stop=True, sig=(ln == 3))
                                k.copy("act", X, px[:, 0:256])
                                yield
                                pu = bank()
                                for ln in range(4):
                                    k.mm(pu[:, ln * 64:(ln + 1) * 64], ptm[:, ln * 128:(ln + 1) * 128], X[:, ln * 64:(ln + 1) * 64], sig=(ln == 3))
                                k.act(nU, pu[:, 0:256], AF.Copy, scale=-1.0)
                                yield
                                ph = bank()
                                for d in range(2):
                                    for hb in range(2):
                                        ln = d * 2 + hb
                                        hs = slice(hb * 64, (hb + 1) * 64); vs = slice(hb * 64, (hb + 1) * 64)
                                        k.mm(ph[hs, d * 64:(d + 1) * 64], kbtm[d][:, hs], vtm[d][:, vs], start=True, stop=False, sig=False)
                                        k.mm(ph[hs, d * 64:(d + 1) * 64], bbtm[d][:, hs], nU[:, ln * 64:(ln + 1) * 64], start=False, stop=True, sig=(ln == 3))
                                for d in range(2):
                                    k.stt("dve", Hn[d], ph[:, d * 64:(d + 1) * 64], gt[d][:, cch[d]:cch[d] + 1], hg[d], ALU.mult, ALU.add)
                                    k.copy("pool", Hbn[d], Hn[d])
                                po = bank()
                                for d in range(2):
                                    if not (need_ctx or cch[d] >= 2):
                                        continue
                                    for hb in range(2):
                                        ln = d * 2 + hb
                                        hs = slice(hb * 64, (hb + 1) * 64); vs = slice(hb * 64, (hb + 1) * 64)
                                        k.mm(po[hs, d * 128:(d + 1) * 128], Hb[d][hs, :], KR[d][hs, 1, cs[d]], start=True, stop=False, sig=False)
                                        k.mm(po[hs, d * 128:(d + 1) * 128], vtm[d][:, vs], akb[:, ln, 128:256], start=False, stop=False, sig=False)
                                        k.mm(po[hs, d * 128:(d + 1) * 128], nU[:, ln * 64:(ln + 1) * 64], akb[:, ln, 384:512], start=False, stop=True, sig=(hb == 1))
                                    k.tt("dve", Oacc[:, cs[d]], Oacc[:, cs[d]], po[:, d * 128:(d + 1) * 128], ALU.add)
                                yield

                        groups = [list(range(g0, g0 + G)) for g0 in range(0, NCH, G)]
                        for _ in stage1(groups[0]):
                            pass
                        for gi in range(len(groups)):
                            g1 = stage1(groups[gi + 1]) if gi + 1 < len(groups) else iter(())
                            g2 = stage2(groups[gi])
                            a1 = a2 = True
                            while a1 or a2:
                                if a1:
                                    a1 = next(g1, "end") != "end"
                                if a1:
                                    a1 = next(g1, "end") != "end"
                                if a2:
                                    a2 = next(g2, "end") != "end"
                        k.dma("sp", sgate, PTb[O_RG // 128 + blk])
                        k.act(sgate, sgate, AF.Silu)
                        for ti, (t0, tn) in enumerate(TILES):
                            if ti == 0 and not need_ctx:
                                continue
                            ts_ = slice(t0, t0 + tn)
                            oc = ft[0]; s_ = ft[1]; r_ = ft[2]; bb_ = ft[3]
                            pb = bank()
                            k.mm(pb[:, 0:tn], bones, Oacc[:, ts_])
                            k.stt("dve", oc[:, 0:tn], pb[:, 0:tn], -1.0 / 64, Oacc[:, ts_], ALU.mult, ALU.add)
                            k.act(s_[:, 0:tn], oc[:, 0:tn], AF.Square)
                            pb2 = bank()
                            k.mm(pb2[:, 0:tn], bones, s_[:, 0:tn])
                            k.act(r_[:, 0:tn], pb2[:, 0:tn], AF.Ln, bias=64e-5, scale=1.0 / 64)
                            k.act(r_[:, 0:tn], r_[:, 0:tn], AF.Exp, scale=-0.5)
                            k.tt("dve", oc[:, 0:tn], oc[:, 0:tn], r_[:, 0:tn], ALU.mult)
                            k.ts("dve", oc[:, 0:tn], oc[:, 0:tn], prm[l][:, P_LNW + blk:P_LNW + blk + 1], ALU.mult,
                                 prm[l][:, P_LNB + blk:P_LNB + blk + 1], ALU.add)
                            k.tt("pool", bb_[:, 0:tn], bsum[:, ts_], v[:, ts_], ALU.mult)
                            k.tt("pool", oc[:, 0:tn], oc[:, 0:tn], bb_[:, 0:tn], ALU.add)
                            k.tt("pool", stg[:, ts_], oc[:, 0:tn], sgate[:, ts_], ALU.mult)
                        if need_ctx:
                            k.dma("sp", mixb[8 + blk], stg)
                        else:
                            k.dma("sp", mixb[8 + blk][:, TC:T], stg[:, TC:T])
                    k.barrier()
            k.barrier()

        def phase_out(l, xsrc, xdst, need_ctx):
            TN = 256
            tiles = [(t0, TN) for t0 in range(0 if need_ctx else TC, T, TN)]
            with contextlib.ExitStack() as ps:
                lsb = lambda n, s, d=F32: V(ps.enter_context(nc.sbuf_tensor(uname(n), list(s), d))[:])
                wo = lsb("o_w", [128, 16, D], BF16)
                wos = [wo.sub((slice(None), slice(None), slice(i * 512, (i + 1) * 512))) for i in range(4)]
                mx = [lsb(f"o_mx{i}", [128, 16, TN], BF16) for i in range(3)]
                xt = [lsb(f"o_x{i}", [128, 16, TN]) for i in range(3)]
                y2 = [lsb(f"o_y{i}", [128, 16, TN]) for i in range(2)]
                ys2 = [[y_.sub((slice(None), i, slice(None))) for i in range(16)] for y_ in y2]
                sq = [lsb(f"o_sq{i}", [128, TN], BF16) for i in range(4)]
                rs2 = [lsb(f"o_rs{i}", [128, TN]) for i in range(2)]
                for i in range(4):
                    k.dma("pool", wos[i], wout[l][:, i * 512:(i + 1) * 512].re("(c p) n -> p c n", p=128))

                def load(it):
                    t0, tn = tiles[it]
                    k.dma("sp", mx[it % 3], mixT[:, t0:t0 + tn].re("(c p) t -> p c t", p=128))
                    k.dma("sp", xt[it % 3], xsrc[:, t0:t0 + tn].re("(c p) t -> p c t", p=128))

                load(0)
                if len(tiles) > 1:
                    load(1)
                for it, (t0, tn) in enumerate(tiles):
                    j = 1 if t0 < TC else 0
                    m_ = mx[it % 3]; x_ = xt[it % 3]; ys = ys2[it % 2]; rs = rs2[it % 2]
                    pss = banks[0]
                    for db in range(16):
                        pb = banks[1 + db % 7]
                        for c in range(16):
                            k.mm(pb[:, 0:tn], wos[db // 4][:, c, (db % 4) * 128:(db % 4 + 1) * 128], m_[:, c, :],
                                 start=(c == 0), stop=(c == 15), sig=(c == 15))
                        k.copy("dve" if db % 2 else "act", ys[db], pb[:, 0:tn])
                        k.act(sq[db % 4], ys[db], AF.Square)
                        if db >= 2:
                            k.mm(pss[:, 0:tn], ones_bg, sq[(db - 2) % 4], start=(db == 2), stop=False)
                    for db in (14, 15):
                        k.mm(pss[:, 0:tn], ones_bg, sq[db % 4], start=False, stop=(db == 15))
                    if it + 2 < len(tiles):
                        load(it + 2)
                    k.act(rs, pss[:, 0:tn], AF.Ln, bias=EPS, scale=1.0 / D)
                    k.act(rs, rs, AF.Exp, scale=-0.5)
                    for db in range(16):
                        k.stt("dve", ys[db], ys[db], g1[l][:, db, j:j + 1], rs, ALU.mult, ALU.mult)
                        k.tt("pool" if db % 3 == 2 else "dve", x_[:, db, :], x_[:, db, :], ys[db], ALU.add)
                    if xdst is outT:
                        k.dma("sp", outT[:, t0 - TC:t0 - TC + tn].re("(c p) t -> p c t", p=128), x_)
                    else:
                        k.dma("sp", xdst[:, t0:t0 + tn].re("(c p) t -> p c t", p=128), x_)
            k.barrier()

        stop = dbg.get("_stop") if dbg else None
        only = dbg.get("_only") if dbg else None
        for l in range(nlayers):
            phase_ada(l)
        xsrc = xT
        for l in range(nlayers):
            last = (l == DEPTH - 1)
            if stop == "ada":
                break
            phase_hproj(l, xsrc)
            if dbg and "PT" in dbg and l == dbg.get("_layer", 0):
                with contextlib.ExitStack() as ps:
                    bnc = V(ps.enter_context(nc.sbuf_tensor(uname("dbg_b"), [128, T], F32))[:])
                    for blk in range(NIN // 128):
                        k.dma("sp", bnc, PTb[blk])
                        k.dma("sp", V(dbg_d["PT"])[blk * 128:(blk + 1) * 128, :], bnc)
                k.barrier()
            if stop in ("h", "proj"):
                break
            if dbg and dbg.get("_zero_mix"):
                with contextlib.ExitStack() as ps:
                    zz = V(ps.enter_context(nc.sbuf_tensor(uname("dbg_z"), [128, T], BF16))[:])
                    k.memset("dve", zz, 0.0)
                    for b_ in range(16):
                        k.dma("sp", mixb[b_], zz)
                k.barrier()
            if only is None or "pool" in only:
                phase_pool(l)
            if only is None or "att" in only:
                phase_att(l, need_ctx=not last)
            if only is None or "rwkv" in only:
                if phase_rwkv(l, need_ctx=not last):
                    break
            if dbg and "mix" in dbg and l == dbg.get("_layer", 0):
                with contextlib.ExitStack() as ps:
                    b16 = V(ps.enter_context(nc.sbuf_tensor(uname("dbg_m16"), [128, T], BF16))[:])
                    b32 = V(ps.enter_context(nc.sbuf_tensor(uname("dbg_m32"), [128, T], F32))[:])
                    for b_ in range(16):
                        k.dma("sp", b16, mixb[b_])
                        k.copy("dve", b32, b16)
                        k.dma("sp", V(dbg_d["mix"])[b_ * 128:(b_ + 1) * 128, :], b32)
                k.barrier()
            if stop == "mix":
                break
            phase_out(l, xsrc, outT if last else x1T, need_ctx=not last)
            if dbg and "x1" in dbg and l == 0:
                with contextlib.ExitStack() as ps:
                    bnc = V(ps.enter_context(nc.sbuf_tensor(uname("dbg_x"), [128, T], F32))[:])
                    for b_ in range(16):
                        k.dma("sp", bnc, x1T[b_ * 128:(b_ + 1) * 128, :])
                        k.dma("sp", V(dbg_d["x1"])[b_ * 128:(b_ + 1) * 128, :], bnc)
                k.barrier()
            xsrc = x1T
        if dbg and "mod" in dbg:
            for l in range(nlayers):
                k.dma("sp", V(dbg_d["mod"])[l], mod[l])
        k.barrier()
        k.final_wait()
    return nc


def make_consts():
    c = np.zeros((128, 512), np.float32)
    c[:, 0:128] = np.eye(128, dtype=np.float32)
    c[:, 128:256] = 1.0
    for i in range(128):
        j = i % 64
        pi = i + 32 if j < 32 else i - 32
        c[pi, 256 + i] = 1.0
    c[0:64, 384:448] = 1.0
    c[64:128, 448:512] = 1.0
    return c


def make_rope():
    half = 64
    nfreq = 32
    inv = (np.float32(10000.0) ** (-(np.arange(nfreq, dtype=np.float32) * np.float32(2.0) / np.float32(half)))).astype(np.float32)
    t = np.arange(TL)
    row = (t // 64).astype(np.float32)
    col = (t % 64).astype(np.float32)
    out = np.zeros((128, 2 * TL), np.float32)
    for i in range(128):
        axis, hf, fr = i // 64, (i % 64) // 32, i % 32
        ang = ((row if axis == 0 else col) * inv[fr]).astype(np.float32)
        out[i, :TL] = np.cos(ang)
        out[i, TL:] = np.sin(ang) * (-1.0 if hf == 0 else 1.0)
    return out


def make_icnt():
    tab = np.zeros((4, XP), np.float32)
    for g, w in enumerate(POOL_W):
        h = w // 2
        for (x0, n) in ((XC0, TC), (XL0, TL)):
            t = np.arange(n)
            lo = np.clip(t - h, 0, n - 1)
            hi = np.clip(t + h - 1, 0, n - 1)
            tab[g, x0:x0 + n] = 1.0 / (hi - lo + 1).astype(np.float32)
    return np.ascontiguousarray(np.broadcast_to(tab.reshape(1, -1), (128, 4 * XP)))


def make_rwc():
    c = np.zeros((128, 3328), np.float32)
    ii = np.arange(128)
    x = ii[:, None] ^ ii[None, :]
    lvl = np.full((128, 128), -1.0, np.float32)
    nz = x > 0
    lvl[nz] = np.floor(np.log2(x[nz])).astype(np.float32)
    lvL = np.where(ii[:, None] > ii[None, :], lvl, -1.0).astype(np.float32)
    c[:, 1792:2048] = np.concatenate([lvL, lvL], axis=1)
    c[:, 2048:2304] = np.concatenate([lvL.T, lvL.T], axis=1)
    c[:, 2304:2816] = np.concatenate([lvL.T, lvL.T, lvL, lvL], axis=1)
    c[:, 2816:3328] = np.concatenate([np.eye(128, dtype=np.float32)] * 4, axis=1)
    tS = np.triu(np.ones((128, 128), np.float32), 1)
    tI = np.triu(np.ones((128, 128), np.float32), 0)
    c[:, 0:512] = np.concatenate([tS, tI, tS, tI], axis=1)
    c[:, 512:1024] = np.concatenate([tS.T, tI.T, tS.T, tI.T], axis=1)
    c[:, 1024:1280] = np.concatenate([np.eye(128, dtype=np.float32)] * 2, axis=1)
    c[:, 1280:1536] = np.concatenate([tS.T, tS.T], axis=1)
    c[:, 1536:1792] = np.concatenate([tS, tS], axis=1)
    return c


def chunkT(v):
    return np.ascontiguousarray(v.reshape(-1, 128).T)


def make_inputs(b, inp):
    f = np.float32
    xT = np.ascontiguousarray(np.concatenate([inp["ctx"][b], inp["x"][b]], axis=0).T.astype(f))
    cT = np.stack([chunkT(inp["c"][b]), chunkT(inp["c_ctx"])], axis=-1).astype(f)
    params = np.zeros((DEPTH, 128, NPAR), f)
    for l in range(DEPTH):
        P = params[l]
        P[:, P_PRE:P_PRE + 16] = chunkT(inp["pre_norm"][l])
        P[:, P_POST:P_POST + 16] = chunkT(inp["post_norm"][l])
        P[:, P_QN] = inp["q_norm"][l]
        P[:, P_KN] = inp["k_norm"][l]
        P[:, P_MU:P_MU + 24] = chunkT(inp["rw_mu"][l].reshape(-1))
        P[:, P_W0:P_W0 + 8] = chunkT(inp["rw_w0"][l].reshape(-1))
        P[:, P_A0:P_A0 + 8] = chunkT(inp["rw_a0"][l].reshape(-1))
        P[:, P_KK:P_KK + 4] = chunkT(inp["rw_k_k"][l])
        P[:, P_KA:P_KA + 4] = chunkT(inp["rw_k_a"][l])
        P[:, P_RK:P_RK + 4] = chunkT(inp["rw_r_k"][l].reshape(-1))
        P[:, P_LNW:P_LNW + 4] = chunkT(inp["rw_ln_w"][l])
        P[:, P_LNB:P_LNB + 4] = chunkT(inp["rw_ln_b"][l])
        P[:, P_PSC:P_PSC + 4] = chunkT(inp["pool_scale"][l])
    adabT = np.stack([chunkT(inp["ada_b"][l]) for l in range(DEPTH)]).astype(f)
    return {
        "xT": xT, "cT": np.ascontiguousarray(cT), "ada_w": inp["ada_w"], "ada_bT": adabT,
        "w_in": inp["w_in"], "w_out": inp["w_out"], "params": params, "consts": make_consts(),
        "rope": make_rope(), "icnt": make_icnt(), "pool_w": inp["pool_w"],
        "rwc": make_rwc(), "rw_w_up": inp["rw_w_up"], "rw_a_up": inp["rw_a_up"],
    }


def kernel(**inputs):
    inp = {k_: np.asarray(v) for k_, v in inputs.items()}
    nc = build_program()
    in_maps = [make_inputs(c % 4, inp) for c in range(8)]
    res = run_bass_kernel_spmd(nc, in_maps, core_ids=list(range(8)))
    out = np.stack([np.ascontiguousarray(res.results[b]["outT"].T) for b in range(4)], axis=0)
    return out.astype(np.float32)
```

```python
import contextlib
import numpy as np
import concourse.bass as bass
import concourse.mybir as mybir
from concourse.bass_utils import run_bass_kernel_spmd

F32 = mybir.dt.float32
BF16 = mybir.dt.bfloat16
AF = mybir.ActivationFunctionType
ALU = mybir.AluOpType

D = 2048
TC = 256
TL = 2048
T = TC + TL
NIN = 5888
DEPTH = 2
EPS = 1e-6
TILES = [(0, 256), (256, 512), (768, 512), (1280, 512), (1792, 512)]
O_Q, O_K, O_V, O_G = 0, 1024, 1280, 1536
O_RR, O_RK, O_RV, O_RG = 2560, 3072, 3584, 4096
O_LW, O_LA = 4608, 4736
O_PX, O_PG = 4864, 5376
P_PRE, P_POST, P_QN, P_KN, P_MU, P_W0, P_A0, P_KK, P_KA, P_RK, P_LNW, P_LNB, P_PSC = (
    0, 16, 32, 33, 34, 58, 66, 74, 78, 82, 86, 90, 94)
NPAR = 98
SEM_CAP = 3000
RW_RATIO = 1
XP = 8 + TC + 16 + TL + 8
XC0, XL0 = 8, 8 + TC + 16
POOL_W = (2, 4, 8, 16)


class StopPhase(Exception):
    pass


class Trk:
    __slots__ = ("w", "r", "excl")

    def __init__(self):
        self.w = {}
        self.r = {}
        self.excl = False


class V:
    def __init__(self, ap, trk=None):
        self.ap = ap
        self.t = trk if trk is not None else Trk()

    def __getitem__(self, idx):
        return V(self.ap[idx], self.t)

    def re(self, pat, **kw):
        return V(self.ap.rearrange(pat, **kw), self.t)

    def sub(self, idx):
        return V(self.ap[idx], Trk())


class K:
    def __init__(self, nc, es):
        self.nc = nc
        self.es = es
        self.eng = {"pe": nc.tensor, "dve": nc.vector, "act": nc.scalar, "pool": nc.gpsimd, "sp": nc.sync}
        self.cnt = {e: 0 for e in self.eng}
        self.sems = {e: [] for e in self.eng}
        self.seen = {e: {} for e in self.eng}
        self.ndma = 24
        self.dsem = [es.enter_context(nc.semaphore(f"dma{i}")) for i in range(self.ndma)]
        self.dval = [0] * self.ndma
        self.dnext = 0
        self.dpools = {"sp": list(range(0, 16)), "act": list(range(0, 16)), "pool": list(range(16, 24))}
        self.dpos = {"sp": 0, "act": 0, "pool": 0}
        self.same_sync = True

    def _sem(self, e, c):
        ep = (c - 1) // SEM_CAP
        while len(self.sems[e]) <= ep:
            self.sems[e].append(self.es.enter_context(self.nc.semaphore(f"s_{e}_{len(self.sems[e])}")))
        return self.sems[e][ep], c - ep * SEM_CAP

    def _wait(self, E, key, c):
        if c <= self.seen[E].get(key, 0):
            return
        self.seen[E][key] = c
        if isinstance(key, tuple):
            self.eng[E].wait_ge(self.dsem[key[1]], c)
        else:
            sem, val = self._sem(key, c)
            self.eng[E].wait_ge(sem, val)

    def _deps(self, E, outs, ins, skip_dma_waw=False):
        need = {}
        for v in ins:
            for k, c in v.t.w.items():
                need[k] = max(need.get(k, 0), c)
            if v.t.excl:
                for k, c in v.t.r.items():
                    if k != E:
                        need[k] = max(need.get(k, 0), c)
        for v in outs:
            for k, c in v.t.w.items():
                if skip_dma_waw and isinstance(k, tuple):
                    continue
                need[k] = max(need.get(k, 0), c)
            for k, c in v.t.r.items():
                need[k] = max(need.get(k, 0), c)
        for k, c in need.items():
            if k == E and (E == "pe" or not self.same_sync):
                continue
            self._wait(E, k, c)

    def op(self, E, fn, outs, ins, sig=True):
        self._deps(E, outs, ins)
        inst = fn()
        c = self.cnt[E] + 1
        if sig:
            self.cnt[E] = c
            sem, _ = self._sem(E, c)
            inst.then_inc(sem, 1)
        for v in ins:
            v.t.r[E] = c
        for v in outs:
            v.t.w = {E: c}
            v.t.r = {}
        return inst

    def dma(self, Q, out, in_, multi=False):
        self._deps(Q, [out], [in_], skip_dma_waw=multi)
        pl = self.dpools[Q]
        i = pl[self.dpos[Q] % len(pl)]
        self.dpos[Q] += 1
        self._wait(Q, ("dma", i), self.dval[i])
        inst = self.eng[Q].dma_start(out=out.ap, in_=in_.ap)
        self.dval[i] += 16
        inst.then_inc(self.dsem[i], 16)
        key = ("dma", i)
        in_.t.r[key] = self.dval[i]
        if multi:
            out.t.w = {k: c for k, c in out.t.w.items() if isinstance(k, tuple)}
            out.t.w[key] = self.dval[i]
        else:
            out.t.w = {key: self.dval[i]}
        out.t.r = {}

    def barrier(self):
        for E in self.eng:
            for e2 in self.eng:
                if e2 != E and self.cnt[e2] > 0:
                    self._wait(E, e2, self.cnt[e2])
            for i in range(self.ndma):
                if self.dval[i] > 0:
                    self._wait(E, ("dma", i), self.dval[i])

    def final_wait(self):
        for i in range(self.ndma):
            if self.dval[i] > 0:
                self._wait("sp", ("dma", i), self.dval[i])

    def mm(self, out, lhsT, rhs, start=True, stop=True, sig=True):
        return self.op("pe", lambda: self.nc.tensor.matmul(out.ap, lhsT.ap, rhs.ap, start=start, stop=stop),
                       [out], [lhsT, rhs], sig=sig)

    def tr(self, out, in_, ident):
        return self.op("pe", lambda: self.nc.tensor.transpose(out.ap, in_.ap, ident.ap), [out], [in_, ident])

    def act(self, out, in_, func, bias=None, scale=None):
        ins = [in_]
        kw = {}
        if bias is not None:
            if isinstance(bias, V):
                ins.append(bias)
                kw["bias"] = bias.ap
            else:
                kw["bias"] = float(bias)
        if scale is not None:
            if isinstance(scale, V):
                ins.append(scale)
                kw["scale"] = scale.ap
            else:
                kw["scale"] = float(scale)
        return self.op("act", lambda: self.nc.scalar.activation(out.ap, in_.ap, func, **kw), [out], ins)

    def _e(self, E):
        return self.eng[E]

    def tt(self, E, out, a, b, op):
        return self.op(E, lambda: self._e(E).tensor_tensor(out.ap, a.ap, b.ap, op), [out], [a, b])

    def ts(self, E, out, a, s1, op0, s2=None, op1=None):
        ins = [a]
        s1a = s1.ap if isinstance(s1, V) else float(s1)
        if isinstance(s1, V):
            ins.append(s1)
        if s2 is None:
            return self.op(E, lambda: self._e(E).tensor_scalar(out.ap, a.ap, s1a, None, op0), [out], ins)
        s2a = s2.ap if isinstance(s2, V) else float(s2)
        if isinstance(s2, V):
            ins.append(s2)
        return self.op(E, lambda: self._e(E).tensor_scalar(out.ap, a.ap, s1a, s2a, op0, op1), [out], ins)

    def stt(self, E, out, in0, s, in1, op0, op1):
        ins = [in0, in1]
        sa = s.ap if isinstance(s, V) else float(s)
        if isinstance(s, V):
            ins.append(s)
        return self.op(E, lambda: self._e(E).scalar_tensor_tensor(out.ap, in0.ap, sa, in1.ap, op0, op1), [out], ins)

    def copy(self, E, out, in_):
        if E == "act":
            return self.op(E, lambda: self.nc.scalar.copy(out.ap, in_.ap), [out], [in_])
        return self.op(E, lambda: self._e(E).tensor_copy(out.ap, in_.ap), [out], [in_])

    def memset(self, E, out, val):
        return self.op(E, lambda: self._e(E).memset(out.ap, val), [out], [])

    def recip(self, out, in_):
        return self.op("dve", lambda: self.nc.vector.reciprocal(out.ap, in_.ap), [out], [in_])

    def scan(self, out, d0, d1, init, op0, op1):
        return self.op("dve", lambda: self.nc.vector.tensor_tensor_scan(out.ap, d0.ap, d1.ap, init, op0, op1),
                       [out], [d0, d1])


def build_program(dbg=None, nlayers=DEPTH):
    nc = bass.Bass("TRN2", target_bir_lowering=False)
    dram = lambda n, s, d=F32, kind="ExternalInput": nc.dram_tensor(n, list(s), d, kind=kind).ap()
    xT_d = dram("xT", [D, T])
    cT_d = dram("cT", [128, 16, 2])
    adaw_d = dram("ada_w", [DEPTH, D, 3 * D])
    adab_d = dram("ada_bT", [DEPTH, 128, 48])
    win_d = dram("w_in", [DEPTH, D, NIN])
    wout_d = dram("w_out", [DEPTH, D, D])
    par_d = dram("params", [DEPTH, 128, NPAR])
    cst_d = dram("consts", [128, 128 * 4])
    rope_d = dram("rope", [128, 2 * TL])
    icnt_d = dram("icnt", [128, 4 * XP])
    poolw_d = dram("pool_w", [DEPTH, 4, 128, 128])
    rwc_d = dram("rwc", [128, 3328])
    rwwup_d = dram("rw_w_up", [DEPTH, 2, 64, 512])
    rwaup_d = dram("rw_a_up", [DEPTH, 2, 64, 512])
    outT_d = dram("outT", [D, TL], kind="ExternalOutput")
    mixT_d = dram("mixT_scr", [D, T], BF16, kind="Internal")
    PT_d = dram("PT_scr", [NIN, T], kind="Internal")
    x1T_d = dram("x1T_scr", [D, T], kind="Internal")
    dbg_d = {}
    if dbg:
        for n, s in dbg.items():
            if not n.startswith("_"):
                dbg_d[n] = dram(n, s, kind="ExternalOutput")

    with contextlib.ExitStack() as es:
        k = K(nc, es)
        sb = lambda n, s, d=F32: V(es.enter_context(nc.sbuf_tensor(n, list(s), d))[:])
        xT = V(xT_d); cT = V(cT_d); adaw = V(adaw_d); adab = V(adab_d); win = V(win_d); wout = V(wout_d)
        par = V(par_d); cst = V(cst_d); outT = V(outT_d); PT = V(PT_d); x1T = V(x1T_d)
        rwc = V(rwc_d)
        rope = V(rope_d); icnt = V(icnt_d); poolw = V(poolw_d); mixT = V(mixT_d)
        PTb = [PT.sub((slice(b_ * 128, (b_ + 1) * 128), slice(None))) for b_ in range(NIN // 128)]
        mixb = [mixT.sub((slice(b_ * 128, (b_ + 1) * 128), slice(None))) for b_ in range(16)]
        cons = sb("cons", [128, 512])
        ident, ones, perm, bones = cons[:, 0:128], cons[:, 128:256], cons[:, 256:384], cons[:, 384:512]
        prm = [sb(f"prm{l}", [128, NPAR]) for l in range(DEPTH)]
        mod = [sb(f"mod{l}", [128, 48, 2]) for l in range(DEPTH)]
        s1 = [sb(f"s1_{l}", [128, 16, 2]) for l in range(DEPTH)]
        g1 = [sb(f"g1_{l}", [128, 16, 2]) for l in range(DEPTH)]
        banks = [V(es.enter_context(nc.psum_tensor(f"ps{i}", [128, 512], F32))[:]) for i in range(8)]
        for b_ in banks:
            b_.t.excl = True
        bank_i = [0]
        uid = [0]

        def uname(n):
            uid[0] += 1
            return f"{n}_u{uid[0]}"

        def bank():
            b = banks[bank_i[0] % 8]
            bank_i[0] += 1
            return b

        k.dma("sp", cons, cst)
        consb = sb("consb", [128, 512], BF16)
        k.copy("dve", consb, cons)
        identb_g, ones_bg, perm_b, bones_b = consb[:, 0:128], consb[:, 128:256], consb[:, 256:384], consb[:, 384:512]
        for l in range(DEPTH):
            k.dma("sp", prm[l], par[l])

        def ada_gen(l, pb):
            with contextlib.ExitStack() as ps:
                lsb = lambda n, s, d=F32: V(ps.enter_context(nc.sbuf_tensor(uname(n), list(s), d))[:])
                ct = lsb("ada_ct", [128, 16, 2])
                cs = lsb("ada_cs", [128, 16, 2])
                ab = lsb("ada_b", [128, 48])
                wb = [lsb(f"ada_w{i}", [128, 16, 256]) for i in range(2)]
                k.dma("sp", ct, cT)
                k.dma("sp", ab, adab[l])
                k.act(cs, ct, AF.Silu)
                pv = pb[:, 0:96].re("p (b j) -> p b j", j=2)
                k.dma("sp", wb[0], adaw[l][:, 0:256].re("(c p) n -> p c n", p=128))
                for cb in range(24):
                    w = wb[cb % 2]
                    if cb + 1 < 24:
                        k.dma("sp", wb[(cb + 1) % 2], adaw[l][:, (cb + 1) * 256:(cb + 2) * 256].re("(c p) n -> p c n", p=128))
                    for j in range(2):
                        blk = cb * 2 + j
                        for c in range(16):
                            k.mm(pv[:, blk, :], w[:, c, j * 128:(j + 1) * 128], cs[:, c, :], start=(c == 0), stop=(c == 15))
                    yield
                for j in range(2):
                    k.tt("dve", mod[l][:, :, j], pv[:, :, j], ab, ALU.add)
                    k.stt("dve", s1[l][:, :, j], mod[l][:, 16:32, j], 1.0, prm[l][:, P_PRE:P_PRE + 16], ALU.add, ALU.mult)
                    k.tt("dve", g1[l][:, :, j], mod[l][:, 32:48, j], prm[l][:, P_POST:P_POST + 16], ALU.mult)

        def phase_ada(l):
            for _ in ada_gen(l, bank()):
                pass
            k.barrier()

        def phase_hproj(l, xsrc):
            with contextlib.ExitStack() as ps:
                lsb = lambda n, s, d=F32: V(ps.enter_context(nc.sbuf_tensor(uname(n), list(s), d))[:])
                hT = lsb("hT", [128, 16, T], BF16)
                hts = [hT.sub((slice(None), slice(None), slice(t0, t0 + tn))) for (t0, tn) in TILES]
                with contextlib.ExitStack() as ps2:
                    lsb2 = lambda n, s, d=F32: V(ps2.enter_context(nc.sbuf_tensor(uname(n), list(s), d))[:])
                    xt = [lsb2(f"h_x{i}", [128, 16, 512]) for i in range(2)]
                    sq = [lsb2(f"h_sq{i}", [128, 512], BF16) for i in range(3)]
                    rs = [lsb2(f"h_rs{i}", [128, 512]) for i in range(2)]
                    tmp = [lsb2(f"h_tmp{i}", [128, 512]) for i in range(3)]
                    for ti, (t0, tn) in enumerate(TILES):
                        j = 1 if ti == 0 else 0
                        x_ = xt[ti % 2]
                        k.dma("sp", x_[:, :, 0:tn], xsrc[:, t0:t0 + tn].re("(c p) t -> p c t", p=128))
                        pb = bank()
                        for c in range(16):
                            s_ = sq[c % 3]
                            k.act(s_[:, 0:tn], x_[:, c, 0:tn], AF.Square)
                            k.mm(pb[:, 0:tn], ones_bg, s_[:, 0:tn], start=(c == 0), stop=(c == 15))
                        r_ = rs[ti % 2]
                        k.act(r_[:, 0:tn], pb[:, 0:tn], AF.Sqrt, bias=EPS, scale=1.0 / D)
                        k.recip(r_[:, 0:tn], r_[:, 0:tn])
                        for c in range(16):
                            t_ = tmp[c % 3]
                            k.stt("dve", t_[:, 0:tn], x_[:, c, 0:tn], s1[l][:, c, j:j + 1], r_[:, 0:tn], ALU.mult, ALU.mult)
                            k.act(hts[ti][:, c, :], t_[:, 0:tn], AF.Identity, bias=mod[l][:, c, j:j + 1])
                    if dbg and "hTd" in dbg and l == dbg.get("_layer", 0):
                        hf = lsb2("h_dbg", [128, T])
                        for c in range(16):
                            for ti, (t0, tn) in enumerate(TILES):
                                k.copy("dve", hf[:, t0:t0 + tn], hts[ti][:, c, :])
                            k.dma("sp", V(dbg_d["hTd"])[c * 128:(c + 1) * 128, :], hf)
                k.barrier()
                if dbg and dbg.get("_stop") == "h":
                    return
                with contextlib.ExitStack() as ps2:
                    lsb2 = lambda n, s, d=F32: V(ps2.enter_context(nc.sbuf_tensor(uname(n), list(s), d))[:])
                    wbuf = [lsb2(f"p_w{i}", [128, 16, 256], BF16) for i in range(3)]
                    stg = [lsb2(f"p_stg{i}", [128, T]) for i in range(3)]
                    for gi in range(NIN // 256):
                        w = wbuf[gi % 3]
                        k.dma("pool", w, win[l][:, gi * 256:(gi + 1) * 256].re("(c p) n -> p c n", p=128))
                        for jb in range(2):
                            blk = gi * 2 + jb
                            st = stg[blk % 3]
                            for ti, (t0, tn) in enumerate(TILES):
                                pb = bank()
                                for c in range(16):
                                    k.mm(pb[:, 0:tn], w[:, c, jb * 128:(jb + 1) * 128], hts[ti][:, c, :],
                                         start=(c == 0), stop=(c == 15), sig=(c == 15))
                                if (blk * 5 + ti) % 2 == 0:
                                    k.copy("act", st[:, t0:t0 + tn], pb[:, 0:tn])
                                else:
                                    k.copy("dve", st[:, t0:t0 + tn], pb[:, 0:tn])
                            k.dma("sp", PTb[blk], st)
            k.barrier()

        def phase_pool(l):
            with contextlib.ExitStack() as ps:
                lsb = lambda n, s, d=F32: V(ps.enter_context(nc.sbuf_tensor(uname(n), list(s), d))[:])
                ic = lsb("pl_ic", [128, 4 * XP])
                pw = lsb("pl_w", [128, 4, 128], BF16)
                xp = [lsb(f"pl_xp{i}", [128, XP]) for i in range(2)]
                A = lsb("pl_A", [128, XP]); B = lsb("pl_B", [128, XP])
                pb16 = [lsb(f"pl_p{i}", [128, XP], BF16) for i in range(2)]
                graw = [lsb(f"pl_g{i}", [128, T]) for i in range(2)]
                stg = [lsb(f"pl_s{i}", [128, T], BF16) for i in range(2)]
                k.dma("sp", ic, icnt)
                k.dma("pool", pw, poolw[l].re("g c d -> c g d"))
                for g in range(4):
                    w = POOL_W[g]; half = w // 2
                    x_ = xp[g % 2]; p16 = pb16[g % 2]; gr = graw[g % 2]; st = stg[g % 2]
                    k.memset("pool", x_, 0.0)
                    k.dma("sp", x_[:, XC0:XC0 + TC], PTb[O_PX // 128 + g][:, 0:TC])
                    k.dma("sp", x_[:, XL0:XL0 + TL], PTb[O_PX // 128 + g][:, TC:T], multi=True)
                    k.dma("sp", gr, PTb[O_PG // 128 + g])
                    src = x_; n = XP; step = 1; bufs = [A, B]; bi = 0
                    while step < w:
                        dst = bufs[bi]; bi ^= 1
                        k.tt("dve", dst[:, 0:n - step], src[:, 0:n - step], src[:, step:n], ALU.add)
                        src = dst; n -= step; step *= 2
                    lo, hi = XC0, XL0 + TL
                    dst = bufs[bi]
                    k.tt("dve", dst[:, lo:hi], src[:, lo - half:hi - half], ic[:, g * XP + lo:g * XP + hi], ALU.mult)
                    k.tt("pool", p16[:, lo:hi], dst[:, lo:hi], x_[:, lo:hi], ALU.subtract)
                    k.act(gr, gr, AF.Silu)
                    for ti, (t0, tn) in enumerate(TILES):
                        xo = XC0 + t0 if ti == 0 else XL0 + (t0 - TC)
                        pb = bank()
                        k.mm(pb[:, 0:tn], pw[:, g, :], p16[:, xo:xo + tn])
                        k.stt("dve", st[:, t0:t0 + tn], pb[:, 0:tn], prm[l][:, P_PSC + g:P_PSC + g + 1], gr[:, t0:t0 + tn],
                              ALU.mult, ALU.mult)
                    k.dma("sp", mixb[12 + g], st)
            k.barrier()

        def phase_att(l, need_ctx, bg=None):
            SC = 128.0 ** -0.5
            with contextlib.ExitStack() as ps:
                lsb = lambda n, s, d=F32: V(ps.enter_context(nc.sbuf_tensor(uname(n), list(s), d))[:])
                rp = lsb("at_rope", [128, 2 * TL])
                raw = [lsb(f"at_raw{i}", [128, T]) for i in range(2)]
                sq = [lsb(f"at_sq{i}", [128, 512], BF16) for i in range(2)]
                rs = [lsb(f"at_rs{i}", [128, 512]) for i in range(2)]
                kn = [lsb(f"at_kn{i}", [128, 512], BF16) for i in range(2)]
                t1 = [lsb(f"at_t1{i}", [128, 512]) for i in range(2)]
                t2 = [lsb(f"at_t2{i}", [128, 512]) for i in range(2)]
                kT = [lsb(f"at_kT{i}", [128, T], BF16) for i in range(2)]
                qT = [lsb(f"at_qT{i}", [128, T], BF16) for i in range(8)]
                vtm = [lsb(f"at_v{i}", [128, 18, 128], BF16) for i in range(2)]
                sg = [lsb(f"at_sg{i}", [128, T], BF16) for i in range(8)]
                pt = [lsb(f"at_pt{i}", [128, 512], BF16) for i in range(6)]
                ones_b = lsb("at_ones", [128, 128], BF16)
                rd = [lsb(f"at_rd{i}", [128, 512]) for i in range(2)]
                ot = [lsb(f"at_o{i}", [128, 512]) for i in range(2)]
                stg = [lsb(f"at_stg{i}", [128, T], BF16) for i in range(2)]
                k.dma("sp", rp, rope)
                k.copy("dve", ones_b, ones)
                cnt = [0]

                def normrope(dst, src_blk, gcol):
                    r_ = raw[cnt[0] % 2]; cnt[0] += 1
                    k.dma("sp", r_, PTb[src_blk])
                    for ti, (t0, tn) in enumerate(TILES):
                        s_ = sq[ti % 2]; rr = rs[ti % 2]; kn_ = kn[ti % 2]
                        k.act(s_[:, 0:tn], r_[:, t0:t0 + tn], AF.Square)
                        pb = bank()
                        k.mm(pb[:, 0:tn], ones_bg, s_[:, 0:tn])
                        k.act(rr[:, 0:tn], pb[:, 0:tn], AF.Ln, bias=EPS, scale=1.0 / 128)
                        k.act(rr[:, 0:tn], rr[:, 0:tn], AF.Exp, scale=-0.5)
                        if ti == 0:
                            k.stt("dve", dst[:, t0:t0 + tn], r_[:, t0:t0 + tn], prm[l][:, gcol:gcol + 1], rr[:, 0:tn],
                                  ALU.mult, ALU.mult)
                            continue
                        k.stt("dve", kn_[:, 0:tn], r_[:, t0:t0 + tn], prm[l][:, gcol:gcol + 1], rr[:, 0:tn],
                              ALU.mult, ALU.mult)
                        pb2 = bank()
                        k.mm(pb2[:, 0:tn], perm_b, kn_[:, 0:tn])
                        a_ = t1[ti % 2]; b_ = t2[ti % 2]
                        lt = t0 - TC
                        k.tt("dve", a_[:, 0:tn], kn_[:, 0:tn], rp[:, lt:lt + tn], ALU.mult)
                        k.tt("dve", b_[:, 0:tn], pb2[:, 0:tn], rp[:, TL + lt:TL + lt + tn], ALU.mult)
                        k.tt("pool", dst[:, t0:t0 + tn], a_[:, 0:tn], b_[:, 0:tn], ALU.add)

                for g in range(2):
                    normrope(kT[g], O_K // 128 + g, P_KN)
                    vr = raw[cnt[0] % 2]; cnt[0] += 1
                    k.dma("sp", vr, PTb[O_V // 128 + g])
                    for kt in range(18):
                        pb = bank()
                        k.tr(pb[:, 0:128], vr[:, kt * 128:(kt + 1) * 128], ident)
                        k.copy("act" if kt % 2 else "dve", vtm[g][:, kt, :], pb[:, 0:128])
                for h in range(8):
                    normrope(qT[h], O_Q // 128 + h, P_QN)
                    gr = raw[cnt[0] % 2]; cnt[0] += 1
                    k.dma("sp", gr, PTb[O_G // 128 + h])
                    k.act(sg[h], gr, AF.Silu)
                gcount = 0
                for h in range(8):
                    g = h // 4
                    q_ = qT[h]; sg_ = sg[h]; st = stg[h % 2]
                    groups = [(TC + 512 * i, 512, 18) for i in range(4)]
                    if need_ctx:
                        groups = [(0, TC, 2)] + groups
                    for (q0, qn, nk) in groups:
                        par = gcount % 2; gcount += 1
                        po = banks[par]; pd = banks[2 + par]
                        sb_ = banks[4:8] if bg is None else banks[4:7]
                        if bg is not None:
                            next(bg, None)
                        pend = []

                        def issue_s(kt):
                            pss = sb_[kt % len(sb_)]
                            k.mm(pss[:, 0:qn], kT[g][:, kt * 128:(kt + 1) * 128], q_[:, q0:q0 + qn])
                            p_ = pt[kt % 6]
                            k.act(p_[:, 0:qn], pss[:, 0:qn], AF.Exp, scale=SC)
                            return p_
                        look = len(sb_) - 1
                        for kt in range(min(look, nk)):
                            pend.append(issue_s(kt))
                        for kt in range(nk):
                            p_ = pend[kt]
                            k.mm(po[:, 0:qn], vtm[g][:, kt, :], p_[:, 0:qn], start=(kt == 0), stop=(kt == nk - 1))
                            k.mm(pd[:, 0:qn], ones_b, p_[:, 0:qn], start=(kt == 0), stop=(kt == nk - 1))
                            if kt + look < nk:
                                pend.append(issue_s(kt + look))
                        rd_ = rd[par]; o_ = ot[par]
                        k.recip(rd_[:, 0:qn], pd[:, 0:qn])
                        k.tt("dve", o_[:, 0:qn], po[:, 0:qn], rd_[:, 0:qn], ALU.mult)
                        k.tt("pool", st[:, q0:q0 + qn], o_[:, 0:qn], sg_[:, q0:q0 + qn], ALU.mult)
                    if need_ctx:
                        k.dma("sp", mixb[h], st)
                    else:
                        k.dma("sp", mixb[h][:, TC:T], st[:, TC:T])
                if bg is not None:
                    for _ in bg:
                        pass
            k.barrier()

        def phase_rwkv(l, need_ctx):
            NCH, C, G = T // 128, 128, 3
            DEC = -float(np.exp(-0.5))
            with contextlib.ExitStack() as ps:
                lsb = lambda n, s, d=F32: V(ps.enter_context(nc.sbuf_tensor(uname(n), list(s), d))[:])
                big = lambda n: lsb(n, [128, T])
                rc = lsb("rw_c", [128, 2048])
                MU2, ML2 = rc[:, 0:512], rc[:, 512:1024]
                LV4, I4 = rc[:, 1024:1536], rc[:, 1536:2048]
                identb = lsb("rw_idb", [128, 128], BF16)
                tdw = big("rw_tdw"); da = big("rw_da")
                wup = lsb("rw_wup", [128, 512]); aup = lsb("rw_aup", [128, 512])
                c0s = lsb("rw_c0", [128, 12]); omka = lsb("rw_omka", [128, 4])
                KR = [lsb(f"rw_KR{d}", [128, 2, T], BF16) for d in range(2)]
                KMb = [lsb(f"rw_KMb{d}", [128, T], BF16) for d in range(2)]
                Abb = [lsb(f"rw_Abb{d}", [128, T], BF16) for d in range(2)]
                gt = [lsb(f"rw_gt{d}", [128, NCH]) for d in range(2)]
                vb = lsb("rw_vb", [128, T], BF16)
                v = big("rw_v"); Oacc = big("rw_O"); bsum = big("rw_bs")
                stg = lsb("rw_stg", [128, T], BF16)

                k.dma("sp", rc[:, 0:1024], rwc[:, 0:1024])
                k.dma("sp", rc[:, 1024:2048], rwc[:, 2304:3328], multi=True)
                k.copy("dve", identb, ident)
                k.dma("sp", wup, V(rwwup_d)[l].re("d r c -> (d r) c"))
                k.dma("sp", aup, V(rwaup_d)[l].re("d r c -> (d r) c"))
                k.dma("sp", tdw, PTb[O_LW // 128])
                k.dma("sp", da, PTb[O_LA // 128])
                k.act(tdw, tdw, AF.Tanh)
                for i in range(3):
                    mcol = P_MU + i * 8
                    k.ts("dve", c0s[:, i * 4:(i + 1) * 4], prm[l][:, mcol:mcol + 4], -1.0, ALU.mult, 1.0, ALU.add)
                    k.tt("dve", c0s[:, i * 4:(i + 1) * 4], c0s[:, i * 4:(i + 1) * 4], prm[l][:, mcol + 4:mcol + 8], ALU.subtract)
                k.ts("dve", omka, prm[l][:, P_KA:P_KA + 4], -1.0, ALU.mult, 1.0, ALU.add)
                SEGS = ((0, TC, 1), (TC, TL, TC + 2))
                orders = [list(range(NCH)), [1, 0] + list(range(NCH - 1, 1, -1))]
                M2s = [MU2, ML2]

                for blk in range(4):
                    with contextlib.ExitStack() as pp_:
                        psb = lambda n, s, d=F32: V(pp_.enter_context(nc.sbuf_tensor(uname(n), list(s), d))[:])
                        pbig = lambda n: psb(n, [128, T])
                        raws = [psb(f"rw_raw{i}", [128, T + 3]) for i in range(2)]; smask = raws[1][:, 0:T]
                        r = pbig("rw_r"); kx = pbig("rw_k"); kk = pbig("rw_kk")
                        Pb = pbig("rw_P"); Pe = pbig("rw_Pe"); Ab = pbig("rw_A"); KM = pbig("rw_KM"); E = pbig("rw_E")
                        sq = [psb(f"rw_sq{i}", [128, 512], BF16) for i in range(2)]
                        rn = [psb(f"rw_rn{i}", [128, 512]) for i in range(2)]
                        k.memset("pool", raws[0], 0.0)
                        k.memset("pool", raws[1], 0.0)
                        for i, (dst, off) in enumerate(((r, O_RR), (kx, O_RK), (v, O_RV))):
                            raw = raws[i % 2]
                            src = PTb[off // 128 + blk]
                            k.dma("sp", raw[:, 1:1 + TC], src[:, 0:TC])
                            k.dma("sp", raw[:, TC + 2:TC + 2 + TL], src[:, TC:T], multi=True)
                            mcol = P_MU + i * 8 + blk
                            for (a, n, ro) in SEGS:
                                k.ts("dve", dst[:, a:a + n], raw[:, ro:ro + n], c0s[:, i * 4 + blk:i * 4 + blk + 1], ALU.mult)
                                k.stt("dve", dst[:, a:a + n], raw[:, ro - 1:ro - 1 + n], prm[l][:, mcol:mcol + 1],
                                      dst[:, a:a + n], ALU.mult, ALU.add)
                                k.stt("dve", dst[:, a:a + n], raw[:, ro + 1:ro + 1 + n], prm[l][:, mcol + 4:mcol + 5],
                                      dst[:, a:a + n], ALU.mult, ALU.add)
                        k.copy("act", vb, v)
                        k.act(E, kx, AF.Copy, scale=prm[l][:, P_KK + blk:P_KK + blk + 1])
                        for ti, (t0, tn) in enumerate(TILES):
                            s_ = sq[ti % 2]; r_ = rn[ti % 2]
                            k.act(s_[:, 0:tn], E[:, t0:t0 + tn], AF.Square)
                            pb = bank()
                            k.mm(pb[:, 0:tn], bones_b, s_[:, 0:tn])
                            k.ts("dve", r_[:, 0:tn], pb[:, 0:tn], 1e-24, ALU.max)
                            k.act(r_[:, 0:tn], r_[:, 0:tn], AF.Ln)
                            k.act(r_[:, 0:tn], r_[:, 0:tn], AF.Exp, scale=-0.5)
                            k.tt("dve", kk[:, t0:t0 + tn], E[:, t0:t0 + tn], r_[:, 0:tn], ALU.mult)
                        k.memset("pool", smask, 1.0)
                        k.memset("pool", smask.re("p (c t) -> p c t", t=C)[:, :, 0:1], 0.0)
                        for d in range(2):
                            ds = slice(d * 64, (d + 1) * 64)
                            bc = slice(blk * 128, (blk + 1) * 128)
                            for ti, (t0, tn) in enumerate(TILES):
                                pb = bank()
                                k.mm(pb[:, 0:tn], wup[ds, bc], tdw[ds, t0:t0 + tn])
                                k.act(Pe[:, t0:t0 + tn], pb[:, 0:tn], AF.Sigmoid, bias=prm[l][:, P_W0 + d * 4 + blk:P_W0 + d * 4 + blk + 1])
                                pb2 = bank()
                                k.mm(pb2[:, 0:tn], aup[ds, bc], da[ds, t0:t0 + tn])
                                k.act(Ab[:, t0:t0 + tn], pb2[:, 0:tn], AF.Sigmoid, bias=prm[l][:, P_A0 + d * 4 + blk:P_A0 + d * 4 + blk + 1])
                            k.act(Pe, Pe, AF.Copy, scale=DEC)
                            k.scan(Pb, smask, Pe, 0.0, ALU.mult, ALU.add)
                            k.tt("dve", Pe, Pb, Pe, ALU.subtract)
                            Ptot = Pb.re("p (c t) -> p c t", t=C)[:, :, C - 1]
                            k.act(gt[d], Ptot, AF.Exp)
                            k.ts("dve", E, Ab, prm[l][:, P_KA + blk:P_KA + blk + 1], ALU.mult, omka[:, blk:blk + 1], ALU.add)
                            k.tt("dve", KM, E, kx, ALU.mult)
                            k.tt("pool", Ab, Ab, kk, ALU.mult)
                            k.stt("dve", E, r, prm[l][:, P_RK + blk:P_RK + blk + 1], KM, ALU.mult, ALU.mult)
                            for ti, (t0, tn) in enumerate(TILES):
                                pb = bank()
                                k.mm(pb[:, 0:tn], bones, E[:, t0:t0 + tn])
                                if d == 0:
                                    k.act(bsum[:, t0:t0 + tn], pb[:, 0:tn], AF.Copy, scale=0.5)
                                else:
                                    k.stt("dve", bsum[:, t0:t0 + tn], pb[:, 0:tn], 0.5, bsum[:, t0:t0 + tn], ALU.mult, ALU.add)
                            if d == 0:
                                Einc, Eexc, Tmp = Pb, Pe, E
                            else:
                                for c in range(NCH):
                                    cs_ = slice(c * C, (c + 1) * C)
                                    k.act(E[:, cs_], Pe[:, cs_], AF.Identity, bias=Ptot[:, c:c + 1], scale=-1.0)
                                for c in range(NCH):
                                    cs_ = slice(c * C, (c + 1) * C)
                                    k.act(Pe[:, cs_], Pb[:, cs_], AF.Identity, bias=Ptot[:, c:c + 1], scale=-1.0)
                                Einc, Eexc, Tmp = E, Pe, Pb
                            k.act(Tmp, Einc, AF.Exp)
                            k.tt("dve", KR[d][:, 1, :], r, Tmp, ALU.mult)
                            k.act(Tmp, Eexc, AF.Exp)
                            k.tt("dve", KR[d][:, 0, :], kk, Tmp, ALU.mult)
                            k.act(Tmp, Einc, AF.Exp, scale=-1.0)
                            k.tt("dve", KMb[d], KM, Tmp, ALU.mult)
                            k.tt("pool", Abb[d], Ab, Tmp, ALU.mult)
                    k.barrier()
                    with contextlib.ExitStack() as sp_:
                        ssb = lambda n, s, d=F32: V(sp_.enter_context(nc.sbuf_tensor(uname(n), list(s), d))[:])
                        NB = 2 * G
                        akbs = [ssb(f"rw_akb{i}", [128, 4, 512], BF16) for i in range(NB)]
                        ptms = [ssb(f"rw_ptm{i}", [128, 512], BF16) for i in range(NB)]
                        tms = [ssb(f"rw_tm{i}", [128, 768], BF16) for i in range(NB)]
                        invs = [ssb(f"rw_inv{i}", [128, 512], BF16) for i in range(G)]
                        ntms = [ssb(f"rw_ntm{i}", [128, 512], BF16) for i in range(G)]
                        t1s = [ssb(f"rw_t1{i}", [128, 512], BF16) for i in range(G)]
                        Xs = [ssb(f"rw_X{i}", [128, 256], BF16) for i in range(2)]
                        nUs = [ssb(f"rw_nU{i}", [128, 256], BF16) for i in range(2)]
                        Hs = [[ssb(f"rw_H{d}{i}", [128, 64]) for i in range(2)] for d in range(2)]
                        Hbs = [[ssb(f"rw_Hb{d}{i}", [128, 64], BF16) for i in range(2)] for d in range(2)]
                        hgs = [[ssb(f"rw_hg{d}{i}", [128, 64]) for i in range(2)] for d in range(2)]
                        sgate = ssb("rw_sg", [128, T])
                        ft = [ssb(f"rw_ft{i}", [128, 512]) for i in range(4)]
                        p3 = lambda t_: t_.re("p (a m) -> p a m", a=4)
                        k.memset("pool", Oacc, 0.0)
                        for d in range(2):
                            k.memset("dve", Hs[d][0], 0.0)
                            k.memset("dve", Hbs[d][0], 0.0)

                        def stage1(steps):
                            for st in steps:
                                sl = st % NB
                                cs = [slice(orders[d][st] * C, (orders[d][st] + 1) * C) for d in range(2)]
                                pA = bank(); pB = bank()
                                k.mm(pA[:, 0:128], vb[:, cs[0]], identb, sig=False)
                                k.mm(pA[:, 128:256], KMb[0][:, cs[0]], identb, sig=False)
                                k.mm(pA[:, 256:384], Abb[0][:, cs[0]], identb, sig=False)
                                k.mm(pA[:, 384:512], vb[:, cs[1]], identb)
                                k.mm(pB[:, 0:128], KMb[1][:, cs[1]], identb, sig=False)
                                k.mm(pB[:, 128:256], Abb[1][:, cs[1]], identb)
                                k.copy("act", tms[sl][:, 0:512], pA)
                                k.copy("act", tms[sl][:, 512:768], pB[:, 0:256])
                            yield
                            for st in steps:
                                sl = st % NB
                                cs = [slice(orders[d][st] * C, (orders[d][st] + 1) * C) for d in range(2)]
                                for d in range(2):
                                    for hb in range(2):
                                        hs = slice(hb * 64, (hb + 1) * 64)
                                        pa = bank()
                                        k.mm(pa[:, 0:256], KMb[d][hs, cs[d]], KR[d][hs, :, cs[d]], sig=False)
                                        k.mm(pa[:, 256:512], Abb[d][hs, cs[d]], KR[d][hs, :, cs[d]])
                                        k.tt("dve", akbs[sl][:, d * 2 + hb, :], pa, M2s[d], ALU.mult)
                            yield
                            for st in steps:
                                sl = st % NB
                                NT4 = akbs[sl][:, :, 256:384]
                                nm4 = ntms[st % G]
                                k.stt("dve", p3(nm4), p3(LV4), 0.0, NT4, ALU.is_equal, ALU.mult)
                                k.tt("pool", ptms[sl], I4, nm4, ALU.subtract)
                            yield
                            for lv in range(1, 7):
                                for st in steps:
                                    sl = st % NB
                                    pq = bank()
                                    for ln in range(4):
                                        k.mm(pq[:, ln * 128:(ln + 1) * 128], ptms[sl][:, ln * 128:(ln + 1) * 128], identb, sig=(ln == 3))
                                    k.copy("act", invs[st % G], pq)
                                    k.stt("dve", p3(ntms[st % G]), p3(LV4), float(lv), akbs[sl][:, :, 256:384], ALU.is_equal, ALU.mult)
                                yield
                                for st in steps:
                                    pt1 = bank()
                                    for ln in range(4):
                                        k.mm(pt1[:, ln * 128:(ln + 1) * 128], ntms[st % G][:, ln * 128:(ln + 1) * 128],
                                             invs[st % G][:, ln * 128:(ln + 1) * 128], sig=(ln == 3))
                                    k.copy("act", t1s[st % G], pt1)
                                yield
                                for st in steps:
                                    sl = st % NB
                                    pp = bank()
                                    for ln in range(4):
                                        k.mm(pp[:, ln * 128:(ln + 1) * 128], t1s[st % G][:, ln * 128:(ln + 1) * 128],
                                             ptms[sl][:, ln * 128:(ln + 1) * 128], sig=(ln == 3))
                                    k.tt("dve", ptms[sl], ptms[sl], pp, ALU.subtract)
                                yield

                        def stage2(steps):
                            for st in steps:
                                sl = st % NB
                                akb = akbs[sl]; ptm = ptms[sl]; tm_ = tms[sl]
                                X = Xs[st % 2]; nU = nUs[st % 2]
                                cch = [orders[d][st] for d in range(2)]
                                cs = [slice(cch[d] * C, (cch[d] + 1) * C) for d in range(2)]
                                vtm = [tm_[:, 0:128], tm_[:, 384:512]]
                                kbtm = [tm_[:, 128:256], tm_[:, 512:640]]
                                bbtm = [tm_[:, 256:384], tm_[:, 640:768]]
                                Hc = [Hs[d][st % 2] for d in range(2)]; Hn = [Hs[d][(st + 1) % 2] for d in range(2)]
                                Hb = [Hbs[d][st % 2] for d in range(2)]; Hbn = [Hbs[d][(st + 1) % 2] for d in range(2)]
                                hg = [hgs[d][st % 2] for d in range(2)]
                                for d in range(2):
                                    k.act(hg[d], Hc[d], AF.Copy, scale=gt[d][:, cch[d]:cch[d] + 1])
                                px = bank()
                                for d in range(2):
                                    for hb in range(2):
                                        ln = d * 2 + hb
                                        hs = slice(hb * 64, (hb + 1) * 64); vs = slice(hb * 64, (hb + 1) * 64)
                                        k.mm(px[:, ln * 64:(ln + 1) * 64], KR[d][hs, 0, cs[d]], Hb[d][hs, :], start=True, stop=False, sig=False)
                                        k.mm(px[:, ln * 64:(ln + 1) * 64], akb[:, ln, 0:128], vtm[d][:, vs], start=False, stop=True, sig=(ln == 3))
                                k.copy("act", X, px[:, 0:256])
                                yield
                                pu = bank()
                                for ln in range(4):
                                    k.mm(pu[:, ln * 64:(ln + 1) * 64], ptm[:, ln * 128:(ln + 1) * 128], X[:, ln * 64:(ln + 1) * 64], sig=(ln == 3))
                                k.act(nU, pu[:, 0:256], AF.Copy, scale=-1.0)
                                yield
                                ph = bank()
                                for d in range(2):
                                    for hb in range(2):
                                        ln = d * 2 + hb
                                        hs = slice(hb * 64, (hb + 1) * 64); vs = slice(hb * 64, (hb + 1) * 64)
                                        k.mm(ph[hs, d * 64:(d + 1) * 64], kbtm[d][:, hs], vtm[d][:, vs], start=True, stop=False, sig=False)
                                        k.mm(ph[hs, d * 64:(d + 1) * 64], bbtm[d][:, hs], nU[:, ln * 64:(ln + 1) * 64], start=False, stop=True, sig=(ln == 3))
                                for d in range(2):
                                    k.stt("dve", Hn[d], ph[:, d * 64:(d + 1) * 64], gt[d][:, cch[d]:cch[d] + 1], hg[d], ALU.mult, ALU.add)
                                    k.copy("pool", Hbn[d], Hn[d])
                                po = bank()
                                for d in range(2):
                                    if not (need_ctx or cch[d] >= 2):
                                        continue
                                    for hb in range(2):
                                        ln = d * 2 + hb
                                        hs = slice(hb * 64, (hb + 1) * 64); vs = slice(hb * 64, (hb + 1) * 64)
                                        k.mm(po[hs, d * 128:(d + 1) * 128], Hb[d][hs, :], KR[d][hs, 1, cs[d]], start=True, stop=False, sig=False)
                                        k.mm(po[hs, d * 128:(d + 1) * 128], vtm[d][:, vs], akb[:, ln, 128:256], start=False, stop=False, sig=False)
                                        k.mm(po[hs, d * 128:(d + 1) * 128], nU[:, ln * 64:(ln + 1) * 64], akb[:, ln, 384:512], start=False, stop=True, sig=(hb == 1))
                                    k.tt("dve", Oacc[:, cs[d]], Oacc[:, cs[d]], po[:, d * 128:(d + 1) * 128], ALU.add)
                                yield

                        groups = [list(range(g0, g0 + G)) for g0 in range(0, NCH, G)]
                        for _ in stage1(groups[0]):
                            pass
                        for gi in range(len(groups)):
                            g1 = stage1(groups[gi + 1]) if gi + 1 < len(groups) else iter(())
                            g2 = stage2(groups[gi])
                            a1 = a2 = True
                            while a1 or a2:
                                for _r in range(RW_RATIO):
                                    if a1:
                                        a1 = next(g1, "end") != "end"
                                if a2:
                                    a2 = next(g2, "end") != "end"
                        k.dma("sp", sgate, PTb[O_RG // 128 + blk])
                        k.act(sgate, sgate, AF.Silu)
                        for ti, (t0, tn) in enumerate(TILES):
                            if ti == 0 and not need_ctx:
                                continue
                            ts_ = slice(t0, t0 + tn)
                            oc = ft[0]; s_ = ft[1]; r_ = ft[2]; bb_ = ft[3]
                            pb = bank()
                            k.mm(pb[:, 0:tn], bones, Oacc[:, ts_])
                            k.stt("dve", oc[:, 0:tn], pb[:, 0:tn], -1.0 / 64, Oacc[:, ts_], ALU.mult, ALU.add)
                            k.act(s_[:, 0:tn], oc[:, 0:tn], AF.Square)
                            pb2 = bank()
                            k.mm(pb2[:, 0:tn], bones, s_[:, 0:tn])
                            k.act(r_[:, 0:tn], pb2[:, 0:tn], AF.Ln, bias=64e-5, scale=1.0 / 64)
                            k.act(r_[:, 0:tn], r_[:, 0:tn], AF.Exp, scale=-0.5)
                            k.tt("dve", oc[:, 0:tn], oc[:, 0:tn], r_[:, 0:tn], ALU.mult)
                            k.ts("dve", oc[:, 0:tn], oc[:, 0:tn], prm[l][:, P_LNW + blk:P_LNW + blk + 1], ALU.mult,
                                 prm[l][:, P_LNB + blk:P_LNB + blk + 1], ALU.add)
                            k.tt("pool", bb_[:, 0:tn], bsum[:, ts_], v[:, ts_], ALU.mult)
                            k.tt("pool", oc[:, 0:tn], oc[:, 0:tn], bb_[:, 0:tn], ALU.add)
                            k.tt("pool", stg[:, ts_], oc[:, 0:tn], sgate[:, ts_], ALU.mult)
                        if need_ctx:
                            k.dma("sp", mixb[8 + blk], stg)
                        else:
                            k.dma("sp", mixb[8 + blk][:, TC:T], stg[:, TC:T])
                    k.barrier()
            k.barrier()

        def phase_out(l, xsrc, xdst, need_ctx):
            TN = 256
            tiles = [(t0, TN) for t0 in range(0 if need_ctx else TC, T, TN)]
            with contextlib.ExitStack() as ps:
                lsb = lambda n, s, d=F32: V(ps.enter_context(nc.sbuf_tensor(uname(n), list(s), d))[:])
                wo = lsb("o_w", [128, 16, D], BF16)
                wos = [wo.sub((slice(None), slice(None), slice(i * 512, (i + 1) * 512))) for i in range(4)]
                mx = [lsb(f"o_mx{i}", [128, 16, TN], BF16) for i in range(3)]
                xt = [lsb(f"o_x{i}", [128, 16, TN]) for i in range(3)]
                y2 = [lsb(f"o_y{i}", [128, 16, TN]) for i in range(2)]
                ys2 = [[y_.sub((slice(None), i, slice(None))) for i in range(16)] for y_ in y2]
                sq = [lsb(f"o_sq{i}", [128, TN], BF16) for i in range(4)]
                rs2 = [lsb(f"o_rs{i}", [128, TN]) for i in range(2)]
                for i in range(4):
                    k.dma("pool", wos[i], wout[l][:, i * 512:(i + 1) * 512].re("(c p) n -> p c n", p=128))

                def load(it):
                    t0, tn = tiles[it]
                    k.dma("sp", mx[it % 3], mixT[:, t0:t0 + tn].re("(c p) t -> p c t", p=128))
                    k.dma("sp", xt[it % 3], xsrc[:, t0:t0 + tn].re("(c p) t -> p c t", p=128))

                load(0)
                if len(tiles) > 1:
                    load(1)
                for it, (t0, tn) in enumerate(tiles):
                    j = 1 if t0 < TC else 0
                    m_ = mx[it % 3]; x_ = xt[it % 3]; ys = ys2[it % 2]; rs = rs2[it % 2]
                    pss = banks[0]
                    for db in range(16):
                        pb = banks[1 + db % 7]
                        for c in range(16):
                            k.mm(pb[:, 0:tn], wos[db // 4][:, c, (db % 4) * 128:(db % 4 + 1) * 128], m_[:, c, :],
                                 start=(c == 0), stop=(c == 15), sig=(c == 15))
                        k.copy("dve" if db % 2 else "act", ys[db], pb[:, 0:tn])
                        k.act(sq[db % 4], ys[db], AF.Square)
                        if db >= 2:
                            k.mm(pss[:, 0:tn], ones_bg, sq[(db - 2) % 4], start=(db == 2), stop=False)
                    for db in (14, 15):
                        k.mm(pss[:, 0:tn], ones_bg, sq[db % 4], start=False, stop=(db == 15))
                    if it + 2 < len(tiles):
                        load(it + 2)
                    k.act(rs, pss[:, 0:tn], AF.Ln, bias=EPS, scale=1.0 / D)
                    k.act(rs, rs, AF.Exp, scale=-0.5)
                    for db in range(16):
                        k.stt("dve", ys[db], ys[db], g1[l][:, db, j:j + 1], rs, ALU.mult, ALU.mult)
                        k.tt("pool" if db % 3 == 2 else "dve", x_[:, db, :], x_[:, db, :], ys[db], ALU.add)
                    if xdst is outT:
                        k.dma("sp", outT[:, t0 - TC:t0 - TC + tn].re("(c p) t -> p c t", p=128), x_)
                    else:
                        k.dma("sp", xdst[:, t0:t0 + tn].re("(c p) t -> p c t", p=128), x_)
            k.barrier()

        stop = dbg.get("_stop") if dbg else None
        only = dbg.get("_only") if dbg else None
        for l in range(nlayers):
            phase_ada(l)
        xsrc = xT
        for l in range(nlayers):
            last = (l == DEPTH - 1)
            if stop == "ada":
                break
            phase_hproj(l, xsrc)
            if dbg and "PT" in dbg and l == dbg.get("_layer", 0):
                with contextlib.ExitStack() as ps:
                    bnc = V(ps.enter_context(nc.sbuf_tensor(uname("dbg_b"), [128, T], F32))[:])
                    for blk in range(NIN // 128):
                        k.dma("sp", bnc, PTb[blk])
                        k.dma("sp", V(dbg_d["PT"])[blk * 128:(blk + 1) * 128, :], bnc)
                k.barrier()
            if stop in ("h", "proj"):
                break
            if dbg and dbg.get("_zero_mix"):
                with contextlib.ExitStack() as ps:
                    zz = V(ps.enter_context(nc.sbuf_tensor(uname("dbg_z"), [128, T], BF16))[:])
                    k.memset("dve", zz, 0.0)
                    for b_ in range(16):
                        k.dma("sp", mixb[b_], zz)
                k.barrier()
            if only is None or "pool" in only:
                phase_pool(l)
            if only is None or "att" in only:
                phase_att(l, need_ctx=not last)
            if only is None or "rwkv" in only:
                if phase_rwkv(l, need_ctx=not last):
                    break
            if dbg and "mix" in dbg and l == dbg.get("_layer", 0):
                with contextlib.ExitStack() as ps:
                    b16 = V(ps.enter_context(nc.sbuf_tensor(uname("dbg_m16"), [128, T], BF16))[:])
                    b32 = V(ps.enter_context(nc.sbuf_tensor(uname("dbg_m32"), [128, T], F32))[:])
                    for b_ in range(16):
                        k.dma("sp", b16, mixb[b_])
                        k.copy("dve", b32, b16)
                        k.dma("sp", V(dbg_d["mix"])[b_ * 128:(b_ + 1) * 128, :], b32)
                k.barrier()
            if stop == "mix":
                break
            phase_out(l, xsrc, outT if last else x1T, need_ctx=not last)
            if dbg and "x1" in dbg and l == 0:
                with contextlib.ExitStack() as ps:
                    bnc = V(ps.enter_context(nc.sbuf_tensor(uname("dbg_x"), [128, T], F32))[:])
                    for b_ in range(16):
                        k.dma("sp", bnc, x1T[b_ * 128:(b_ + 1) * 128, :])
                        k.dma("sp", V(dbg_d["x1"])[b_ * 128:(b_ + 1) * 128, :], bnc)
                k.barrier()
            xsrc = x1T
        if dbg and "mod" in dbg:
            for l in range(nlayers):
                k.dma("sp", V(dbg_d["mod"])[l], mod[l])
        k.barrier()
        k.final_wait()
    return nc


def make_consts():
    c = np.zeros((128, 512), np.float32)
    c[:, 0:128] = np.eye(128, dtype=np.float32)
    c[:, 128:256] = 1.0
    for i in range(128):
        j = i % 64
        pi = i + 32 if j < 32 else i - 32
        c[pi, 256 + i] = 1.0
    c[0:64, 384:448] = 1.0
    c[64:128, 448:512] = 1.0
    return c


def make_rope():
    half = 64
    nfreq = 32
    inv = (np.float32(10000.0) ** (-(np.arange(nfreq, dtype=np.float32) * np.float32(2.0) / np.float32(half)))).astype(np.float32)
    t = np.arange(TL)
    row = (t // 64).astype(np.float32)
    col = (t % 64).astype(np.float32)
    out = np.zeros((128, 2 * TL), np.float32)
    for i in range(128):
        axis, hf, fr = i // 64, (i % 64) // 32, i % 32
        ang = ((row if axis == 0 else col) * inv[fr]).astype(np.float32)
        out[i, :TL] = np.cos(ang)
        out[i, TL:] = np.sin(ang) * (-1.0 if hf == 0 else 1.0)
    return out


def make_icnt():
    tab = np.zeros((4, XP), np.float32)
    for g, w in enumerate(POOL_W):
        h = w // 2
        for (x0, n) in ((XC0, TC), (XL0, TL)):
            t = np.arange(n)
            lo = np.clip(t - h, 0, n - 1)
            hi = np.clip(t + h - 1, 0, n - 1)
            tab[g, x0:x0 + n] = 1.0 / (hi - lo + 1).astype(np.float32)
    return np.ascontiguousarray(np.broadcast_to(tab.reshape(1, -1), (128, 4 * XP)))


def make_rwc():
    c = np.zeros((128, 3328), np.float32)
    ii = np.arange(128)
    x = ii[:, None] ^ ii[None, :]
    lvl = np.full((128, 128), -1.0, np.float32)
    nz = x > 0
    lvl[nz] = np.floor(np.log2(x[nz])).astype(np.float32)
    lvL = np.where(ii[:, None] > ii[None, :], lvl, -1.0).astype(np.float32)
    c[:, 1792:2048] = np.concatenate([lvL, lvL], axis=1)
    c[:, 2048:2304] = np.concatenate([lvL.T, lvL.T], axis=1)
    c[:, 2304:2816] = np.concatenate([lvL.T, lvL.T, lvL, lvL], axis=1)
    c[:, 2816:3328] = np.concatenate([np.eye(128, dtype=np.float32)] * 4, axis=1)
    tS = np.triu(np.ones((128, 128), np.float32), 1)
    tI = np.triu(np.ones((128, 128), np.float32), 0)
    c[:, 0:512] = np.concatenate([tS, tI, tS, tI], axis=1)
    c[:, 512:1024] = np.concatenate([tS.T, tI.T, tS.T, tI.T], axis=1)
    c[:, 1024:1280] = np.concatenate([np.eye(128, dtype=np.float32)] * 2, axis=1)
    c[:, 1280:1536] = np.concatenate([tS.T, tS.T], axis=1)
    c[:, 1536:1792] = np.concatenate([tS, tS], axis=1)
    return c


def chunkT(v):
    return np.ascontiguousarray(v.reshape(-1, 128).T)


def make_inputs(b, inp):
    f = np.float32
    xT = np.ascontiguousarray(np.concatenate([inp["ctx"][b], inp["x"][b]], axis=0).T.astype(f))
    cT = np.stack([chunkT(inp["c"][b]), chunkT(inp["c_ctx"])], axis=-1).astype(f)
    params = np.zeros((DEPTH, 128, NPAR), f)
    for l in range(DEPTH):
        P = params[l]
        P[:, P_PRE:P_PRE + 16] = chunkT(inp["pre_norm"][l])
        P[:, P_POST:P_POST + 16] = chunkT(inp["post_norm"][l])
        P[:, P_QN] = inp["q_norm"][l]
        P[:, P_KN] = inp["k_norm"][l]
        P[:, P_MU:P_MU + 24] = chunkT(inp["rw_mu"][l].reshape(-1))
        P[:, P_W0:P_W0 + 8] = chunkT(inp["rw_w0"][l].reshape(-1))
        P[:, P_A0:P_A0 + 8] = chunkT(inp["rw_a0"][l].reshape(-1))
        P[:, P_KK:P_KK + 4] = chunkT(inp["rw_k_k"][l])
        P[:, P_KA:P_KA + 4] = chunkT(inp["rw_k_a"][l])
        P[:, P_RK:P_RK + 4] = chunkT(inp["rw_r_k"][l].reshape(-1))
        P[:, P_LNW:P_LNW + 4] = chunkT(inp["rw_ln_w"][l])
        P[:, P_LNB:P_LNB + 4] = chunkT(inp["rw_ln_b"][l])
        P[:, P_PSC:P_PSC + 4] = chunkT(inp["pool_scale"][l])
    adabT = np.stack([chunkT(inp["ada_b"][l]) for l in range(DEPTH)]).astype(f)
    return {
        "xT": xT, "cT": np.ascontiguousarray(cT), "ada_w": inp["ada_w"], "ada_bT": adabT,
        "w_in": inp["w_in"], "w_out": inp["w_out"], "params": params, "consts": make_consts(),
        "rope": make_rope(), "icnt": make_icnt(), "pool_w": inp["pool_w"],
        "rwc": make_rwc(), "rw_w_up": inp["rw_w_up"], "rw_a_up": inp["rw_a_up"],
    }


def kernel(**inputs):
    inp = {k_: np.asarray(v) for k_, v in inputs.items()}
    nc = build_program()
    in_maps = [make_inputs(c % 4, inp) for c in range(8)]
    res = run_bass_kernel_spmd(nc, in_maps, core_ids=list(range(8)))
    out = np.stack([np.ascontiguousarray(res.results[b]["outT"].T) for b in range(4)], axis=0)
    return out.astype(np.float32)
```
